# Optimizing a Trainium2 kernel written in Bass

```python
import math
import jax, jax.numpy as jnp
from jax import lax
import numpy as np

D_MODEL = 1024
BATCH = 8
SEQ = 2048
DEPTH = 2
DEC_BATCH = 128
DEC_SEQ = 4
PAST_LEN = 16384
PAGE_SIZE = 128

N_RET_HEADS = 4
HD_QK = 128
HD_V = 256
D_RET_QK = N_RET_HEADS * HD_QK
D_RET_V = N_RET_HEADS * HD_V
D_CONV = 1024
CONV_W = 3
D_FF = 2816
CHUNK = 128
ROPE_BASE = 10000.0
NORM_EPS = 1e-6
N_NORMS = 6
IN_SIZES = (D_RET_QK, D_RET_QK, D_RET_V, D_RET_V, D_CONV, D_CONV, D_CONV, 2 * D_MODEL)
N_IN = sum(IN_SIZES)

kernel_name = "retention_shortconv_gated_hybrid_step"


def rmsnorm(x, g):
    xf = x.astype(jnp.float32)
    y = xf * lax.rsqrt(jnp.mean(xf * xf, axis=-1, keepdims=True) + NORM_EPS)
    return (y * g.astype(jnp.float32)).astype(x.dtype)


def head_rmsnorm(x):
    xf = x.astype(jnp.float32)
    return (xf * lax.rsqrt(jnp.mean(xf * xf, axis=-1, keepdims=True) + NORM_EPS)).astype(x.dtype)


def swiglu(x, w_up, w_down):
    gate, up = jnp.split(x @ w_up, 2, axis=-1)
    return (jax.nn.silu(gate) * up) @ w_down


def rotary(x, pos):
    d = x.shape[-1]
    half = d // 2
    inv_freq = ROPE_BASE ** (-(jnp.arange(half, dtype=jnp.float32) * 2.0 / d))
    ang = pos.astype(jnp.float32)[:, None] * inv_freq[None, :]
    cos = jnp.cos(ang)[None, :, None, :]
    sin = jnp.sin(ang)[None, :, None, :]
    xf = x.astype(jnp.float32)
    x1, x2 = xf[..., :half], xf[..., half:]
    return jnp.concatenate([x1 * cos - x2 * sin, x2 * cos + x1 * sin], axis=-1).astype(x.dtype)


def log_gammas():
    gam = 1.0 - jnp.exp(jnp.linspace(math.log(1.0 / 32), math.log(1.0 / 512), N_RET_HEADS, dtype=jnp.float32))
    return jnp.log(gam)


def retention(q, k, v, s0):
    B, T, H, _ = q.shape
    C = math.gcd(T, CHUNK)
    n = T // C
    lg = log_gammas()
    idx = jnp.arange(C, dtype=jnp.float32)
    diff = idx[:, None] - idx[None, :]
    dmat = jnp.where(diff[None] >= 0, jnp.exp(jnp.maximum(diff, 0.0)[None] * lg[:, None, None]), 0.0)
    xi = jnp.exp((idx[:, None] + 1.0) * lg[None, :])[None, :, :, None]
    zeta = jnp.exp((C - 1.0 - idx)[:, None] * lg[None, :])[None, :, :, None]
    g_chunk = jnp.exp(C * lg)[None, :, None, None]

    def to_chunks(t):
        return t.astype(jnp.float32).reshape(B, n, C, H, t.shape[-1]).transpose(1, 0, 2, 3, 4)

    def step(s, blk):
        qc, kc, vc = blk
        scores = jnp.einsum('bihd,bjhd->bhij', qc, kc) * dmat[None]
        inner = jnp.einsum('bhij,bjhe->bihe', scores, vc)
        cross = jnp.einsum('bihd,bhde->bihe', qc, s) * xi
        s_new = g_chunk * s + jnp.einsum('bjhd,bjhe->bhde', kc * zeta, vc)
        return s_new, inner + cross

    s_fin, o = lax.scan(step, s0.astype(jnp.float32), (to_chunks(q), to_chunks(k), to_chunks(v)))
    o = o.transpose(1, 0, 2, 3, 4).reshape(B, T, H, HD_V)
    return o.astype(v.dtype), s_fin.astype(s0.dtype)


def short_conv(a, buf, w):
    T = a.shape[1]
    full = jnp.concatenate([buf.astype(a.dtype), a], axis=1)
    z = sum(w[i] * full[:, i:i + T] for i in range(CONV_W))
    return z, full[:, -(CONV_W - 1):]


def mixer(u, pos, s0, buf0, w_in, conv_w, w_ret_out, w_conv_out, w_o):
    B, T, _ = u.shape
    proj = u @ w_in
    cuts = [int(c) for c in np.cumsum(IN_SIZES)[:-1]]
    q, k, v, g, bg, cg, xc, gates = jnp.split(proj, cuts, axis=-1)
    q = rotary(q.reshape(B, T, N_RET_HEADS, HD_QK), pos)
    k = rotary(k.reshape(B, T, N_RET_HEADS, HD_QK), pos) * (HD_QK ** -0.5)
    v = v.reshape(B, T, N_RET_HEADS, HD_V)
    o, s_new = retention(q, k, v, s0)
    o = head_rmsnorm(o).reshape(B, T, D_RET_V)
    o_ret = (jax.nn.silu(g) * o) @ w_ret_out
    z, buf_new = short_conv(cg * xc, buf0, conv_w)
    o_conv = (bg * z) @ w_conv_out
    gate_r, gate_c = jnp.split(jax.nn.sigmoid(gates), 2, axis=-1)
    merged = gate_r * o_ret + gate_c * o_conv
    return merged @ w_o, s_new, buf_new


def layer(x, pos, s0, buf0, norms, w_ffn1_up, w_ffn1_down, w_in, conv_w, w_ret_out, w_conv_out, w_o,
          w_ffn2_up, w_ffn2_down):
    h = x + 0.5 * rmsnorm(swiglu(rmsnorm(x, norms[0]), w_ffn1_up, w_ffn1_down), norms[1])
    m, s_new, buf_new = mixer(rmsnorm(h, norms[2]), pos, s0, buf0, w_in, conv_w, w_ret_out, w_conv_out, w_o)
    h = h + rmsnorm(m, norms[3])
    h = h + 0.5 * rmsnorm(swiglu(rmsnorm(h, norms[4]), w_ffn2_up, w_ffn2_down), norms[5])
    return h, s_new, buf_new


def setup_inputs(seed: int = 0) -> dict:
    key = jax.random.key(seed)
    ks = jax.random.split(key, 16)
    f32 = jnp.float32

    def nrm(k, shape, scale):
        return jax.random.normal(k, shape, f32) * scale

    return {
        "x_prompt": nrm(ks[0], (BATCH, SEQ, D_MODEL), 1.0),
        "x_sample": nrm(ks[1], (DEC_BATCH, DEC_SEQ, D_MODEL), 1.0),
        "state_ret": nrm(ks[2], (DEPTH, DEC_BATCH, N_RET_HEADS, HD_QK, HD_V), 1.0),
        "state_conv": nrm(ks[3], (DEPTH, DEC_BATCH, CONV_W - 1, D_CONV), 0.5),
        "norms": 1.0 + nrm(ks[4], (DEPTH, N_NORMS, D_MODEL), 0.05),
        "w_ffn1_up": nrm(ks[5], (DEPTH, D_MODEL, 2 * D_FF), D_MODEL ** -0.5),
        "w_ffn1_down": nrm(ks[6], (DEPTH, D_FF, D_MODEL), D_FF ** -0.5),
        "w_in": nrm(ks[7], (DEPTH, D_MODEL, N_IN), D_MODEL ** -0.5),
        "conv_w": nrm(ks[8], (DEPTH, CONV_W, D_CONV), CONV_W ** -0.5),
        "w_ret_out": nrm(ks[9], (DEPTH, D_RET_V, D_MODEL), D_RET_V ** -0.5),
        "w_conv_out": nrm(ks[10], (DEPTH, D_CONV, D_MODEL), D_CONV ** -0.5),
        "w_o": nrm(ks[11], (DEPTH, D_MODEL, D_MODEL), D_MODEL ** -0.5),
        "w_ffn2_up": nrm(ks[12], (DEPTH, D_MODEL, 2 * D_FF), D_MODEL ** -0.5),
        "w_ffn2_down": nrm(ks[13], (DEPTH, D_FF, D_MODEL), D_FF ** -0.5),
    }


def reference(x_prompt, x_sample, state_ret, state_conv, norms, w_ffn1_up, w_ffn1_down, w_in, conv_w,
              w_ret_out, w_conv_out, w_o, w_ffn2_up, w_ffn2_down):
    pos_p = jnp.arange(SEQ, dtype=jnp.int32)
    pos_s = PAST_LEN + jnp.arange(DEC_SEQ, dtype=jnp.int32)
    yp, ys = x_prompt, x_sample
    sp_list, bp_list, ss_list, bs_list = [], [], [], []
    for l in range(DEPTH):
        params = (norms[l], w_ffn1_up[l], w_ffn1_down[l], w_in[l], conv_w[l], w_ret_out[l], w_conv_out[l],
                  w_o[l], w_ffn2_up[l], w_ffn2_down[l])
        s0_p = jnp.zeros((BATCH, N_RET_HEADS, HD_QK, HD_V), x_prompt.dtype)
        b0_p = jnp.zeros((BATCH, CONV_W - 1, D_CONV), x_prompt.dtype)
        yp, sp, bp = layer(yp, pos_p, s0_p, b0_p, *params)
        ys, ss, bs = layer(ys, pos_s, state_ret[l], state_conv[l], *params)
        sp_list.append(sp); bp_list.append(bp); ss_list.append(ss); bs_list.append(bs)
    ret_state_prompt = jnp.stack(sp_list)
    conv_state_prompt = jnp.stack(bp_list)
    ret_state_sample = jnp.stack(ss_list)
    conv_state_sample = jnp.stack(bs_list)
    return (yp, ys, ret_state_prompt, conv_state_prompt, ret_state_sample, conv_state_sample)
```

```python
import math
from contextlib import ExitStack

import numpy as np
import concourse.bass as bass
import concourse.mybir as mybir
from concourse.bass_utils import run_bass_kernel_spmd

F32 = mybir.dt.float32
BF16 = mybir.dt.bfloat16
AF = mybir.ActivationFunctionType
ALU = mybir.AluOpType

D = 1024
KC = 8
DFF = 2816
NJ = 22
NH = 4
HQ = 128
HV = 256
DEPTH = 2
SEQ = 2048
TP = 512
NT = 4
NS = 64
NSEQ = 16
TM = TP + NS
EPS = 1e-6
NSLOT = 5
N_IN = 8192
NCORES = 8
DBG_LEVEL = 9
DBG_NI = 6
DBG_CHUNKS = None
DBG_SUB = 0


class Lane:
    def __init__(self, name, sem, unit):
        self.name, self.sem, self.unit, self.count = name, sem, unit, 0


class Sched:
    def __init__(self, nc, stack):
        self.nc, self.stack = nc, stack
        self.E = {}
        self.lastw = {}
        self.readers = {}
        self.nlanes = 0

    def add_engine(self, name, handle, lane=True):
        ln = None
        if lane:
            sem = self.stack.enter_context(self.nc.semaphore("s_" + name))
            ln = Lane(name, sem, 1)
        self.E[name] = dict(h=handle, lane=ln, seen={})

    def dma_lane(self, name):
        sem = self.stack.enter_context(self.nc.semaphore("d_" + name))
        return Lane("d_" + name, sem, 16)

    def _need(self, mylane, reads, writes):
        need = {}

        def add(tok, same_ok):
            if tok is None:
                return
            l, v = tok
            if l is mylane and l.name == "pe":
                return
            if need.get(l.name, (None, 0))[1] < v:
                need[l.name] = (l, v)

        for k in reads:
            add(self.lastw.get(k), False)
        for k in writes:
            add(self.lastw.get(k), True)
            for tok in self.readers.get(k, {}).values():
                add(tok, True)
        return need

    def _wait(self, ename, need):
        e = self.E[ename]
        for l, v in need.values():
            if e["seen"].get(l.name, 0) < v:
                e["h"].wait_ge(l.sem, v)
                e["seen"][l.name] = v

    def _record(self, tok, reads, writes):
        l = tok[0]
        for k in reads:
            self.readers.setdefault(k, {})[l.name] = tok
        for k in writes:
            self.lastw[k] = tok
            self.readers[k] = {}

    def op(self, ename, fn, reads=(), writes=(), inc=True):
        e = self.E[ename]
        lane = e["lane"]
        psr = [k for k in reads if isinstance(k, tuple) and k[0] == "ps"]
        if psr:
            reads = [k for k in reads if k not in psr]
            writes = list(writes) + [k for k in psr if k not in writes]
        self._wait(ename, self._need(lane, reads, writes))
        ins = fn(e["h"])
        tok = (lane, lane.count + 1)
        if inc:
            ins.then_inc(lane.sem, 1)
            lane.count += 1
        self._record(tok, reads, writes)
        return tok

    def dma(self, qname, lane, fn, reads=(), writes=(), serialize=True):
        need = self._need(None, reads, writes)
        if serialize and lane.count > 0:
            if need.get(lane.name, (None, 0))[1] < lane.count:
                need[lane.name] = (lane, lane.count)
        self._wait(qname, need)
        ins = fn(self.E[qname]["h"])
        ins.then_inc(lane.sem, 16)
        lane.count += 16
        tok = (lane, lane.count)
        self._record(tok, reads, writes)
        return tok

    def transfer(self, old_keys, new_keys):
        merged = {}
        for k in old_keys:
            toks = list(self.readers.get(k, {}).values())
            if self.lastw.get(k) is not None:
                toks.append(self.lastw[k])
            for l, v in toks:
                if merged.get(l.name, (None, 0))[1] < v:
                    merged[l.name] = (l, v)
        for k in new_keys:
            self.lastw[k] = None
            self.readers[k] = dict(merged)

    def wait_all(self, ename, lanes):
        need = {l.name: (l, l.count) for l in lanes if l.count > 0}
        self._wait(ename, need)


def _host_consts():
    f32 = np.float32
    lg = np.log((1.0 - np.exp(np.linspace(math.log(1.0 / 32), math.log(1.0 / 512), NH, dtype=f32))).astype(f32)).astype(f32)
    half = HQ // 2
    inv_freq = (10000.0 ** (-(np.arange(half, dtype=f32) * f32(2.0) / f32(HQ)))).astype(f32)
    pos = np.concatenate([np.arange(SEQ, dtype=f32), (16384 + (np.arange(NS) % 4)).astype(f32)])
    ang = (pos[:, None] * inv_freq[None, :]).astype(f32)
    cos = np.cos(ang).astype(f32)
    sin = np.sin(ang).astype(f32)
    cos2 = np.concatenate([cos, cos], axis=1)
    sinm = np.concatenate([-sin, sin], axis=1)
    idx = np.arange(128, dtype=f32)
    diff = idx[None, :] - idx[:, None]
    dmT = np.where(diff[None] >= 0, np.exp(np.maximum(diff, 0.0)[None] * lg[:, None, None]), 0.0).astype(f32)
    dmT = np.ascontiguousarray(dmT.transpose(1, 0, 2))
    xi = np.exp((idx[None, :] + 1.0) * lg[:, None]).astype(f32)
    xiP = np.ascontiguousarray(np.broadcast_to(xi[None], (128, NH, 128))).astype(f32)
    zeta = np.exp((127.0 - idx)[:, None] * lg[None, :]).astype(f32) * f32(HQ ** -0.5)
    zetaP = np.ascontiguousarray(np.broadcast_to(zeta[:, :, None], (128, NH, 128))).astype(f32)
    t = np.arange(NS)
    sq, jj = t // 4, t % 4
    same = (sq[:, None] == sq[None, :])
    dd = (jj[None, :] - jj[:, None]).astype(f32)
    dmS = np.where((same & (dd >= 0))[None], np.exp(np.maximum(dd, 0.0)[None] * lg[:, None, None]), 0.0).astype(f32)
    dmS = np.ascontiguousarray(dmS.transpose(1, 0, 2))
    xis = np.exp((jj.astype(f32)[None, :] + 1.0) * lg[:, None]).astype(f32)
    xiS = np.ascontiguousarray(np.broadcast_to(xis[None], (128, NH, NS))).astype(f32)
    zs = np.exp((3.0 - jj.astype(f32))[:, None] * lg[None, :]).astype(f32) * f32(HQ ** -0.5)
    zetaS = np.ascontiguousarray(np.broadcast_to(zs[:, :, None], (NS, NH, 128))).astype(f32)
    kmask = (sq[:, None] == np.arange(NSEQ)[None, :]).astype(f32)
    cmask = np.ascontiguousarray(np.broadcast_to((np.arange(NSEQ)[:, None] == sq[None, :])[None], (128, NSEQ, NS))).astype(f32)
    gC = [float(np.exp(f32(128.0) * lg[h])) for h in range(NH)]
    g4 = [float(np.exp(f32(4.0) * lg[h])) for h in range(NH)]
    ident = np.eye(128, dtype=f32)
    return dict(cos2=cos2, sinm=sinm, dmT=dmT, xiP=xiP, zetaP=zetaP, dmS=dmS, xiS=xiS, zetaS=zetaS,
                kmask=kmask, cmask=cmask, ident=ident), gC, g4


_CONSTS, _GC, _G4 = _host_consts()
_CONST_SHAPES = {k: list(v.shape) for k, v in _CONSTS.items()}


def build_program(n_tiles=NT, n_layers=DEPTH, stages=("ffn1", "mixer", "ffn2"), taps=()):
    nc = bass.Bass("TRN2", target_bir_lowering=False)
    stack = ExitStack()

    def din(name, shape):
        return nc.dram_tensor(name, list(shape), F32, kind="ExternalInput").ap()

    def dout(name, shape):
        return nc.dram_tensor(name, list(shape), F32, kind="ExternalOutput").ap()

    xp = din("xp", [SEQ, D])
    xs = din("xs", [NS, D])
    sret = din("sret", [DEPTH, NSEQ, NH, HQ, HV])
    sconv = din("sconv", [DEPTH, NSEQ * 2, D])
    norms = din("norms", [DEPTH * 6 * KC, 128])
    convw = din("convw", [DEPTH * 3 * KC, 128])
    w_up = [din("w_ffn1_up", [DEPTH, D, 2 * DFF]), din("w_ffn2_up", [DEPTH, D, 2 * DFF])]
    w_dn = [din("w_ffn1_down", [DEPTH, DFF, D]), din("w_ffn2_down", [DEPTH, DFF, D])]
    w_in = din("w_in", [DEPTH, D, N_IN])
    w_ro = din("w_ret_out", [DEPTH, D, D])
    w_co = din("w_conv_out", [DEPTH, D, D])
    w_oo = din("w_o", [DEPTH, D, D])
    cst = {k: din("c_" + k, shp) for k, shp in _CONST_SHAPES.items()}

    yp = dout("yp", [SEQ, D])
    ys = dout("ys", [NS, D])
    rsp = dout("rsp", [DEPTH, NH, HQ, HV])
    csp = dout("csp", [DEPTH, 2, D])
    rss = dout("rss", [DEPTH, NSEQ, NH, HQ, HV])
    css = dout("css", [DEPTH, NSEQ * 2, D])
    tap_out = {name: dout("tap_" + name, [128, KC, TM]) for name in taps}

    NW = 53200
    big = stack.enter_context(nc.sbuf_tensor("big", [128, NW], F32))
    PS = stack.enter_context(nc.psum_tensor("ps", [128, 8, 512], F32))
    S = Sched(nc, stack)
    S.add_engine("pe", nc.tensor)
    S.add_engine("act", nc.scalar)
    S.add_engine("dve", nc.vector)
    S.add_engine("pool", nc.gpsimd)
    S.add_engine("sp", nc.sync, lane=False)

    cur = [0]

    def alloc(nbytes):
        nbytes = (nbytes + 63) // 64 * 64
        off = cur[0]
        cur[0] += nbytes
        assert cur[0] <= NW * 4, f"SBUF overflow {cur[0]}"
        return off

    def view(off, shape, dt, parts=128):
        n = int(np.prod(shape))
        nb = n * (2 if dt == BF16 else 4)
        ap = big[0:parts, off // 4:(off + nb) // 4]
        if dt != F32:
            ap = ap.bitcast(dt)
        if len(shape) == 2:
            ap = ap.rearrange("p (a b) -> p a b", a=shape[0])
        elif len(shape) == 3:
            ap = ap.rearrange("p (a b c) -> p a b c", a=shape[0], b=shape[1])
        return ap

    def newbuf(shape, dt, parts=128):
        n = int(np.prod(shape))
        return view(alloc(n * (2 if dt == BF16 else 4)), shape, dt, parts)

    xT = newbuf([KC, TM], F32)
    xn = newbuf([KC, TM], BF16)
    rstd = newbuf([TM], F32)
    R1 = alloc(NJ * TM * 2)
    hT = view(R1, [NJ, TM], BF16)
    qT = view(R1, [NH, TM], BF16)
    qxT = view(R1 + NH * TM * 2, [NH, TM], BF16)
    kT = view(R1 + 2 * NH * TM * 2, [NH, TM], BF16)
    kz = view(R1 + 3 * NH * TM * 2, [5, 512], BF16)
    r1_tail = R1 + 3 * NH * TM * 2 + 5 * 512 * 2
    KZh = view(r1_tail, [NSEQ, 128], BF16, parts=NS)
    Qb = view(r1_tail + NSEQ * 128 * 2, [NSEQ, NS], BF16)
    assert r1_tail + NSEQ * 128 * 2 + NSEQ * NS * 2 <= R1 + NJ * TM * 2
    SQo = alloc(KC * TM * 2)
    sq = view(SQo, [KC, TM], BF16)
    mg = view(SQo, [KC, TM], BF16)
    R2 = alloc(20480)
    fT = view(R2, [KC, TM], F32)
    vtm = view(R2, [5, 1024], BF16)
    sgm = view(R2 + 10240, [5, 1024], BF16)
    yT = newbuf([KC, TM], BF16)
    bzT = newbuf([KC, TM], BF16)
    Wsl = [newbuf([4096], BF16) for _ in range(NSLOT)]
    NTMP = 4
    tmp_off = alloc(NTMP * TM * 4)
    tmps = [view(tmp_off + i * TM * 4, [TM], F32) for i in range(NTMP)]
    S32b = view(tmp_off, [8, HV], F32)
    tm4 = [newbuf([1024], F32) for _ in range(2)]
    qk_off = alloc(2 * 512 * 4)
    qk2 = [view(qk_off + i * 2048, [512], F32) for i in range(2)]
    abuf = newbuf([2 + TP], F32)
    abufs = newbuf([NSEQ, 6], F32)
    sTm = [newbuf([NH, 128], BF16) for _ in range(2)]
    Sst = newbuf([DEPTH, NH, HV], F32)
    Sbf = newbuf([DEPTH, NH, HV], BF16)
    S32s = newbuf([4, HV], F32)
    Sbfs = newbuf([4, HV], BF16)
    ident = newbuf([128], F32)
    ones = newbuf([128], BF16)
    DMT = newbuf([NH, 128], F32)
    XI = newbuf([NH, 128], F32)
    ZETA = newbuf([NH, 128], F32)
    cs_off = alloc(2 * 5 * 128 * 4)
    COS = view(cs_off, [5, 128], F32)
    SIN = view(cs_off + 5 * 128 * 4, [5, 128], F32)
    SoutA = view(cs_off, [4, HV], F32)
    SoutB = view(qk_off, [4, HV], F32)

    def sout(i):
        return (SoutA if i < 4 else SoutB)[:, i % 4, :]
    DMS = newbuf([NH, NS], F32)
    XIS = newbuf([NH, NS], F32)
    ZETAS = newbuf([NH, 128], F32)
    KMASK = newbuf([NSEQ], F32)
    CMASK = newbuf([NSEQ, NS], F32)
    gains = newbuf([DEPTH * 6 * KC], F32)
    cw = newbuf([DEPTH * 3 * KC], F32)
    akeep = newbuf([DEPTH, KC, 2], F32)
    epsc = newbuf([1], F32)
    lnwarm = newbuf([1], F32)
    ssq = newbuf([8], F32)
    cstg = newbuf([KC, 2 * NSEQ], F32)
    cprev = newbuf([KC, 2 * NSEQ], F32)
    print("SBUF bytes used per partition:", cur[0])

    wl = [S.dma_lane(f"w{i}") for i in range(NSLOT)]
    ld_lanes = [S.dma_lane(f"ld{i}") for i in range(4)]
    st_lanes = [S.dma_lane(f"st{i}") for i in range(4)]
    ldi = [0]
    sti = [0]

    def load(fn, writes, reads=()):
        ln = ld_lanes[ldi[0] % len(ld_lanes)]
        ldi[0] += 1
        return S.dma("sp", ln, fn, reads=reads, writes=writes)

    def store(fn, reads, q="sp"):
        ln = st_lanes[sti[0] % len(st_lanes)]
        sti[0] += 1
        return S.dma(q, ln, fn, reads=reads, writes=())

    bank_rr = [0]

    def bank():
        b = bank_rr[0] % 6
        bank_rr[0] += 1
        return b

    pair_rr = [0]

    def bank_pair():
        b = (pair_rr[0] % 3) * 2
        pair_rr[0] += 1
        return b

    sstep = [0, 0]

    def sample_step():
        sstep[0] ^= 1
        sstep[1] = 0

    def psum_part(part):
        c0, n = part
        if n == TP:
            b = bank()
            return PS[:, b, :], [("ps", b)]
        sb = 6 + sstep[0]
        a = sstep[1]
        sstep[1] += 1
        assert a < 8
        return PS[:, sb, a * 64:a * 64 + n], [("ps", sb)]

    tmp_rr = [0]

    def tmp():
        i = tmp_rr[0] % NTMP
        tmp_rr[0] += 1
        return tmps[i], ("tmp", i)

    blocks = []

    def wsrc(w, l, r0, nrows, c0, ncols):
        return w[l, r0:r0 + nrows, c0:c0 + ncols].rearrange("(kc p) n -> p kc n", p=128)

    def add_block(srcs):
        blocks.append(srcs)
        return len(blocks) - 1

    class WS:
        issued = 0
        released = 0

    def w_pump():
        while WS.issued < len(blocks) and WS.issued < WS.released + NSLOT:
            b = WS.issued
            s = b % NSLOT
            srcs = blocks[b]
            off = 0
            for i, (ap, kcn, ncols) in enumerate(srcs):
                dst = Wsl[s][:, off:off + kcn * ncols].rearrange("p (kc n) -> p kc n", kc=kcn)
                S.dma("pool", wl[s], (lambda h, dst=dst, ap=ap: h.dma_start(out=dst, in_=ap)),
                      writes=[("w", s)], serialize=False)
                off += kcn * ncols
            WS.issued += 1

    def w_get(b, kcn, ncols):
        assert b < WS.issued, "weight block not issued (too many live slots)"
        s = b % NSLOT
        return Wsl[s][:, 0:kcn * ncols].rearrange("p (kc n) -> p kc n", kc=kcn), ("w", s)

    def w_get2(b, kcn, ncols):
        assert b < WS.issued, "weight block not issued (too many live slots)"
        s = b % NSLOT
        n = kcn * ncols
        return (Wsl[s][:, 0:n].rearrange("p (kc n) -> p kc n", kc=kcn),
                Wsl[s][:, n:2 * n].rearrange("p (kc n) -> p kc n", kc=kcn), ("w", s))

    def w_release(b):
        assert b == WS.released
        WS.released += 1
        w_pump()

    def cload(dst, src, key, parts=128):
        load(lambda h: h.dma_start(out=dst, in_=src), writes=[key])

    cload(ident, cst["ident"], "ident")
    cload(DMT, cst["dmT"], "DMT")
    cload(XI, cst["xiP"], "XI")
    cload(ZETA, cst["zetaP"], "ZETA")
    cload(DMS[0:NS], cst["dmS"], "DMS")
    cload(XIS, cst["xiS"], "XIS")
    cload(ZETAS[0:NS], cst["zetaS"], "ZETAS")
    cload(KMASK[0:NS], cst["kmask"], "KMASK")
    cload(CMASK, cst["cmask"], "CMASK")
    S.op("dve", lambda h: h.memset(ones, 1.0), writes=["ones"])
    S.op("dve", lambda h: h.memset(epsc, EPS), writes=["eps"])
    S.op("dve", lambda h: h.memset(Sst.rearrange("p a b c -> p (a b c)"), 0.0), writes=[("S", 0), ("S", 1)])
    S.op("dve", lambda h: h.memset(Sbf.rearrange("p a b c -> p (a b c)"), 0.0), writes=[("Sbf", 0), ("Sbf", 1)])
    S.op("dve", lambda h: h.memset(akeep.rearrange("p a b c -> p (a b c)"), 0.0), writes=[("akeep", 0), ("akeep", 1)])

    def load_small_T(src, nrows, dst, key):
        st = tm4[0]
        load(lambda h: h.dma_start(out=st[0:nrows, 0:128], in_=src), writes=[("tm4", 0)])
        b = bank()
        S.op("pe", lambda h: h.transpose(PS[:, b, 0:nrows], st[0:nrows, 0:128], ident[0:nrows, 0:nrows]),
             reads=[("tm4", 0), "ident"], writes=[("ps", b)])
        S.op("act", lambda h: h.activation(out=dst, in_=PS[:, b, 0:nrows], func=AF.Copy),
             reads=[("ps", b)], writes=[key])

    load_small_T(norms, DEPTH * 6 * KC, gains, "gains")
    load_small_T(convw, DEPTH * 3 * KC, cw, "cw")
    for l in range(DEPTH):
        for ni in (1, 5):
            o = (l * 6 + ni) * KC
            S.op("act", lambda h, o=o: h.mul(gains[:, o:o + KC], gains[:, o:o + KC], 0.5),
                 reads=["gains"], writes=["gains"])

    def gcol(l, ni, c):
        o = (l * 6 + ni) * KC + c
        return gains[:, o:o + 1]

    def cwcol(l, i, c):
        o = (l * 3 + i) * KC + c
        return cw[:, o:o + 1]

    def parts_of(ti):
        return [(0, TP)] + ([(TP, NS)] if ti == 0 else [])

    def chunks_of(ti):
        if DBG_CHUNKS is not None:
            return [x for x in ([(c, c * 128, 128) for c in range(4)] + [(4, TP, NS)]) if x[0] in DBG_CHUNKS]
        return [(c, c * 128, 128) for c in range(4)] + ([(4, TP, NS)] if ti == 0 else [])

    def allk(name, c0):
        return [(name, c0, c) for c in range(KC)]

    def rstd_from_psum(pap, pk, c0, n, inv_n):
        S.op("act", lambda h: h.activation(out=rstd[:, c0:c0 + n], in_=pap, func=AF.Ln, bias=epsc, scale=inv_n),
             reads=pk + ["eps"], writes=[("rstd", c0)])
        S.op("act", lambda h: h.activation(out=rstd[:, c0:c0 + n], in_=rstd[:, c0:c0 + n], func=AF.Exp, scale=-0.5),
             reads=[("rstd", c0)], writes=[("rstd", c0)])

    def stat_mm(pap, pk, sqb, sqname, c0, n, kc):
        S.op("pe", lambda h: h.matmul(pap, lhsT=ones, rhs=sqb[:, kc, c0:c0 + n], start=(kc == 0), stop=(kc == KC - 1)),
             reads=["ones", (sqname, c0, kc)], writes=pk, inc=(kc == KC - 1))

    def norm_in(l, ni, parts):
        sample_step()
        for part in parts:
            c0, n = part
            pap, pk = psum_part(part)
            for c in range(KC):
                S.op("act", lambda h, c=c: h.activation(out=sq[:, c, c0:c0 + n], in_=xT[:, c, c0:c0 + n], func=AF.Square),
                     reads=[("xT", c0, c)], writes=[("sq", c0, c)])
                stat_mm(pap, pk, sq, "sq", c0, n, c)
            rstd_from_psum(pap, pk, c0, n, 1.0 / D)
            for c in range(KC):
                S.op("dve", lambda h, c=c: h.scalar_tensor_tensor(out=xn[:, c, c0:c0 + n], in0=xT[:, c, c0:c0 + n],
                                                                    scalar=gcol(l, ni, c), in1=rstd[:, c0:c0 + n],
                                                                    op0=ALU.mult, op1=ALU.mult),
                     reads=[("xT", c0, c), ("rstd", c0), "gains"], writes=[("xn", c0, c)])

    def preload_ln_table():
        S.op("act", lambda h: h.activation(out=lnwarm, in_=epsc, func=AF.Ln), reads=["eps"], writes=["lnwarm"])

    def boundary(l, parts, sqb, sqname, l_next, ni_next):
        sample_step()
        for part in parts:
            c0, n = part
            pap, pk = psum_part(part)
            for kc in range(KC):
                stat_mm(pap, pk, sqb, sqname, c0, n, kc)
            rstd_from_psum(pap, pk, c0, n, 1.0 / D)
            if ni_next is not None:
                pap2, pk2 = psum_part(part)
            for c in range(KC):
                S.op("dve", lambda h, c=c: h.tensor_tensor(out=fT[:, c, c0:c0 + n], in0=fT[:, c, c0:c0 + n], in1=rstd[:, c0:c0 + n], op=ALU.mult),
                     reads=[("fT", c0, c), ("rstd", c0)], writes=[("fT", c0, c)])
                S.op("dve", lambda h, c=c: h.tensor_tensor(out=xT[:, c, c0:c0 + n], in0=xT[:, c, c0:c0 + n], in1=fT[:, c, c0:c0 + n], op=ALU.add),
                     reads=[("fT", c0, c), ("xT", c0, c)], writes=[("xT", c0, c)])
                if ni_next is not None:
                    S.op("act", lambda h, c=c: h.activation(out=sq[:, c, c0:c0 + n], in_=xT[:, c, c0:c0 + n], func=AF.Square),
                         reads=[("xT", c0, c)], writes=[("sq", c0, c)])
                    stat_mm(pap2, pk2, sq, "sq", c0, n, c)
                    S.op("act", lambda h, c=c: h.activation(out=fT[:, c, c0:c0 + n], in_=xT[:, c, c0:c0 + n], func=AF.Copy,
                                                            scale=gcol(l_next, ni_next, c)),
                         reads=[("xT", c0, c), "gains"], writes=[("fT", c0, c)])
            if ni_next is not None:
                rstd_from_psum(pap2, pk2, c0, n, 1.0 / D)
                for c in range(KC):
                    S.op("dve", lambda h, c=c: h.tensor_tensor(out=xn[:, c, c0:c0 + n], in0=fT[:, c, c0:c0 + n], in1=rstd[:, c0:c0 + n], op=ALU.mult),
                         reads=[("fT", c0, c), ("rstd", c0)], writes=[("xn", c0, c)])

    def fm_group(pap, pk, wv, wkey, ocol, rhs_buf, rhs_key, c0, n, nk=KC, first=True, last=True, kbase=0, force_inc=False):
        for kc in range(nk):
            S.op("pe", lambda h, kc=kc: h.matmul(pap, lhsT=wv[:, kc, ocol:ocol + 128], rhs=rhs_buf[:, kbase + kc, c0:c0 + n],
                                                    start=(first and kc == 0), stop=(last and kc == nk - 1)),
                 reads=[wkey, ((rhs_key, c0, kbase + kc) if rhs_key == "xn" else (rhs_key, c0))], writes=pk,
                 inc=((last or force_inc) and kc == nk - 1))

    def ffn_blocks(which, l):
        ids = []
        for i in range(6):
            ncols = 512 if i < 5 else 256
            g = add_block([(wsrc(w_up[which], l, 0, D, i * 512, ncols), KC, ncols)])
            u = add_block([(wsrc(w_up[which], l, 0, D, DFF + i * 512, ncols), KC, ncols)])
            ids.append((g, u, ncols))
        dn = []
        for op_ in range(4):
            for kh in range(2):
                dn.append(add_block([(wsrc(w_dn[which], l, kh * 1408, 1408, op_ * 256, 256), 11, 256)]))
        return ids, dn

    def ffn(ti, l, ni_in, ni_out, blk, need_norm_in, nxt):
        parts = parts_of(ti)
        ids, dn = blk
        if need_norm_in:
            norm_in(l, ni_in, parts)
        if DBG_LEVEL == 0:
            return
        for i, (gb, ub, ncols) in enumerate(ids):
            gv, gk = w_get(gb, KC, ncols)
            uv, uk = w_get(ub, KC, ncols)
            for jj in range(ncols // 128):
                j = i * 4 + jj
                sample_step()
                for part in parts:
                    c0, n = part
                    gp, gpk = psum_part(part)
                    up, upk = psum_part(part)
                    fm_group(gp, gpk, gv, gk, jj * 128, xn, "xn", c0, n)
                    fm_group(up, upk, uv, uk, jj * 128, xn, "xn", c0, n)
                    t, tk = tmp()
                    S.op("act", lambda h: h.activation(out=t[:, 0:n], in_=gp, func=AF.Silu), reads=gpk, writes=[tk])
                    S.op("dve", lambda h: h.tensor_tensor(out=hT[:, j, c0:c0 + n], in0=up, in1=t[:, 0:n], op=ALU.mult),
                         reads=upk + [tk], writes=[("hT", c0)])
            w_release(gb)
            w_release(ub)
            if DBG_LEVEL == 1 and i == DBG_NI - 1:
                return
        if DBG_LEVEL == 1:
            return
        for op_ in range(4):
            pa = {}
            sample_step()
            for kh in range(2):
                b = dn[op_ * 2 + kh]
                wv, wk = w_get(b, 11, 256)
                if op_ == 3 and kh == 1:
                    preload_ln_table()
                for o in range(2):
                    oc = op_ * 2 + o
                    for part in parts:
                        c0, n = part
                        if kh == 0:
                            pa[(o, part)] = psum_part(part) if n == TP else (PS[:, 6 + o, 0:n], [("ps", 6 + o)])
                        pap, pk = pa[(o, part)]
                        fm_group(pap, pk, wv, wk, o * 128, hT, "hT", c0, n, nk=11, first=(kh == 0), last=(kh == 1),
                                 kbase=kh * 11, force_inc=True)
                        if kh == 1:
                            S.op("act", lambda h: h.activation(out=fT[:, oc, c0:c0 + n], in_=pap, func=AF.Copy, scale=gcol(l, ni_out, oc)),
                                 reads=pk + ["gains"], writes=[("fT", c0, oc)])
                            S.op("act", lambda h: h.activation(out=sq[:, oc, c0:c0 + n], in_=pap, func=AF.Square),
                                 reads=pk, writes=[("sq", c0, oc)])
                w_release(b)
        if DBG_LEVEL == 2:
            return
        boundary(l, parts, sq, "sq", nxt[0], nxt[1])

    def mixer_blocks(l):
        tm_blocks = [add_block([(wsrc(w_in, l, 0, D, c, 512), KC, 512)]) for c in (0, 1024, 1536, 512, 2048, 2560)]
        conv_blocks = []
        for r in range(2):
            conv_blocks.append(tuple(add_block([(wsrc(w_in, l, 0, D, base + r * 512, 512), KC, 512)])
                                     for base in (4096, 5120, 3072)))
        merge_blocks = []
        for r in range(4):
            merge_blocks.append((add_block([(wsrc(w_in, l, 0, D, 6144 + r * 256, 256), KC, 256),
                                            (wsrc(w_in, l, 0, D, 7168 + r * 256, 256), KC, 256)]),
                                 add_block([(wsrc(w_ro, l, 0, D, r * 256, 256), KC, 256),
                                            (wsrc(w_co, l, 0, D, r * 256, 256), KC, 256)])))
        wo_blocks = [add_block([(wsrc(w_oo, l, 0, D, r * 512, 512), KC, 512)]) for r in range(2)]
        return tm_blocks, conv_blocks, merge_blocks, wo_blocks

    def tm_group(b, wv, wk, c0, ntok):
        for kc in range(KC):
            S.op("pe", lambda h, kc=kc: h.matmul(PS[0:ntok, b, :], lhsT=xn[:, kc, c0:c0 + ntok], rhs=wv[:, kc, :],
                                                    start=(kc == 0), stop=(kc == KC - 1)),
                 reads=[wk, ("xn", 0 if c0 < TP else TP, kc)], writes=[("ps", b)], inc=(kc == KC - 1))

    def rotary(b, ntok, cs, dst, dkey):
        src = PS[0:ntok, b, :].rearrange("p (h d) -> p h d", h=NH)
        d3 = dst[0:ntok, :].rearrange("p (h d) -> p h d", h=NH)
        t, tk = tmp()
        t3 = t[0:ntok, 0:512].rearrange("p (h d) -> p h d", h=NH)
        cosb = COS[0:ntok, cs, :].unsqueeze(1).to_broadcast([ntok, NH, 128])
        S.op("dve", lambda h: h.tensor_tensor(out=d3, in0=src, in1=cosb, op=ALU.mult),
             reads=[("ps", b), "COS"], writes=[dkey])
        sl = SIN[0:ntok, cs, 0:64].unsqueeze(1).to_broadcast([ntok, NH, 64])
        sh = SIN[0:ntok, cs, 64:128].unsqueeze(1).to_broadcast([ntok, NH, 64])
        S.op("dve", lambda h: h.tensor_tensor(out=t3[:, :, 0:64], in0=src[:, :, 64:128], in1=sl, op=ALU.mult),
             reads=[("ps", b), "SIN"], writes=[tk])
        S.op("dve", lambda h: h.tensor_tensor(out=t3[:, :, 64:128], in0=src[:, :, 0:64], in1=sh, op=ALU.mult),
             reads=[("ps", b), "SIN"], writes=[tk])
        S.op("dve", lambda h: h.tensor_tensor(out=dst[0:ntok, :], in0=dst[0:ntok, :], in1=t[0:ntok, 0:512], op=ALU.add),
             reads=[dkey, tk], writes=[dkey])

    def mixer(ti, l, blk):
        parts = parts_of(ti)
        chunks = chunks_of(ti)
        tm_blocks, conv_blocks, merge_blocks, wo_blocks = blk
        last_tile = (ti == n_tiles - 1)
        S.transfer([("hT", 0), ("hT", TP)], [("qT", c) for c in range(5)] + [("qxT", c) for c in range(5)]
                   + [("kT", c) for c in range(5)] + [("kz", c) for c in range(5)] + ["KZh", "Qb"])
        S.transfer(allk("fT", 0) + allk("fT", TP), [("v", c) for c in range(5)] + [("sgm", c) for c in range(5)])
        S.transfer([("Sout", i) for i in range(8)], ["COS", "SIN", ("qk2", 0), ("qk2", 1)])
        load(lambda h: h.dma_start(out=COS[:, 0:4, :], in_=cst["cos2"][ti * TP:(ti + 1) * TP, :].rearrange("(c p) f -> p c f", p=128)),
             writes=["COS"])
        load(lambda h: h.dma_start(out=SIN[:, 0:4, :], in_=cst["sinm"][ti * TP:(ti + 1) * TP, :].rearrange("(c p) f -> p c f", p=128)),
             writes=["SIN"])
        if ti == 0:
            load(lambda h: h.dma_start(out=COS[0:NS, 4, :], in_=cst["cos2"][SEQ:SEQ + NS, :]), writes=["COS"])
            load(lambda h: h.dma_start(out=SIN[0:NS, 4, :], in_=cst["sinm"][SEQ:SEQ + NS, :]), writes=["SIN"])

        if DBG_LEVEL == 10:
            return
        def qk_phase(blk_id, is_q, fill_ids, fill_dst, fill_name, fill_func):
            wv, wk = w_get(blk_id, KC, 512)
            fills = [w_get(fb, KC, 512) for fb in fill_ids]
            pend = None

            def finish(p):
                cs, c0, ntok, rb, rbk = p
                b2 = bank()
                for hh in range(NH):
                    S.op("pe", lambda h, hh=hh: h.transpose(PS[:, b2, hh * 128:hh * 128 + ntok], rb[0:ntok, hh * 128:(hh + 1) * 128],
                                                              ident[0:ntok, 0:ntok]),
                         reads=[rbk, "ident"], writes=[("ps", b2)], inc=(hh == NH - 1))
                src_ = PS[:, b2, :].rearrange("p (h t) -> p h t", h=NH)[:, :, 0:ntok]
                if is_q:
                    S.op("act", lambda h: h.activation(out=qT[:, :, c0:c0 + ntok], in_=src_, func=AF.Copy),
                         reads=[("ps", b2)], writes=[("qT", cs)])
                    xi_c, xkey = (XI, "XI") if ntok == 128 else (XIS, "XIS")
                    S.op("dve", lambda h: h.tensor_tensor(out=qxT[:, :, c0:c0 + ntok], in0=src_, in1=xi_c, op=ALU.mult),
                         reads=[("ps", b2), xkey], writes=[("qxT", cs)])
                else:
                    S.op("act", lambda h: h.activation(out=kT[:, :, c0:c0 + ntok], in_=src_, func=AF.Copy, scale=float(HQ ** -0.5)),
                         reads=[("ps", b2)], writes=[("kT", cs)])
                    z_c, zkey = (ZETA, "ZETA") if ntok == 128 else (ZETAS, "ZETAS")
                    S.op("dve", lambda h: h.tensor_tensor(out=kz[0:ntok, cs, :], in0=rb[0:ntok, :],
                                                            in1=z_c[0:ntok].rearrange("p h d -> p (h d)"), op=ALU.mult),
                         reads=[rbk, zkey], writes=[("kz", cs)])

            for i, (cs, c0, ntok) in enumerate(chunks):
                b = bank()
                tm_group(b, wv, wk, c0, ntok)
                for half, (fv, fk) in enumerate(fills):
                    fb_ = bank()
                    tm_group(fb_, fv, fk, c0, ntok)
                    S.op("act", lambda h, half=half, fb_=fb_: h.activation(out=fill_dst[0:ntok, cs, half * 512:(half + 1) * 512],
                                                                          in_=PS[0:ntok, fb_, :], func=fill_func),
                         reads=[("ps", fb_)], writes=[(fill_name, cs)])
                if pend is not None:
                    finish(pend)
                rb, rbk = qk2[i % 2], ("qk2", i % 2)
                rotary(b, ntok, cs, rb, rbk)
                pend = (cs, c0, ntok, rb, rbk)
            finish(pend)
            w_release(blk_id)
            for fb in fill_ids:
                w_release(fb)

        qk_phase(tm_blocks[0], True, tm_blocks[1:3], vtm, "v", AF.Copy)
        qk_phase(tm_blocks[3], False, tm_blocks[4:6], sgm, "sgm", AF.Silu)
        if DBG_LEVEL == 13:
            return
        if ti == 0:
            st = tm4[0]
            load(lambda h: h.dma_start(out=st[0:2 * NSEQ, :], in_=sconv[l, :, :]), writes=[("tm4", 0)])
            bp = bank_pair()
            for c in range(KC):
                bb, off = bp + c // 4, (c % 4) * 128
                S.op("pe", lambda h, c=c, bb=bb, off=off: h.transpose(PS[:, bb, off:off + 2 * NSEQ], st[0:2 * NSEQ, c * 128:(c + 1) * 128],
                                                                        ident[0:2 * NSEQ, 0:2 * NSEQ]),
                     reads=[("tm4", 0), "ident"], writes=[("ps", bp), ("ps", bp + 1)], inc=(c == KC - 1))
            S.op("act", lambda h: h.activation(out=cprev, in_=PS[:, bp:bp + 2, :].rearrange("p a (c t) -> p (a c) t", c=4)[:, :, 0:2 * NSEQ],
                                               func=AF.Copy),
                 reads=[("ps", bp), ("ps", bp + 1)], writes=["cprev"])

        def conv_gen():
            for r in range(2):
                cgb, xcb, bgb = conv_blocks[r]
                cgv, cgk = w_get(cgb, KC, 512)
                xcv, xck = w_get(xcb, KC, 512)
                bgv, bgk = w_get(bgb, KC, 512)
                for o in range(4):
                    oc = r * 4 + o
                    sample_step()
                    for part in parts:
                        c0, n = part
                        p1, k1 = psum_part(part)
                        p2, k2 = psum_part(part)
                        p3, k3 = psum_part(part)
                        fm_group(p1, k1, cgv, cgk, o * 128, xn, "xn", c0, n)
                        fm_group(p2, k2, xcv, xck, o * 128, xn, "xn", c0, n)
                        fm_group(p3, k3, bgv, bgk, o * 128, xn, "xn", c0, n)
                        t, tk = tmp()
                        S.op("act", lambda h: h.activation(out=t[:, 0:n], in_=p1, func=AF.Copy), reads=k1, writes=[tk])
                        if n == TP:
                            S.op("act", lambda h: h.activation(out=abuf[:, 0:2], in_=akeep[:, l, oc, :], func=AF.Copy),
                                 reads=[("akeep", l)], writes=["abuf"])
                            S.op("dve", lambda h: h.tensor_tensor(out=abuf[:, 2:2 + TP], in0=p2, in1=t[:, 0:n], op=ALU.mult),
                                 reads=k2 + [tk], writes=["abuf"])
                            S.op("act", lambda h: h.activation(out=akeep[:, l, oc, :], in_=abuf[:, TP:TP + 2], func=AF.Copy),
                                 reads=["abuf"], writes=[("akeep", l)])
                            a0, a1, a2 = abuf[:, 0:TP], abuf[:, 1:1 + TP], abuf[:, 2:2 + TP]
                            tz = t[:, 0:n]
                            akey = "abuf"
                        else:
                            S.op("act", lambda h: h.activation(out=abufs[:, :, 0:2], in_=cprev[:, oc, :].rearrange("p (s r) -> p s r", s=NSEQ), func=AF.Copy),
                                 reads=["cprev"], writes=["abufs"])
                            S.op("dve", lambda h: h.tensor_tensor(out=abufs[:, :, 2:6], in0=p2.rearrange("p (s j) -> p s j", s=NSEQ),
                                                                    in1=t[:, 0:n].rearrange("p (s j) -> p s j", s=NSEQ), op=ALU.mult),
                                 reads=k2 + [tk], writes=["abufs"])
                            a0, a1, a2 = abufs[:, :, 0:4], abufs[:, :, 1:5], abufs[:, :, 2:6]
                            tz = t[:, 0:n].rearrange("p (s j) -> p s j", s=NSEQ)
                            akey = "abufs"
                        S.op("dve", lambda h: h.tensor_scalar(out=tz, in0=a2, scalar1=cwcol(l, 2, oc), scalar2=None, op0=ALU.mult),
                             reads=[akey, "cw"], writes=[tk])
                        S.op("dve", lambda h: h.scalar_tensor_tensor(out=tz, in0=a1, scalar=cwcol(l, 1, oc), in1=tz, op0=ALU.mult, op1=ALU.add),
                             reads=[akey, "cw", tk], writes=[tk])
                        S.op("dve", lambda h: h.scalar_tensor_tensor(out=tz, in0=a0, scalar=cwcol(l, 0, oc), in1=tz, op0=ALU.mult, op1=ALU.add),
                             reads=[akey, "cw", tk], writes=[tk])
                        S.op("dve", lambda h: h.tensor_tensor(out=bzT[:, oc, c0:c0 + n], in0=p3, in1=t[:, 0:n], op=ALU.mult),
                             reads=k3 + [tk], writes=[("bzT", c0)])
                        if n == NS:
                            S.op("act", lambda h: h.activation(out=cstg[:, oc, :].rearrange("p (s r) -> p s r", s=NSEQ), in_=abufs[:, :, 4:6], func=AF.Copy),
                                 reads=["abufs"], writes=["cstg"])
                    yield
                w_release(cgb)
                w_release(xcb)
                w_release(bgb)
        conv = conv_gen()

        def conv_step():
            for _ in conv:
                return

        for (cs, c0, ntok) in chunks:
            smp = (ntok == NS)
            b = bank()
            for hh in range(NH):
                S.op("pe", lambda h, hh=hh: h.matmul(PS[0:ntok, b, hh * 128:hh * 128 + ntok], lhsT=kT[:, hh, c0:c0 + ntok],
                                                       rhs=qT[:, hh, c0:c0 + ntok], start=True, stop=True),
                     reads=[("kT", cs), ("qT", cs)], writes=[("ps", b)], inc=(hh == NH - 1))
            sm = sTm[cs % 2]
            smk = ("sTm", cs % 2)
            msk = DMS if smp else DMT
            S.op("dve", lambda h: h.tensor_tensor(out=sm[0:ntok, :, 0:ntok],
                                                    in0=PS[0:ntok, b, :].rearrange("p (h t) -> p h t", h=NH)[:, :, 0:ntok],
                                                    in1=msk[0:ntok], op=ALU.mult),
                 reads=[("ps", b), "DMS" if smp else "DMT"], writes=[smk])
            if not smp:
                conv_step()
            ob = bank_pair() if ti == 0 else 6
            okeys = [("ps", ob), ("ps", ob + 1)]

            def o_ap(hh):
                return PS[0:ntok, ob + hh // 2, (hh % 2) * HV:(hh % 2 + 1) * HV]

            if not smp:
                for hh in range(NH):
                    S.op("pe", lambda h, hh=hh: h.matmul(o_ap(hh), lhsT=sm[0:ntok, hh, 0:ntok], rhs=vtm[0:ntok, cs, hh * HV:(hh + 1) * HV],
                                                           start=True, stop=False),
                         reads=[smk, ("v", cs)], writes=okeys, inc=False)
                    S.op("pe", lambda h, hh=hh: h.matmul(o_ap(hh), lhsT=qxT[:, hh, c0:c0 + ntok], rhs=Sbf[:, l, hh, :],
                                                           start=False, stop=True),
                         reads=[("qxT", cs), ("Sbf", l)], writes=okeys, inc=(hh == NH - 1))
                S.op("dve", lambda h: h.memset(ssq[:, 0:4], 0.0), writes=["ssq"])
                sb_ = bank_pair()
                skeys = [("ps", sb_), ("ps", sb_ + 1)]
                for hh in range(NH):
                    S.op("pe", lambda h, hh=hh: h.matmul(PS[:, sb_ + hh // 2, (hh % 2) * HV:(hh % 2 + 1) * HV],
                                                           lhsT=kz[0:ntok, cs, hh * 128:(hh + 1) * 128], rhs=vtm[0:ntok, cs, hh * HV:(hh + 1) * HV],
                                                           start=True, stop=True),
                         reads=[("kz", cs), ("v", cs)], writes=skeys, inc=(hh == NH - 1))
                for hh in range(NH):
                    S.op("dve", lambda h, hh=hh: h.scalar_tensor_tensor(out=Sst[:, l, hh, :], in0=Sst[:, l, hh, :], scalar=_GC[hh],
                                                                          in1=PS[:, sb_ + hh // 2, (hh % 2) * HV:(hh % 2 + 1) * HV],
                                                                          op0=ALU.mult, op1=ALU.add),
                         reads=skeys + [("S", l)], writes=[("S", l)])
                need_sbf_cast = True
                if last_tile and cs == 3:
                    store(lambda h: h.dma_start(out=rsp[l].rearrange("h d e -> d h e"), in_=Sst[:, l]), reads=[("S", l)])
            else:
                its = [(hh, s) for hh in range(NH) for s in range(NSEQ)]
                PF = 3

                PF = 6

                def emit_load(it):
                    hh_, s_ = its[it]
                    i8_ = it % 8
                    load(lambda h: h.dma_start(out=S32b[:, i8_, :], in_=sret[l, s_, hh_]), writes=[("S32b", i8_)])

                S.transfer(["COS", "SIN", ("qk2", 0), ("qk2", 1)], [("Sout", i) for i in range(8)])
                S.transfer([("tmp", i) for i in range(NTMP)], [("S32b", i) for i in range(8)])
                for it in range(PF):
                    emit_load(it)
                pend_stores = []
                for it, (hh, s) in enumerate(its):
                    if s == 0:
                        S.op("dve", lambda h, hh=hh: h.tensor_tensor(out=Qb, in0=qxT[:, hh, c0:c0 + ntok].unsqueeze(1).to_broadcast([128, NSEQ, NS]),
                                                                       in1=CMASK, op=ALU.mult),
                             reads=[("qxT", cs), "CMASK"], writes=["Qb"])
                        S.op("dve", lambda h, hh=hh: h.tensor_tensor(out=KZh, in0=kz[0:ntok, cs, hh * 128:(hh + 1) * 128].unsqueeze(1).to_broadcast([NS, NSEQ, 128]),
                                                                       in1=KMASK[0:NS, :].unsqueeze(2).to_broadcast([NS, NSEQ, 128]), op=ALU.mult),
                             reads=[("kz", cs), "KMASK"], writes=["KZh"])
                        S.op("pe", lambda h, hh=hh: h.matmul(o_ap(hh), lhsT=sm[0:ntok, hh, 0:ntok], rhs=vtm[0:ntok, cs, hh * HV:(hh + 1) * HV],
                                                               start=True, stop=False),
                             reads=[smk, ("v", cs)], writes=okeys, inc=True)
                    i4 = it % 4
                    s32, s32k = S32b[:, it % 8, :], ("S32b", it % 8)
                    sbf, sbfk = Sbfs[:, i4, :], ("Sbfs", i4)
                    S.op("act", lambda h, s32=s32, sbf=sbf: h.activation(out=sbf, in_=s32, func=AF.Copy), reads=[s32k], writes=[sbfk])
                    if len(pend_stores) >= 3:
                        pend_stores.pop(0)()
                    S.op("pe", lambda h, hh=hh, s=s, sbf=sbf: h.matmul(o_ap(hh), lhsT=Qb[:, s, :], rhs=sbf, start=False, stop=(s == NSEQ - 1)),
                         reads=["Qb", sbfk], writes=okeys, inc=True)
                    hb = i4 % 2
                    up_ap = PS[:, 6 + hb, 0:HV]
                    S.op("pe", lambda h, hh=hh, s=s, up_ap=up_ap: h.matmul(up_ap, lhsT=KZh[:, s, :], rhs=vtm[0:ntok, cs, hh * HV:(hh + 1) * HV],
                                                                          start=True, stop=True),
                         reads=["KZh", ("v", cs)], writes=[("ps", 6 + hb)], inc=True)
                    so, sok = sout(it % 8), ("Sout", it % 8)
                    S.op("dve", lambda h, hh=hh, s32=s32, so=so, up_ap=up_ap: h.scalar_tensor_tensor(out=so, in0=s32, scalar=_G4[hh], in1=up_ap,
                                                                                                      op0=ALU.mult, op1=ALU.add),
                         reads=[("ps", 6 + hb), s32k], writes=[sok])
                    if it + PF < len(its):
                        emit_load(it + PF)
                    pend_stores.append(lambda s=s, hh=hh, so=so, sok=sok:
                                       store(lambda h: h.dma_start(out=rss[l, s, hh], in_=so), reads=[sok], q="act"))
                for ps_ in pend_stores:
                    ps_()
                S.transfer([("S32b", i) for i in range(8)], [("tmp", i) for i in range(NTMP)])
            ytm, ytk = tm4[cs % 2], ("tm4", cs % 2)
            if smp:
                S.op("dve", lambda h: h.memset(ssq[:, 0:4], 0.0), writes=["ssq"])
            for hh in range(NH):
                S.op("act", lambda h, hh=hh: h.activation(out=ytm[0:ntok, hh * HV:(hh + 1) * HV], in_=o_ap(hh), func=AF.Square,
                                                            accum_out=ssq[0:ntok, hh:hh + 1]),
                     reads=okeys, writes=[ytk, "ssq"])
            S.op("act", lambda h: h.activation(out=ssq[0:ntok, 4:8], in_=ssq[0:ntok, 0:4], func=AF.Ln, bias=epsc[0:ntok], scale=1.0 / HV),
                 reads=["ssq", "eps"], writes=["ssq2"])
            S.op("act", lambda h: h.activation(out=ssq[0:ntok, 4:8], in_=ssq[0:ntok, 4:8], func=AF.Exp, scale=-0.5),
                 reads=["ssq2"], writes=["ssq2"])
            for hh in range(NH):
                S.op("dve", lambda h, hh=hh: h.scalar_tensor_tensor(out=ytm[0:ntok, hh * HV:(hh + 1) * HV], in0=o_ap(hh),
                                                                      scalar=ssq[0:ntok, 4 + hh:5 + hh],
                                                                      in1=sgm[0:ntok, cs, hh * HV:(hh + 1) * HV], op0=ALU.mult, op1=ALU.mult),
                     reads=okeys + ["ssq2", ("sgm", cs)], writes=[ytk])
            if not smp:
                S.op("act", lambda h: h.activation(out=Sbf[:, l].rearrange("p a b -> p (a b)"), in_=Sst[:, l].rearrange("p a b -> p (a b)"), func=AF.Copy),
                     reads=[("S", l)], writes=[("Sbf", l)])
                for _ in range({0: 0, 1: 1, 2: 1, 3: 2}[cs]):
                    conv_step()
            tb = bank_pair()
            for c in range(KC):
                bb, off = tb + c // 4, (c % 4) * 128
                S.op("pe", lambda h, c=c, bb=bb, off=off: h.transpose(PS[:, bb, off:off + ntok], ytm[0:ntok, c * 128:(c + 1) * 128],
                                                                        ident[0:ntok, 0:ntok]),
                     reads=[ytk, "ident"], writes=[("ps", tb), ("ps", tb + 1)], inc=(c == KC - 1))
            S.op("act", lambda h: h.activation(out=yT[:, :, c0:c0 + ntok],
                                               in_=PS[:, tb:tb + 2, :].rearrange("p a (c t) -> p (a c) t", c=4)[:, :, 0:ntok], func=AF.Copy),
                 reads=[("ps", tb), ("ps", tb + 1)], writes=[("yT", 0 if c0 < TP else TP)])

        for _ in conv:
            pass
        if ti == 0:
            bp2 = bank_pair()
            for c in range(KC):
                bb, off = bp2 + c // 4, (c % 4) * 128
                S.op("pe", lambda h, c=c, bb=bb, off=off: h.transpose(PS[0:2 * NSEQ, bb, off:off + 128], cstg[:, c, :], ident),
                     reads=["cstg", "ident"], writes=[("ps", bp2), ("ps", bp2 + 1)], inc=(c == KC - 1))
            so = tm4[1]
            S.op("act", lambda h: h.activation(out=so[0:2 * NSEQ, :], in_=PS[0:2 * NSEQ, bp2:bp2 + 2, :].rearrange("p a b -> p (a b)"), func=AF.Copy),
                 reads=[("ps", bp2), ("ps", bp2 + 1)], writes=[("tm4", 1)])
            store(lambda h: h.dma_start(out=css[l, :, :], in_=so[0:2 * NSEQ, :]), reads=[("tm4", 1)])
        if last_tile:
            bp2 = bank_pair()
            for c in range(KC):
                bb, off = bp2 + c // 4, (c % 4) * 128
                S.op("pe", lambda h, c=c, bb=bb, off=off: h.transpose(PS[0:2, bb, off:off + 128], akeep[:, l, c, :], ident),
                     reads=[("akeep", l), "ident"], writes=[("ps", bp2), ("ps", bp2 + 1)], inc=(c == KC - 1))
            so = tm4[1]
            S.op("act", lambda h: h.activation(out=so[0:2, :], in_=PS[0:2, bp2:bp2 + 2, :].rearrange("p a b -> p (a b)"), func=AF.Copy),
                 reads=[("ps", bp2), ("ps", bp2 + 1)], writes=[("tm4", 1)])
            store(lambda h: h.dma_start(out=csp[l, :, :], in_=so[0:2, :]), reads=[("tm4", 1)])

        S.transfer(allk("sq", 0) + allk("sq", TP), [("mg", 0), ("mg", TP)])
        for r in range(4):
            gb_, ob_ = merge_blocks[r]
            grv, gcv, grk = w_get2(gb_, KC, 256)
            gck = grk
            rov, cov, rok = w_get2(ob_, KC, 256)
            cok = rok
            for o in range(2):
                oc = r * 2 + o
                sample_step()
                for part in parts:
                    c0, n = part
                    p1, k1 = psum_part(part)
                    p2, k2 = psum_part(part)
                    p3, k3 = psum_part(part)
                    p4, k4 = psum_part(part)
                    fm_group(p1, k1, grv, grk, o * 128, xn, "xn", c0, n)
                    fm_group(p2, k2, rov, rok, o * 128, yT, "yT", c0, n)
                    fm_group(p3, k3, gcv, gck, o * 128, xn, "xn", c0, n)
                    fm_group(p4, k4, cov, cok, o * 128, bzT, "bzT", c0, n)
                    t1, tk1 = tmp()
                    t2, tk2 = tmp()
                    S.op("act", lambda h: h.activation(out=t1[:, 0:n], in_=p1, func=AF.Sigmoid), reads=k1, writes=[tk1])
                    S.op("dve", lambda h: h.tensor_tensor(out=t1[:, 0:n], in0=p2, in1=t1[:, 0:n], op=ALU.mult), reads=k2 + [tk1], writes=[tk1])
                    S.op("act", lambda h: h.activation(out=t2[:, 0:n], in_=p3, func=AF.Sigmoid), reads=k3, writes=[tk2])
                    S.op("dve", lambda h: h.tensor_tensor(out=t2[:, 0:n], in0=p4, in1=t2[:, 0:n], op=ALU.mult), reads=k4 + [tk2], writes=[tk2])
                    S.op("dve", lambda h: h.tensor_tensor(out=mg[:, oc, c0:c0 + n], in0=t1[:, 0:n], in1=t2[:, 0:n], op=ALU.add),
                         reads=[tk1, tk2], writes=[("mg", c0)])
            for bq in (gb_, ob_):
                w_release(bq)
        if DBG_LEVEL == 17:
            return
        S.transfer([("v", c) for c in range(5)] + [("sgm", c) for c in range(5)], allk("fT", 0) + allk("fT", TP))
        S.transfer([("yT", 0), ("yT", TP)], allk("sq2", 0) + allk("sq2", TP))
        for r in range(2):
            wv, wk = w_get(wo_blocks[r], KC, 512)
            if r == 1:
                preload_ln_table()
            for o in range(4):
                oc = r * 4 + o
                sample_step()
                for part in parts:
                    c0, n = part
                    pap, pk = psum_part(part)
                    fm_group(pap, pk, wv, wk, o * 128, mg, "mg", c0, n)
                    S.op("act", lambda h: h.activation(out=fT[:, oc, c0:c0 + n], in_=pap, func=AF.Copy, scale=gcol(l, 3, oc)),
                         reads=pk + ["gains"], writes=[("fT", c0, oc)])
                    S.op("act", lambda h: h.activation(out=yT[:, oc, c0:c0 + n], in_=pap, func=AF.Square),
                         reads=pk, writes=[("sq2", c0, oc)])
            w_release(wo_blocks[r])
        S.transfer([("mg", 0), ("mg", TP)], allk("sq", 0) + allk("sq", TP))
        boundary(l, parts, yT, "sq2", l, 4)
        S.transfer(allk("sq2", 0) + allk("sq2", TP), [("yT", 0), ("yT", TP)])
        S.transfer([("qT", c) for c in range(5)] + [("qxT", c) for c in range(5)] + [("kT", c) for c in range(5)]
                   + [("kz", c) for c in range(5)] + ["KZh", "Qb"], [("hT", 0), ("hT", TP)])

    def load_x(ti):
        for (cs, c0, ntok) in chunks_of(ti):
            st, stk = tm4[cs % 2], ("tm4", cs % 2)
            if ntok == 128:
                r0 = ti * TP + cs * 128
                load(lambda h: h.dma_start(out=st[0:ntok, :], in_=xp[r0:r0 + ntok, :]), writes=[stk])
            else:
                load(lambda h: h.dma_start(out=st[0:ntok, :], in_=xs[:, :]), writes=[stk])
            tb = bank_pair()
            for c in range(KC):
                bb, off = tb + c // 4, (c % 4) * 128
                S.op("pe", lambda h, c=c, bb=bb, off=off: h.transpose(PS[:, bb, off:off + ntok], st[0:ntok, c * 128:(c + 1) * 128],
                                                                        ident[0:ntok, 0:ntok]),
                     reads=[stk, "ident"], writes=[("ps", tb), ("ps", tb + 1)], inc=(c == KC - 1))
            S.op("act", lambda h: h.activation(out=xT[:, :, c0:c0 + ntok],
                                               in_=PS[:, tb:tb + 2, :].rearrange("p a (c t) -> p (a c) t", c=4)[:, :, 0:ntok], func=AF.Copy),
                 reads=[("ps", tb), ("ps", tb + 1)], writes=allk("xT", 0 if c0 < TP else TP))

    def store_y(ti):
        for (cs, c0, ntok) in chunks_of(ti):
            st, stk = tm4[cs % 2], ("tm4", cs % 2)
            tb = bank_pair()
            for c in range(KC):
                bb, off = tb + c // 4, (c % 4) * 128
                S.op("pe", lambda h, c=c, bb=bb, off=off: h.transpose(PS[0:ntok, bb, off:off + 128], xT[:, c, c0:c0 + ntok], ident),
                     reads=[("xT", 0 if c0 < TP else TP, c), "ident"], writes=[("ps", tb), ("ps", tb + 1)], inc=(c == KC - 1))
            S.op("act", lambda h: h.activation(out=st[0:ntok, :], in_=PS[0:ntok, tb:tb + 2, :].rearrange("p a b -> p (a b)"), func=AF.Copy),
                 reads=[("ps", tb), ("ps", tb + 1)], writes=[stk])
            if ntok == 128:
                r0 = ti * TP + cs * 128
                store(lambda h: h.dma_start(out=yp[r0:r0 + ntok, :], in_=st[0:ntok, :]), reads=[stk])
            else:
                store(lambda h: h.dma_start(out=ys[:, :], in_=st[0:ntok, :]), reads=[stk])

    def tap(name, ti):
        if name in tap_out and ti == 0:
            store(lambda h: h.dma_start(out=tap_out[name], in_=xT), reads=allk("xT", 0) + allk("xT", TP))

    plan = []
    for ti in range(n_tiles):
        for l in range(n_layers):
            ent = {}
            if "ffn1" in stages:
                ent["ffn1"] = ffn_blocks(0, l)
            if "mixer" in stages:
                ent["mixer"] = mixer_blocks(l)
            if "ffn2" in stages:
                ent["ffn2"] = ffn_blocks(1, l)
            plan.append((ti, l, ent))
    w_pump()
    full = ("ffn1" in stages and "mixer" in stages and "ffn2" in stages)
    for (ti, l, ent) in plan:
        if l == 0:
            load_x(ti)
        if full:
            ffn(ti, l, 0, 1, ent["ffn1"], need_norm_in=(l == 0), nxt=(l, 2))
            tap(f"ffn1_{l}", ti)
            mixer(ti, l, ent["mixer"])
            tap(f"mixer_{l}", ti)
            ffn(ti, l, 4, 5, ent["ffn2"], need_norm_in=False, nxt=((l + 1, 0) if l + 1 < n_layers else (None, None)))
            tap(f"ffn2_{l}", ti)
        else:
            if "ffn1" in ent:
                ffn(ti, l, 0, 1, ent["ffn1"], need_norm_in=True, nxt=(None, None))
                tap(f"ffn1_{l}", ti)
        if l == n_layers - 1:
            store_y(ti)
    S.wait_all("sp", st_lanes + ld_lanes + wl)
    stack.close()
    print("instruction counts:", {k: (v["lane"].count if v["lane"] else None) for k, v in S.E.items()}, "weight blocks:", len(blocks))
    return nc


_PROGRAM = None


def _get_program():
    global _PROGRAM
    if _PROGRAM is None:
        _PROGRAM = build_program()
    return _PROGRAM


def make_in_maps(inputs):
    f = lambda a: np.ascontiguousarray(np.asarray(a, dtype=np.float32))
    x_prompt = f(inputs["x_prompt"])
    x_sample = f(inputs["x_sample"])
    state_ret = f(inputs["state_ret"])
    state_conv = f(inputs["state_conv"])
    shared = {
        "norms": f(inputs["norms"]).reshape(DEPTH * 6 * KC, 128),
        "convw": f(inputs["conv_w"]).reshape(DEPTH * 3 * KC, 128),
        "w_ffn1_up": f(inputs["w_ffn1_up"]), "w_ffn2_up": f(inputs["w_ffn2_up"]),
        "w_ffn1_down": f(inputs["w_ffn1_down"]), "w_ffn2_down": f(inputs["w_ffn2_down"]),
        "w_in": f(inputs["w_in"]), "w_ret_out": f(inputs["w_ret_out"]),
        "w_conv_out": f(inputs["w_conv_out"]), "w_o": f(inputs["w_o"]),
    }
    for k, v in _CONSTS.items():
        shared["c_" + k] = v
    in_maps = []
    for c in range(NCORES):
        m = dict(shared)
        m["xp"] = x_prompt[c]
        m["xs"] = np.ascontiguousarray(x_sample[c * NSEQ:(c + 1) * NSEQ].reshape(NS, D))
        m["sret"] = np.ascontiguousarray(state_ret[:, c * NSEQ:(c + 1) * NSEQ])
        m["sconv"] = np.ascontiguousarray(state_conv[:, c * NSEQ:(c + 1) * NSEQ].reshape(DEPTH, NSEQ * 2, D))
        in_maps.append(m)
    return in_maps


def kernel(**inputs):
    nc = _get_program()
    in_maps = make_in_maps(inputs)
    res = run_bass_kernel_spmd(nc, in_maps, core_ids=list(range(NCORES)))
    R = res.results
    y_prompt = np.stack([R[c]["yp"] for c in range(NCORES)], axis=0)
    y_sample = np.concatenate([R[c]["ys"].reshape(NSEQ, 4, D) for c in range(NCORES)], axis=0)
    ret_p = np.stack([R[c]["rsp"] for c in range(NCORES)], axis=1)
    conv_p = np.stack([R[c]["csp"] for c in range(NCORES)], axis=1)
    ret_s = np.concatenate([R[c]["rss"] for c in range(NCORES)], axis=1)
    conv_s = np.concatenate([R[c]["css"].reshape(DEPTH, NSEQ, 2, D) for c in range(NCORES)], axis=1)
    return (y_prompt.astype(np.float32), y_sample.astype(np.float32), ret_p.astype(np.float32),
            conv_p.astype(np.float32), ret_s.astype(np.float32), conv_s.astype(np.float32))
```

```python
import math
from contextlib import ExitStack

import numpy as np
import concourse.bass as bass
import concourse.mybir as mybir
from concourse.bass_utils import run_bass_kernel_spmd

F32 = mybir.dt.float32
BF16 = mybir.dt.bfloat16
AF = mybir.ActivationFunctionType
ALU = mybir.AluOpType

D = 1024
KC = 8
DFF = 2816
NJ = 22
NH = 4
HQ = 128
HV = 256
DEPTH = 2
SEQ = 2048
TP = 512
NT = 4
NS = 64
NSEQ = 16
TM = TP + NS
EPS = 1e-6
NSLOT = 5
N_IN = 8192
NCORES = 8
DBG_LEVEL = 9
DBG_NI = 6
DBG_CHUNKS = None
DBG_SUB = 0


class Lane:
    def __init__(self, name, sem, unit):
        self.name, self.sem, self.unit, self.count = name, sem, unit, 0


class Sched:
    def __init__(self, nc, stack):
        self.nc, self.stack = nc, stack
        self.E = {}
        self.lastw = {}
        self.readers = {}
        self.nlanes = 0

    def add_engine(self, name, handle, lane=True):
        ln = None
        if lane:
            sem = self.stack.enter_context(self.nc.semaphore("s_" + name))
            ln = Lane(name, sem, 1)
        self.E[name] = dict(h=handle, lane=ln, seen={})

    def dma_lane(self, name):
        sem = self.stack.enter_context(self.nc.semaphore("d_" + name))
        return Lane("d_" + name, sem, 16)

    def _need(self, mylane, reads, writes):
        need = {}

        def add(tok, same_ok):
            if tok is None:
                return
            l, v = tok
            if l is mylane and l.name == "pe":
                return
            if need.get(l.name, (None, 0))[1] < v:
                need[l.name] = (l, v)

        for k in reads:
            add(self.lastw.get(k), False)
        for k in writes:
            add(self.lastw.get(k), True)
            for tok in self.readers.get(k, {}).values():
                add(tok, True)
        return need

    def _wait(self, ename, need):
        e = self.E[ename]
        for l, v in need.values():
            if e["seen"].get(l.name, 0) < v:
                e["h"].wait_ge(l.sem, v)
                e["seen"][l.name] = v

    def _record(self, tok, reads, writes):
        l = tok[0]
        for k in reads:
            self.readers.setdefault(k, {})[l.name] = tok
        for k in writes:
            self.lastw[k] = tok
            self.readers[k] = {}

    def op(self, ename, fn, reads=(), writes=(), inc=True):
        e = self.E[ename]
        lane = e["lane"]
        psr = [k for k in reads if isinstance(k, tuple) and k[0] == "ps"]
        if psr:
            reads = [k for k in reads if k not in psr]
            writes = list(writes) + [k for k in psr if k not in writes]
        self._wait(ename, self._need(lane, reads, writes))
        ins = fn(e["h"])
        tok = (lane, lane.count + 1)
        if inc:
            ins.then_inc(lane.sem, 1)
            lane.count += 1
        self._record(tok, reads, writes)
        return tok

    def dma(self, qname, lane, fn, reads=(), writes=(), serialize=True):
        need = self._need(None, reads, writes)
        if serialize and lane.count > 0:
            if need.get(lane.name, (None, 0))[1] < lane.count:
                need[lane.name] = (lane, lane.count)
        self._wait(qname, need)
        ins = fn(self.E[qname]["h"])
        ins.then_inc(lane.sem, 16)
        lane.count += 16
        tok = (lane, lane.count)
        self._record(tok, reads, writes)
        return tok

    def transfer(self, old_keys, new_keys):
        merged = {}
        for k in old_keys:
            toks = list(self.readers.get(k, {}).values())
            if self.lastw.get(k) is not None:
                toks.append(self.lastw[k])
            for l, v in toks:
                if merged.get(l.name, (None, 0))[1] < v:
                    merged[l.name] = (l, v)
        for k in new_keys:
            self.lastw[k] = None
            self.readers[k] = dict(merged)

    def wait_all(self, ename, lanes):
        need = {l.name: (l, l.count) for l in lanes if l.count > 0}
        self._wait(ename, need)


def _host_consts():
    f32 = np.float32
    lg = np.log((1.0 - np.exp(np.linspace(math.log(1.0 / 32), math.log(1.0 / 512), NH, dtype=f32))).astype(f32)).astype(f32)
    half = HQ // 2
    inv_freq = (10000.0 ** (-(np.arange(half, dtype=f32) * f32(2.0) / f32(HQ)))).astype(f32)
    pos = np.concatenate([np.arange(SEQ, dtype=f32), (16384 + (np.arange(NS) % 4)).astype(f32)])
    ang = (pos[:, None] * inv_freq[None, :]).astype(f32)
    cos = np.cos(ang).astype(f32)
    sin = np.sin(ang).astype(f32)
    cos2 = np.concatenate([cos, cos], axis=1)
    sinm = np.concatenate([-sin, sin], axis=1)
    idx = np.arange(128, dtype=f32)
    diff = idx[None, :] - idx[:, None]
    dmT = np.where(diff[None] >= 0, np.exp(np.maximum(diff, 0.0)[None] * lg[:, None, None]), 0.0).astype(f32)
    dmT = np.ascontiguousarray(dmT.transpose(1, 0, 2))
    xi = np.exp((idx[None, :] + 1.0) * lg[:, None]).astype(f32)
    xiP = np.ascontiguousarray(np.broadcast_to(xi[None], (128, NH, 128))).astype(f32)
    zeta = np.exp((127.0 - idx)[:, None] * lg[None, :]).astype(f32) * f32(HQ ** -0.5)
    zetaP = np.ascontiguousarray(np.broadcast_to(zeta[:, :, None], (128, NH, 128))).astype(f32)
    t = np.arange(NS)
    sq, jj = t // 4, t % 4
    same = (sq[:, None] == sq[None, :])
    dd = (jj[None, :] - jj[:, None]).astype(f32)
    dmS = np.where((same & (dd >= 0))[None], np.exp(np.maximum(dd, 0.0)[None] * lg[:, None, None]), 0.0).astype(f32)
    dmS = np.ascontiguousarray(dmS.transpose(1, 0, 2))
    xis = np.exp((jj.astype(f32)[None, :] + 1.0) * lg[:, None]).astype(f32)
    xiS = np.ascontiguousarray(np.broadcast_to(xis[None], (128, NH, NS))).astype(f32)
    zs = np.exp((3.0 - jj.astype(f32))[:, None] * lg[None, :]).astype(f32) * f32(HQ ** -0.5)
    zetaS = np.ascontiguousarray(np.broadcast_to(zs[:, :, None], (NS, NH, 128))).astype(f32)
    kmask = (sq[:, None] == np.arange(NSEQ)[None, :]).astype(f32)
    cmask = np.ascontiguousarray(np.broadcast_to((np.arange(NSEQ)[:, None] == sq[None, :])[None], (128, NSEQ, NS))).astype(f32)
    gC = [float(np.exp(f32(128.0) * lg[h])) for h in range(NH)]
    g4 = [float(np.exp(f32(4.0) * lg[h])) for h in range(NH)]
    ident = np.eye(128, dtype=f32)
    return dict(cos2=cos2, sinm=sinm, dmT=dmT, xiP=xiP, zetaP=zetaP, dmS=dmS, xiS=xiS, zetaS=zetaS,
                kmask=kmask, cmask=cmask, ident=ident), gC, g4


_CONSTS, _GC, _G4 = _host_consts()
_CONST_SHAPES = {k: list(v.shape) for k, v in _CONSTS.items()}


def build_program(n_tiles=NT, n_layers=DEPTH, stages=("ffn1", "mixer", "ffn2"), taps=()):
    nc = bass.Bass("TRN2", target_bir_lowering=False)
    stack = ExitStack()

    def din(name, shape):
        return nc.dram_tensor(name, list(shape), F32, kind="ExternalInput").ap()

    def dout(name, shape):
        return nc.dram_tensor(name, list(shape), F32, kind="ExternalOutput").ap()

    xp = din("xp", [SEQ, D])
    xs = din("xs", [NS, D])
    sret = din("sret", [DEPTH, NSEQ, NH, HQ, HV])
    sconv = din("sconv", [DEPTH, NSEQ * 2, D])
    norms = din("norms", [DEPTH * 6 * KC, 128])
    convw = din("convw", [DEPTH * 3 * KC, 128])
    w_up = [din("w_ffn1_up", [DEPTH, D, 2 * DFF]), din("w_ffn2_up", [DEPTH, D, 2 * DFF])]
    w_dn = [din("w_ffn1_down", [DEPTH, DFF, D]), din("w_ffn2_down", [DEPTH, DFF, D])]
    w_in = din("w_in", [DEPTH, D, N_IN])
    w_ro = din("w_ret_out", [DEPTH, D, D])
    w_co = din("w_conv_out", [DEPTH, D, D])
    w_oo = din("w_o", [DEPTH, D, D])
    cst = {k: din("c_" + k, shp) for k, shp in _CONST_SHAPES.items()}

    yp = dout("yp", [SEQ, D])
    ys = dout("ys", [NS, D])
    rsp = dout("rsp", [DEPTH, NH, HQ, HV])
    csp = dout("csp", [DEPTH, 2, D])
    rss = dout("rss", [DEPTH, NSEQ, NH, HQ, HV])
    css = dout("css", [DEPTH, NSEQ * 2, D])
    tap_out = {name: dout("tap_" + name, [128, KC, TM]) for name in taps}

    NW = 53200
    big = stack.enter_context(nc.sbuf_tensor("big", [128, NW], F32))
    PS = stack.enter_context(nc.psum_tensor("ps", [128, 8, 512], F32))
    S = Sched(nc, stack)
    S.add_engine("pe", nc.tensor)
    S.add_engine("act", nc.scalar)
    S.add_engine("dve", nc.vector)
    S.add_engine("pool", nc.gpsimd)
    S.add_engine("sp", nc.sync, lane=False)

    cur = [0]

    def alloc(nbytes):
        nbytes = (nbytes + 63) // 64 * 64
        off = cur[0]
        cur[0] += nbytes
        assert cur[0] <= NW * 4, f"SBUF overflow {cur[0]}"
        return off

    def view(off, shape, dt, parts=128):
        n = int(np.prod(shape))
        nb = n * (2 if dt == BF16 else 4)
        ap = big[0:parts, off // 4:(off + nb) // 4]
        if dt != F32:
            ap = ap.bitcast(dt)
        if len(shape) == 2:
            ap = ap.rearrange("p (a b) -> p a b", a=shape[0])
        elif len(shape) == 3:
            ap = ap.rearrange("p (a b c) -> p a b c", a=shape[0], b=shape[1])
        return ap

    def newbuf(shape, dt, parts=128):
        n = int(np.prod(shape))
        return view(alloc(n * (2 if dt == BF16 else 4)), shape, dt, parts)

    xT = newbuf([KC, TM], F32)
    xn = newbuf([KC, TM], BF16)
    rstd = newbuf([TM], F32)
    R1 = alloc(NJ * TM * 2)
    hT = view(R1, [NJ, TM], BF16)
    qT = view(R1, [NH, TM], BF16)
    qxT = view(R1 + NH * TM * 2, [NH, TM], BF16)
    kT = view(R1 + 2 * NH * TM * 2, [NH, TM], BF16)
    kz = view(R1 + 3 * NH * TM * 2, [5, 512], BF16)
    r1_tail = R1 + 3 * NH * TM * 2 + 5 * 512 * 2
    KZh = view(r1_tail, [NSEQ, 128], BF16, parts=NS)
    Qb = view(r1_tail + NSEQ * 128 * 2, [NSEQ, NS], BF16)
    assert r1_tail + NSEQ * 128 * 2 + NSEQ * NS * 2 <= R1 + NJ * TM * 2
    SQo = alloc(KC * TM * 2)
    sq = view(SQo, [KC, TM], BF16)
    mg = view(SQo, [KC, TM], BF16)
    R2 = alloc(20480)
    fT = view(R2, [KC, TM], F32)
    vtm = view(R2, [5, 1024], BF16)
    sgm = view(R2 + 10240, [5, 1024], BF16)
    yT = newbuf([KC, TM], BF16)
    bzT = newbuf([KC, TM], BF16)
    Wsl = [newbuf([4096], BF16) for _ in range(NSLOT)]
    NTMP = 4
    tmp_off = alloc(NTMP * TM * 4)
    tmps = [view(tmp_off + i * TM * 4, [TM], F32) for i in range(NTMP)]
    S32b = view(tmp_off, [8, HV], F32)
    tm4 = [newbuf([1024], F32) for _ in range(2)]
    qk_off = alloc(2 * 512 * 4)
    qk2 = [view(qk_off + i * 2048, [512], F32) for i in range(2)]
    abuf = newbuf([2 + TP], F32)
    abufs = newbuf([NSEQ, 6], F32)
    sTm = [newbuf([NH, 128], BF16) for _ in range(2)]
    Sst = newbuf([DEPTH, NH, HV], F32)
    Sbf = newbuf([DEPTH, NH, HV], BF16)
    S32s = newbuf([4, HV], F32)
    Sbfs = newbuf([4, HV], BF16)
    ident = newbuf([128], F32)
    ones = newbuf([128], BF16)
    DMT = newbuf([NH, 128], F32)
    XI = newbuf([NH, 128], F32)
    ZETA = newbuf([NH, 128], F32)
    cs_off = alloc(2 * 5 * 128 * 4)
    COS = view(cs_off, [5, 128], F32)
    SIN = view(cs_off + 5 * 128 * 4, [5, 128], F32)
    SoutA = view(cs_off, [4, HV], F32)
    SoutB = view(qk_off, [4, HV], F32)

    def sout(i):
        return (SoutA if i < 4 else SoutB)[:, i % 4, :]
    DMS = newbuf([NH, NS], F32)
    XIS = newbuf([NH, NS], F32)
    ZETAS = newbuf([NH, 128], F32)
    KMASK = newbuf([NSEQ], F32)
    CMASK = newbuf([NSEQ, NS], F32)
    gains = newbuf([DEPTH * 6 * KC], F32)
    cw = newbuf([DEPTH * 3 * KC], F32)
    akeep = newbuf([DEPTH, KC, 2], F32)
    epsc = newbuf([1], F32)
    lnwarm = newbuf([1], F32)
    ssq = newbuf([8], F32)
    cstg = newbuf([KC, 2 * NSEQ], F32)
    cprev = newbuf([KC, 2 * NSEQ], F32)
    print("SBUF bytes used per partition:", cur[0])

    wl = [S.dma_lane(f"w{i}") for i in range(NSLOT)]
    wst = [S.dma_lane(f"wst{i}") for i in range(NSLOT)]
    ld_lanes = [S.dma_lane(f"ld{i}") for i in range(4)]
    st_lanes = [S.dma_lane(f"st{i}") for i in range(4)]
    ldi = [0]
    sti = [0]

    def load(fn, writes, reads=()):
        ln = ld_lanes[ldi[0] % len(ld_lanes)]
        ldi[0] += 1
        return S.dma("sp", ln, fn, reads=reads, writes=writes)

    def store(fn, reads, q="sp"):
        ln = st_lanes[sti[0] % len(st_lanes)]
        sti[0] += 1
        return S.dma(q, ln, fn, reads=reads, writes=())

    bank_rr = [0]

    def bank():
        b = bank_rr[0] % 6
        bank_rr[0] += 1
        return b

    pair_rr = [0]

    def bank_pair():
        b = (pair_rr[0] % 3) * 2
        pair_rr[0] += 1
        return b

    sstep = [0, 0]

    def sample_step():
        sstep[0] ^= 1
        sstep[1] = 0

    def psum_part(part):
        c0, n = part
        if n == TP:
            b = bank()
            return PS[:, b, :], [("ps", b)]
        sb = 6 + sstep[0]
        a = sstep[1]
        sstep[1] += 1
        assert a < 8
        return PS[:, sb, a * 64:a * 64 + n], [("ps", sb)]

    tmp_rr = [0]

    def tmp():
        i = tmp_rr[0] % NTMP
        tmp_rr[0] += 1
        return tmps[i], ("tmp", i)

    blocks = []

    def wsrc(w, l, r0, nrows, c0, ncols):
        return w[l, r0:r0 + nrows, c0:c0 + ncols].rearrange("(kc p) n -> p kc n", p=128)

    def add_block(srcs):
        blocks.append(srcs)
        return len(blocks) - 1

    class WS:
        issued = 0
        released = 0

    WS.nblk = None
    WS.scr = None
    WS.scr_pending = []

    def blk_info(b):
        nb = WS.nblk
        return b // (nb * n_layers), (b // nb) % n_layers, b % nb, sum(k * n for _, k, n in blocks[b])

    def scr_store(b):
        ti_, l_, idx_, nel = blk_info(b)
        s = b % NSLOT
        S.dma("pool", wst[s], (lambda h: h.dma_start(out=WS.scr[l_ * WS.nblk + idx_][:, 0:nel], in_=Wsl[s][:, 0:nel])),
              reads=[("w", s)], writes=[("scr", l_, idx_)], serialize=False)

    def w_pump():
        if WS.nblk is None:
            WS.nblk = len(blocks) // (n_tiles * n_layers)
            assert WS.nblk * n_tiles * n_layers == len(blocks)
            if n_tiles > 1:
                WS.scr = nc.dram_tensor("wscr", [n_layers * WS.nblk, 128, 4096], BF16).ap()
        while WS.issued < len(blocks) and WS.issued < WS.released + NSLOT:
            b = WS.issued
            s = b % NSLOT
            ti_, l_, idx_, nel = blk_info(b)
            if ti_ == 0 or WS.scr is None:
                off = 0
                for i, (ap, kcn, ncols) in enumerate(blocks[b]):
                    dst = Wsl[s][:, off:off + kcn * ncols].rearrange("p (kc n) -> p kc n", kc=kcn)
                    S.dma("pool", wl[s], (lambda h, dst=dst, ap=ap: h.dma_start(out=dst, in_=ap)),
                          writes=[("w", s)], serialize=False)
                    off += kcn * ncols
                if WS.scr is not None:
                    WS.scr_pending.append(b)
            else:
                S.dma("pool", wl[s], (lambda h: h.dma_start(out=Wsl[s][:, 0:nel], in_=WS.scr[l_ * WS.nblk + idx_][:, 0:nel])),
                      reads=[("scr", l_, idx_)], writes=[("w", s)], serialize=False)
            WS.issued += 1
            while WS.scr_pending and (WS.scr_pending[0] <= b - 3 or ti_ > 0):
                scr_store(WS.scr_pending.pop(0))

    def w_get(b, kcn, ncols):
        assert b < WS.issued, "weight block not issued (too many live slots)"
        s = b % NSLOT
        return Wsl[s][:, 0:kcn * ncols].rearrange("p (kc n) -> p kc n", kc=kcn), ("w", s)

    def w_get2(b, kcn, ncols):
        assert b < WS.issued, "weight block not issued (too many live slots)"
        s = b % NSLOT
        n = kcn * ncols
        return (Wsl[s][:, 0:n].rearrange("p (kc n) -> p kc n", kc=kcn),
                Wsl[s][:, n:2 * n].rearrange("p (kc n) -> p kc n", kc=kcn), ("w", s))

    def w_release(b):
        assert b == WS.released
        WS.released += 1
        w_pump()

    def cload(dst, src, key, parts=128):
        load(lambda h: h.dma_start(out=dst, in_=src), writes=[key])

    cload(ident, cst["ident"], "ident")
    cload(DMT, cst["dmT"], "DMT")
    cload(XI, cst["xiP"], "XI")
    cload(ZETA, cst["zetaP"], "ZETA")
    cload(DMS[0:NS], cst["dmS"], "DMS")
    cload(XIS, cst["xiS"], "XIS")
    cload(ZETAS[0:NS], cst["zetaS"], "ZETAS")
    cload(KMASK[0:NS], cst["kmask"], "KMASK")
    cload(CMASK, cst["cmask"], "CMASK")
    S.op("dve", lambda h: h.memset(ones, 1.0), writes=["ones"])
    S.op("dve", lambda h: h.memset(epsc, EPS), writes=["eps"])
    S.op("dve", lambda h: h.memset(Sst.rearrange("p a b c -> p (a b c)"), 0.0), writes=[("S", 0), ("S", 1)])
    S.op("dve", lambda h: h.memset(Sbf.rearrange("p a b c -> p (a b c)"), 0.0), writes=[("Sbf", 0), ("Sbf", 1)])
    S.op("dve", lambda h: h.memset(akeep.rearrange("p a b c -> p (a b c)"), 0.0), writes=[("akeep", 0), ("akeep", 1)])

    def load_small_T(src, nrows, dst, key):
        st = tm4[0]
        load(lambda h: h.dma_start(out=st[0:nrows, 0:128], in_=src), writes=[("tm4", 0)])
        b = bank()
        S.op("pe", lambda h: h.transpose(PS[:, b, 0:nrows], st[0:nrows, 0:128], ident[0:nrows, 0:nrows]),
             reads=[("tm4", 0), "ident"], writes=[("ps", b)])
        S.op("act", lambda h: h.activation(out=dst, in_=PS[:, b, 0:nrows], func=AF.Copy),
             reads=[("ps", b)], writes=[key])

    load_small_T(norms, DEPTH * 6 * KC, gains, "gains")
    load_small_T(convw, DEPTH * 3 * KC, cw, "cw")
    for l in range(DEPTH):
        for ni in (1, 5):
            o = (l * 6 + ni) * KC
            S.op("act", lambda h, o=o: h.mul(gains[:, o:o + KC], gains[:, o:o + KC], 0.5),
                 reads=["gains"], writes=["gains"])

    def gcol(l, ni, c):
        o = (l * 6 + ni) * KC + c
        return gains[:, o:o + 1]

    def cwcol(l, i, c):
        o = (l * 3 + i) * KC + c
        return cw[:, o:o + 1]

    def parts_of(ti):
        return [(0, TP)] + ([(TP, NS)] if ti == 0 else [])

    def chunks_of(ti):
        if DBG_CHUNKS is not None:
            return [x for x in ([(c, c * 128, 128) for c in range(4)] + [(4, TP, NS)]) if x[0] in DBG_CHUNKS]
        return [(c, c * 128, 128) for c in range(4)] + ([(4, TP, NS)] if ti == 0 else [])

    def allk(name, c0):
        return [(name, c0, c) for c in range(KC)]

    def rstd_from_psum(pap, pk, c0, n, inv_n):
        S.op("act", lambda h: h.activation(out=rstd[:, c0:c0 + n], in_=pap, func=AF.Ln, bias=epsc, scale=inv_n),
             reads=pk + ["eps"], writes=[("rstd", c0)])
        S.op("act", lambda h: h.activation(out=rstd[:, c0:c0 + n], in_=rstd[:, c0:c0 + n], func=AF.Exp, scale=-0.5),
             reads=[("rstd", c0)], writes=[("rstd", c0)])

    def stat_mm(pap, pk, sqb, sqname, c0, n, kc):
        S.op("pe", lambda h: h.matmul(pap, lhsT=ones, rhs=sqb[:, kc, c0:c0 + n], start=(kc == 0), stop=(kc == KC - 1)),
             reads=["ones", (sqname, c0, kc)], writes=pk, inc=(kc == KC - 1))

    def norm_in(l, ni, parts):
        sample_step()
        for part in parts:
            c0, n = part
            pap, pk = psum_part(part)
            for c in range(KC):
                S.op("act", lambda h, c=c: h.activation(out=sq[:, c, c0:c0 + n], in_=xT[:, c, c0:c0 + n], func=AF.Square),
                     reads=[("xT", c0, c)], writes=[("sq", c0, c)])
                stat_mm(pap, pk, sq, "sq", c0, n, c)
            rstd_from_psum(pap, pk, c0, n, 1.0 / D)
            for c in range(KC):
                S.op("dve", lambda h, c=c: h.scalar_tensor_tensor(out=xn[:, c, c0:c0 + n], in0=xT[:, c, c0:c0 + n],
                                                                    scalar=gcol(l, ni, c), in1=rstd[:, c0:c0 + n],
                                                                    op0=ALU.mult, op1=ALU.mult),
                     reads=[("xT", c0, c), ("rstd", c0), "gains"], writes=[("xn", c0, c)])

    def preload_ln_table():
        S.op("act", lambda h: h.activation(out=lnwarm, in_=epsc, func=AF.Ln), reads=["eps"], writes=["lnwarm"])

    def boundary(l, parts, sqb, sqname, l_next, ni_next):
        sample_step()
        for part in parts:
            c0, n = part
            pap, pk = psum_part(part)
            for kc in range(KC):
                stat_mm(pap, pk, sqb, sqname, c0, n, kc)
            rstd_from_psum(pap, pk, c0, n, 1.0 / D)
            if ni_next is not None:
                pap2, pk2 = psum_part(part)
            for c in range(KC):
                S.op("dve", lambda h, c=c: h.tensor_tensor(out=fT[:, c, c0:c0 + n], in0=fT[:, c, c0:c0 + n], in1=rstd[:, c0:c0 + n], op=ALU.mult),
                     reads=[("fT", c0, c), ("rstd", c0)], writes=[("fT", c0, c)])
                S.op("dve", lambda h, c=c: h.tensor_tensor(out=xT[:, c, c0:c0 + n], in0=xT[:, c, c0:c0 + n], in1=fT[:, c, c0:c0 + n], op=ALU.add),
                     reads=[("fT", c0, c), ("xT", c0, c)], writes=[("xT", c0, c)])
                if ni_next is not None:
                    S.op("act", lambda h, c=c: h.activation(out=sq[:, c, c0:c0 + n], in_=xT[:, c, c0:c0 + n], func=AF.Square),
                         reads=[("xT", c0, c)], writes=[("sq", c0, c)])
                    stat_mm(pap2, pk2, sq, "sq", c0, n, c)
                    S.op("act", lambda h, c=c: h.activation(out=fT[:, c, c0:c0 + n], in_=xT[:, c, c0:c0 + n], func=AF.Copy,
                                                            scale=gcol(l_next, ni_next, c)),
                         reads=[("xT", c0, c), "gains"], writes=[("fT", c0, c)])
            if ni_next is not None:
                rstd_from_psum(pap2, pk2, c0, n, 1.0 / D)
                for c in range(KC):
                    S.op("dve", lambda h, c=c: h.tensor_tensor(out=xn[:, c, c0:c0 + n], in0=fT[:, c, c0:c0 + n], in1=rstd[:, c0:c0 + n], op=ALU.mult),
                         reads=[("fT", c0, c), ("rstd", c0)], writes=[("xn", c0, c)])

    def fm_group(pap, pk, wv, wkey, ocol, rhs_buf, rhs_key, c0, n, nk=KC, first=True, last=True, kbase=0, force_inc=False):
        for kc in range(nk):
            S.op("pe", lambda h, kc=kc: h.matmul(pap, lhsT=wv[:, kc, ocol:ocol + 128], rhs=rhs_buf[:, kbase + kc, c0:c0 + n],
                                                    start=(first and kc == 0), stop=(last and kc == nk - 1)),
                 reads=[wkey, ((rhs_key, c0, kbase + kc) if rhs_key == "xn" else (rhs_key, c0))], writes=pk,
                 inc=((last or force_inc) and kc == nk - 1))

    def ffn_blocks(which, l):
        ids = []
        for i in range(6):
            ncols = 512 if i < 5 else 256
            g = add_block([(wsrc(w_up[which], l, 0, D, i * 512, ncols), KC, ncols)])
            u = add_block([(wsrc(w_up[which], l, 0, D, DFF + i * 512, ncols), KC, ncols)])
            ids.append((g, u, ncols))
        dn = []
        for op_ in range(4):
            for kh in range(2):
                dn.append(add_block([(wsrc(w_dn[which], l, kh * 1408, 1408, op_ * 256, 256), 11, 256)]))
        return ids, dn

    def ffn(ti, l, ni_in, ni_out, blk, need_norm_in, nxt):
        parts = parts_of(ti)
        ids, dn = blk
        if need_norm_in:
            norm_in(l, ni_in, parts)
        if DBG_LEVEL == 0:
            return
        for i, (gb, ub, ncols) in enumerate(ids):
            gv, gk = w_get(gb, KC, ncols)
            uv, uk = w_get(ub, KC, ncols)
            for jj in range(ncols // 128):
                j = i * 4 + jj
                sample_step()
                for part in parts:
                    c0, n = part
                    gp, gpk = psum_part(part)
                    up, upk = psum_part(part)
                    fm_group(gp, gpk, gv, gk, jj * 128, xn, "xn", c0, n)
                    fm_group(up, upk, uv, uk, jj * 128, xn, "xn", c0, n)
                    t, tk = tmp()
                    S.op("act", lambda h: h.activation(out=t[:, 0:n], in_=gp, func=AF.Silu), reads=gpk, writes=[tk])
                    S.op("dve", lambda h: h.tensor_tensor(out=hT[:, j, c0:c0 + n], in0=up, in1=t[:, 0:n], op=ALU.mult),
                         reads=upk + [tk], writes=[("hT", c0)])
            w_release(gb)
            w_release(ub)
            if DBG_LEVEL == 1 and i == DBG_NI - 1:
                return
        if DBG_LEVEL == 1:
            return
        for op_ in range(4):
            pa = {}
            sample_step()
            for kh in range(2):
                b = dn[op_ * 2 + kh]
                wv, wk = w_get(b, 11, 256)
                if op_ == 3 and kh == 1:
                    preload_ln_table()
                for o in range(2):
                    oc = op_ * 2 + o
                    for part in parts:
                        c0, n = part
                        if kh == 0:
                            pa[(o, part)] = psum_part(part) if n == TP else (PS[:, 6 + o, 0:n], [("ps", 6 + o)])
                        pap, pk = pa[(o, part)]
                        fm_group(pap, pk, wv, wk, o * 128, hT, "hT", c0, n, nk=11, first=(kh == 0), last=(kh == 1),
                                 kbase=kh * 11, force_inc=True)
                        if kh == 1:
                            S.op("act", lambda h: h.activation(out=fT[:, oc, c0:c0 + n], in_=pap, func=AF.Copy, scale=gcol(l, ni_out, oc)),
                                 reads=pk + ["gains"], writes=[("fT", c0, oc)])
                            S.op("act", lambda h: h.activation(out=sq[:, oc, c0:c0 + n], in_=pap, func=AF.Square),
                                 reads=pk, writes=[("sq", c0, oc)])
                w_release(b)
        if DBG_LEVEL == 2:
            return
        boundary(l, parts, sq, "sq", nxt[0], nxt[1])

    def mixer_blocks(l):
        tm_blocks = [add_block([(wsrc(w_in, l, 0, D, c, 512), KC, 512)]) for c in (0, 1024, 1536, 512, 2048, 2560)]
        conv_blocks = []
        for r in range(2):
            conv_blocks.append(tuple(add_block([(wsrc(w_in, l, 0, D, base + r * 512, 512), KC, 512)])
                                     for base in (4096, 5120, 3072)))
        merge_blocks = []
        for r in range(4):
            merge_blocks.append((add_block([(wsrc(w_in, l, 0, D, 6144 + r * 256, 256), KC, 256),
                                            (wsrc(w_in, l, 0, D, 7168 + r * 256, 256), KC, 256)]),
                                 add_block([(wsrc(w_ro, l, 0, D, r * 256, 256), KC, 256),
                                            (wsrc(w_co, l, 0, D, r * 256, 256), KC, 256)])))
        wo_blocks = [add_block([(wsrc(w_oo, l, 0, D, r * 512, 512), KC, 512)]) for r in range(2)]
        return tm_blocks, conv_blocks, merge_blocks, wo_blocks

    def tm_group(b, wv, wk, c0, ntok):
        for kc in range(KC):
            S.op("pe", lambda h, kc=kc: h.matmul(PS[0:ntok, b, :], lhsT=xn[:, kc, c0:c0 + ntok], rhs=wv[:, kc, :],
                                                    start=(kc == 0), stop=(kc == KC - 1)),
                 reads=[wk, ("xn", 0 if c0 < TP else TP, kc)], writes=[("ps", b)], inc=(kc == KC - 1))

    def rotary(b, ntok, cs, dst, dkey):
        src = PS[0:ntok, b, :].rearrange("p (h d) -> p h d", h=NH)
        d3 = dst[0:ntok, :].rearrange("p (h d) -> p h d", h=NH)
        t, tk = tmp()
        t3 = t[0:ntok, 0:512].rearrange("p (h d) -> p h d", h=NH)
        cosb = COS[0:ntok, cs, :].unsqueeze(1).to_broadcast([ntok, NH, 128])
        S.op("dve", lambda h: h.tensor_tensor(out=d3, in0=src, in1=cosb, op=ALU.mult),
             reads=[("ps", b), "COS"], writes=[dkey])
        sl = SIN[0:ntok, cs, 0:64].unsqueeze(1).to_broadcast([ntok, NH, 64])
        sh = SIN[0:ntok, cs, 64:128].unsqueeze(1).to_broadcast([ntok, NH, 64])
        S.op("dve", lambda h: h.tensor_tensor(out=t3[:, :, 0:64], in0=src[:, :, 64:128], in1=sl, op=ALU.mult),
             reads=[("ps", b), "SIN"], writes=[tk])
        S.op("dve", lambda h: h.tensor_tensor(out=t3[:, :, 64:128], in0=src[:, :, 0:64], in1=sh, op=ALU.mult),
             reads=[("ps", b), "SIN"], writes=[tk])
        S.op("dve", lambda h: h.tensor_tensor(out=dst[0:ntok, :], in0=dst[0:ntok, :], in1=t[0:ntok, 0:512], op=ALU.add),
             reads=[dkey, tk], writes=[dkey])

    def mixer(ti, l, blk):
        parts = parts_of(ti)
        chunks = chunks_of(ti)
        tm_blocks, conv_blocks, merge_blocks, wo_blocks = blk
        last_tile = (ti == n_tiles - 1)
        S.transfer([("hT", 0), ("hT", TP)], [("qT", c) for c in range(5)] + [("qxT", c) for c in range(5)]
                   + [("kT", c) for c in range(5)] + [("kz", c) for c in range(5)] + ["KZh", "Qb"])
        S.transfer(allk("fT", 0) + allk("fT", TP), [("v", c) for c in range(5)] + [("sgm", c) for c in range(5)])
        S.transfer([("Sout", i) for i in range(8)], ["COS", "SIN", ("qk2", 0), ("qk2", 1)])
        load(lambda h: h.dma_start(out=COS[:, 0:4, :], in_=cst["cos2"][ti * TP:(ti + 1) * TP, :].rearrange("(c p) f -> p c f", p=128)),
             writes=["COS"])
        load(lambda h: h.dma_start(out=SIN[:, 0:4, :], in_=cst["sinm"][ti * TP:(ti + 1) * TP, :].rearrange("(c p) f -> p c f", p=128)),
             writes=["SIN"])
        if ti == 0:
            load(lambda h: h.dma_start(out=COS[0:NS, 4, :], in_=cst["cos2"][SEQ:SEQ + NS, :]), writes=["COS"])
            load(lambda h: h.dma_start(out=SIN[0:NS, 4, :], in_=cst["sinm"][SEQ:SEQ + NS, :]), writes=["SIN"])

        if DBG_LEVEL == 10:
            return
        def qk_phase(blk_id, is_q, fill_ids, fill_dst, fill_name, fill_func):
            wv, wk = w_get(blk_id, KC, 512)
            fills = [w_get(fb, KC, 512) for fb in fill_ids]
            pend = None

            def finish(p):
                cs, c0, ntok, rb, rbk = p
                b2 = bank()
                for hh in range(NH):
                    S.op("pe", lambda h, hh=hh: h.transpose(PS[:, b2, hh * 128:hh * 128 + ntok], rb[0:ntok, hh * 128:(hh + 1) * 128],
                                                              ident[0:ntok, 0:ntok]),
                         reads=[rbk, "ident"], writes=[("ps", b2)], inc=(hh == NH - 1))
                src_ = PS[:, b2, :].rearrange("p (h t) -> p h t", h=NH)[:, :, 0:ntok]
                if is_q:
                    S.op("act", lambda h: h.activation(out=qT[:, :, c0:c0 + ntok], in_=src_, func=AF.Copy),
                         reads=[("ps", b2)], writes=[("qT", cs)])
                    xi_c, xkey = (XI, "XI") if ntok == 128 else (XIS, "XIS")
                    S.op("dve", lambda h: h.tensor_tensor(out=qxT[:, :, c0:c0 + ntok], in0=src_, in1=xi_c, op=ALU.mult),
                         reads=[("ps", b2), xkey], writes=[("qxT", cs)])
                else:
                    S.op("act", lambda h: h.activation(out=kT[:, :, c0:c0 + ntok], in_=src_, func=AF.Copy, scale=float(HQ ** -0.5)),
                         reads=[("ps", b2)], writes=[("kT", cs)])
                    z_c, zkey = (ZETA, "ZETA") if ntok == 128 else (ZETAS, "ZETAS")
                    S.op("dve", lambda h: h.tensor_tensor(out=kz[0:ntok, cs, :], in0=rb[0:ntok, :],
                                                            in1=z_c[0:ntok].rearrange("p h d -> p (h d)"), op=ALU.mult),
                         reads=[rbk, zkey], writes=[("kz", cs)])

            for i, (cs, c0, ntok) in enumerate(chunks):
                b = bank()
                tm_group(b, wv, wk, c0, ntok)
                for half, (fv, fk) in enumerate(fills):
                    fb_ = bank()
                    tm_group(fb_, fv, fk, c0, ntok)
                    S.op("act", lambda h, half=half, fb_=fb_: h.activation(out=fill_dst[0:ntok, cs, half * 512:(half + 1) * 512],
                                                                          in_=PS[0:ntok, fb_, :], func=fill_func),
                         reads=[("ps", fb_)], writes=[(fill_name, cs)])
                if pend is not None:
                    finish(pend)
                rb, rbk = qk2[i % 2], ("qk2", i % 2)
                rotary(b, ntok, cs, rb, rbk)
                pend = (cs, c0, ntok, rb, rbk)
            finish(pend)
            w_release(blk_id)
            for fb in fill_ids:
                w_release(fb)

        qk_phase(tm_blocks[0], True, tm_blocks[1:3], vtm, "v", AF.Copy)
        qk_phase(tm_blocks[3], False, tm_blocks[4:6], sgm, "sgm", AF.Silu)
        if DBG_LEVEL == 13:
            return
        if ti == 0:
            st = tm4[0]
            load(lambda h: h.dma_start(out=st[0:2 * NSEQ, :], in_=sconv[l, :, :]), writes=[("tm4", 0)])
            bp = bank_pair()
            for c in range(KC):
                bb, off = bp + c // 4, (c % 4) * 128
                S.op("pe", lambda h, c=c, bb=bb, off=off: h.transpose(PS[:, bb, off:off + 2 * NSEQ], st[0:2 * NSEQ, c * 128:(c + 1) * 128],
                                                                        ident[0:2 * NSEQ, 0:2 * NSEQ]),
                     reads=[("tm4", 0), "ident"], writes=[("ps", bp), ("ps", bp + 1)], inc=(c == KC - 1))
            S.op("act", lambda h: h.activation(out=cprev, in_=PS[:, bp:bp + 2, :].rearrange("p a (c t) -> p (a c) t", c=4)[:, :, 0:2 * NSEQ],
                                               func=AF.Copy),
                 reads=[("ps", bp), ("ps", bp + 1)], writes=["cprev"])

        def conv_gen():
            for r in range(2):
                cgb, xcb, bgb = conv_blocks[r]
                cgv, cgk = w_get(cgb, KC, 512)
                xcv, xck = w_get(xcb, KC, 512)
                bgv, bgk = w_get(bgb, KC, 512)
                for o in range(4):
                    oc = r * 4 + o
                    sample_step()
                    for part in parts:
                        c0, n = part
                        p1, k1 = psum_part(part)
                        p2, k2 = psum_part(part)
                        p3, k3 = psum_part(part)
                        fm_group(p1, k1, cgv, cgk, o * 128, xn, "xn", c0, n)
                        fm_group(p2, k2, xcv, xck, o * 128, xn, "xn", c0, n)
                        fm_group(p3, k3, bgv, bgk, o * 128, xn, "xn", c0, n)
                        t, tk = tmp()
                        S.op("act", lambda h: h.activation(out=t[:, 0:n], in_=p1, func=AF.Copy), reads=k1, writes=[tk])
                        if n == TP:
                            S.op("act", lambda h: h.activation(out=abuf[:, 0:2], in_=akeep[:, l, oc, :], func=AF.Copy),
                                 reads=[("akeep", l)], writes=["abuf"])
                            S.op("dve", lambda h: h.tensor_tensor(out=abuf[:, 2:2 + TP], in0=p2, in1=t[:, 0:n], op=ALU.mult),
                                 reads=k2 + [tk], writes=["abuf"])
                            S.op("act", lambda h: h.activation(out=akeep[:, l, oc, :], in_=abuf[:, TP:TP + 2], func=AF.Copy),
                                 reads=["abuf"], writes=[("akeep", l)])
                            a0, a1, a2 = abuf[:, 0:TP], abuf[:, 1:1 + TP], abuf[:, 2:2 + TP]
                            tz = t[:, 0:n]
                            akey = "abuf"
                        else:
                            S.op("act", lambda h: h.activation(out=abufs[:, :, 0:2], in_=cprev[:, oc, :].rearrange("p (s r) -> p s r", s=NSEQ), func=AF.Copy),
                                 reads=["cprev"], writes=["abufs"])
                            S.op("dve", lambda h: h.tensor_tensor(out=abufs[:, :, 2:6], in0=p2.rearrange("p (s j) -> p s j", s=NSEQ),
                                                                    in1=t[:, 0:n].rearrange("p (s j) -> p s j", s=NSEQ), op=ALU.mult),
                                 reads=k2 + [tk], writes=["abufs"])
                            a0, a1, a2 = abufs[:, :, 0:4], abufs[:, :, 1:5], abufs[:, :, 2:6]
                            tz = t[:, 0:n].rearrange("p (s j) -> p s j", s=NSEQ)
                            akey = "abufs"
                        S.op("dve", lambda h: h.tensor_scalar(out=tz, in0=a2, scalar1=cwcol(l, 2, oc), scalar2=None, op0=ALU.mult),
                             reads=[akey, "cw"], writes=[tk])
                        S.op("dve", lambda h: h.scalar_tensor_tensor(out=tz, in0=a1, scalar=cwcol(l, 1, oc), in1=tz, op0=ALU.mult, op1=ALU.add),
                             reads=[akey, "cw", tk], writes=[tk])
                        S.op("dve", lambda h: h.scalar_tensor_tensor(out=tz, in0=a0, scalar=cwcol(l, 0, oc), in1=tz, op0=ALU.mult, op1=ALU.add),
                             reads=[akey, "cw", tk], writes=[tk])
                        S.op("dve", lambda h: h.tensor_tensor(out=bzT[:, oc, c0:c0 + n], in0=p3, in1=t[:, 0:n], op=ALU.mult),
                             reads=k3 + [tk], writes=[("bzT", c0)])
                        if n == NS:
                            S.op("act", lambda h: h.activation(out=cstg[:, oc, :].rearrange("p (s r) -> p s r", s=NSEQ), in_=abufs[:, :, 4:6], func=AF.Copy),
                                 reads=["abufs"], writes=["cstg"])
                    yield
                w_release(cgb)
                w_release(xcb)
                w_release(bgb)
        conv = conv_gen()

        def conv_step():
            for _ in conv:
                return

        for (cs, c0, ntok) in chunks:
            smp = (ntok == NS)
            b = bank()
            for hh in range(NH):
                S.op("pe", lambda h, hh=hh: h.matmul(PS[0:ntok, b, hh * 128:hh * 128 + ntok], lhsT=kT[:, hh, c0:c0 + ntok],
                                                       rhs=qT[:, hh, c0:c0 + ntok], start=True, stop=True),
                     reads=[("kT", cs), ("qT", cs)], writes=[("ps", b)], inc=(hh == NH - 1))
            sm = sTm[cs % 2]
            smk = ("sTm", cs % 2)
            msk = DMS if smp else DMT
            S.op("dve", lambda h: h.tensor_tensor(out=sm[0:ntok, :, 0:ntok],
                                                    in0=PS[0:ntok, b, :].rearrange("p (h t) -> p h t", h=NH)[:, :, 0:ntok],
                                                    in1=msk[0:ntok], op=ALU.mult),
                 reads=[("ps", b), "DMS" if smp else "DMT"], writes=[smk])
            if not smp:
                conv_step()
            ob = bank_pair() if ti == 0 else 6
            okeys = [("ps", ob), ("ps", ob + 1)]

            def o_ap(hh):
                return PS[0:ntok, ob + hh // 2, (hh % 2) * HV:(hh % 2 + 1) * HV]

            if not smp:
                for hh in range(NH):
                    S.op("pe", lambda h, hh=hh: h.matmul(o_ap(hh), lhsT=sm[0:ntok, hh, 0:ntok], rhs=vtm[0:ntok, cs, hh * HV:(hh + 1) * HV],
                                                           start=True, stop=False),
                         reads=[smk, ("v", cs)], writes=okeys, inc=False)
                    S.op("pe", lambda h, hh=hh: h.matmul(o_ap(hh), lhsT=qxT[:, hh, c0:c0 + ntok], rhs=Sbf[:, l, hh, :],
                                                           start=False, stop=True),
                         reads=[("qxT", cs), ("Sbf", l)], writes=okeys, inc=(hh == NH - 1))
                S.op("dve", lambda h: h.memset(ssq[:, 0:4], 0.0), writes=["ssq"])
                sb_ = bank_pair()
                skeys = [("ps", sb_), ("ps", sb_ + 1)]
                for hh in range(NH):
                    S.op("pe", lambda h, hh=hh: h.matmul(PS[:, sb_ + hh // 2, (hh % 2) * HV:(hh % 2 + 1) * HV],
                                                           lhsT=kz[0:ntok, cs, hh * 128:(hh + 1) * 128], rhs=vtm[0:ntok, cs, hh * HV:(hh + 1) * HV],
                                                           start=True, stop=True),
                         reads=[("kz", cs), ("v", cs)], writes=skeys, inc=(hh == NH - 1))
                for hh in range(NH):
                    S.op("dve", lambda h, hh=hh: h.scalar_tensor_tensor(out=Sst[:, l, hh, :], in0=Sst[:, l, hh, :], scalar=_GC[hh],
                                                                          in1=PS[:, sb_ + hh // 2, (hh % 2) * HV:(hh % 2 + 1) * HV],
                                                                          op0=ALU.mult, op1=ALU.add),
                         reads=skeys + [("S", l)], writes=[("S", l)])
                need_sbf_cast = True
                if last_tile and cs == 3:
                    store(lambda h: h.dma_start(out=rsp[l].rearrange("h d e -> d h e"), in_=Sst[:, l]), reads=[("S", l)])
            else:
                its = [(hh, s) for hh in range(NH) for s in range(NSEQ)]
                PF = 3

                PF = 6

                def emit_load(it):
                    hh_, s_ = its[it]
                    i8_ = it % 8
                    load(lambda h: h.dma_start(out=S32b[:, i8_, :], in_=sret[l, s_, hh_]), writes=[("S32b", i8_)])

                S.transfer(["COS", "SIN", ("qk2", 0), ("qk2", 1)], [("Sout", i) for i in range(8)])
                S.transfer([("tmp", i) for i in range(NTMP)], [("S32b", i) for i in range(8)])
                for it in range(PF):
                    emit_load(it)
                pend_stores = []
                for it, (hh, s) in enumerate(its):
                    if s == 0:
                        S.op("dve", lambda h, hh=hh: h.tensor_tensor(out=Qb, in0=qxT[:, hh, c0:c0 + ntok].unsqueeze(1).to_broadcast([128, NSEQ, NS]),
                                                                       in1=CMASK, op=ALU.mult),
                             reads=[("qxT", cs), "CMASK"], writes=["Qb"])
                        S.op("dve", lambda h, hh=hh: h.tensor_tensor(out=KZh, in0=kz[0:ntok, cs, hh * 128:(hh + 1) * 128].unsqueeze(1).to_broadcast([NS, NSEQ, 128]),
                                                                       in1=KMASK[0:NS, :].unsqueeze(2).to_broadcast([NS, NSEQ, 128]), op=ALU.mult),
                             reads=[("kz", cs), "KMASK"], writes=["KZh"])
                        S.op("pe", lambda h, hh=hh: h.matmul(o_ap(hh), lhsT=sm[0:ntok, hh, 0:ntok], rhs=vtm[0:ntok, cs, hh * HV:(hh + 1) * HV],
                                                               start=True, stop=False),
                             reads=[smk, ("v", cs)], writes=okeys, inc=True)
                    i4 = it % 4
                    s32, s32k = S32b[:, it % 8, :], ("S32b", it % 8)
                    sbf, sbfk = Sbfs[:, i4, :], ("Sbfs", i4)
                    S.op("act", lambda h, s32=s32, sbf=sbf: h.activation(out=sbf, in_=s32, func=AF.Copy), reads=[s32k], writes=[sbfk])
                    if len(pend_stores) >= 3:
                        pend_stores.pop(0)()
                    S.op("pe", lambda h, hh=hh, s=s, sbf=sbf: h.matmul(o_ap(hh), lhsT=Qb[:, s, :], rhs=sbf, start=False, stop=(s == NSEQ - 1)),
                         reads=["Qb", sbfk], writes=okeys, inc=True)
                    hb = i4 % 2
                    up_ap = PS[:, 6 + hb, 0:HV]
                    S.op("pe", lambda h, hh=hh, s=s, up_ap=up_ap: h.matmul(up_ap, lhsT=KZh[:, s, :], rhs=vtm[0:ntok, cs, hh * HV:(hh + 1) * HV],
                                                                          start=True, stop=True),
                         reads=["KZh", ("v", cs)], writes=[("ps", 6 + hb)], inc=True)
                    so, sok = sout(it % 8), ("Sout", it % 8)
                    S.op("dve", lambda h, hh=hh, s32=s32, so=so, up_ap=up_ap: h.scalar_tensor_tensor(out=so, in0=s32, scalar=_G4[hh], in1=up_ap,
                                                                                                      op0=ALU.mult, op1=ALU.add),
                         reads=[("ps", 6 + hb), s32k], writes=[sok])
                    if it + PF < len(its):
                        emit_load(it + PF)
                    pend_stores.append(lambda s=s, hh=hh, so=so, sok=sok:
                                       store(lambda h: h.dma_start(out=rss[l, s, hh], in_=so), reads=[sok], q="act"))
                for ps_ in pend_stores:
                    ps_()
                S.transfer([("S32b", i) for i in range(8)], [("tmp", i) for i in range(NTMP)])
            ytm, ytk = tm4[cs % 2], ("tm4", cs % 2)
            if smp:
                S.op("dve", lambda h: h.memset(ssq[:, 0:4], 0.0), writes=["ssq"])
            for hh in range(NH):
                S.op("act", lambda h, hh=hh: h.activation(out=ytm[0:ntok, hh * HV:(hh + 1) * HV], in_=o_ap(hh), func=AF.Square,
                                                            accum_out=ssq[0:ntok, hh:hh + 1]),
                     reads=okeys, writes=[ytk, "ssq"])
            S.op("act", lambda h: h.activation(out=ssq[0:ntok, 4:8], in_=ssq[0:ntok, 0:4], func=AF.Ln, bias=epsc[0:ntok], scale=1.0 / HV),
                 reads=["ssq", "eps"], writes=["ssq2"])
            S.op("act", lambda h: h.activation(out=ssq[0:ntok, 4:8], in_=ssq[0:ntok, 4:8], func=AF.Exp, scale=-0.5),
                 reads=["ssq2"], writes=["ssq2"])
            for hh in range(NH):
                S.op("dve", lambda h, hh=hh: h.scalar_tensor_tensor(out=ytm[0:ntok, hh * HV:(hh + 1) * HV], in0=o_ap(hh),
                                                                      scalar=ssq[0:ntok, 4 + hh:5 + hh],
                                                                      in1=sgm[0:ntok, cs, hh * HV:(hh + 1) * HV], op0=ALU.mult, op1=ALU.mult),
                     reads=okeys + ["ssq2", ("sgm", cs)], writes=[ytk])
            if not smp:
                S.op("act", lambda h: h.activation(out=Sbf[:, l].rearrange("p a b -> p (a b)"), in_=Sst[:, l].rearrange("p a b -> p (a b)"), func=AF.Copy),
                     reads=[("S", l)], writes=[("Sbf", l)])
                for _ in range({0: 0, 1: 1, 2: 1, 3: 2}[cs]):
                    conv_step()
            tb = bank_pair()
            for c in range(KC):
                bb, off = tb + c // 4, (c % 4) * 128
                S.op("pe", lambda h, c=c, bb=bb, off=off: h.transpose(PS[:, bb, off:off + ntok], ytm[0:ntok, c * 128:(c + 1) * 128],
                                                                        ident[0:ntok, 0:ntok]),
                     reads=[ytk, "ident"], writes=[("ps", tb), ("ps", tb + 1)], inc=(c == KC - 1))
            S.op("act", lambda h: h.activation(out=yT[:, :, c0:c0 + ntok],
                                               in_=PS[:, tb:tb + 2, :].rearrange("p a (c t) -> p (a c) t", c=4)[:, :, 0:ntok], func=AF.Copy),
                 reads=[("ps", tb), ("ps", tb + 1)], writes=[("yT", 0 if c0 < TP else TP)])

        for _ in conv:
            pass
        if ti == 0:
            bp2 = bank_pair()
            for c in range(KC):
                bb, off = bp2 + c // 4, (c % 4) * 128
                S.op("pe", lambda h, c=c, bb=bb, off=off: h.transpose(PS[0:2 * NSEQ, bb, off:off + 128], cstg[:, c, :], ident),
                     reads=["cstg", "ident"], writes=[("ps", bp2), ("ps", bp2 + 1)], inc=(c == KC - 1))
            so = tm4[1]
            S.op("act", lambda h: h.activation(out=so[0:2 * NSEQ, :], in_=PS[0:2 * NSEQ, bp2:bp2 + 2, :].rearrange("p a b -> p (a b)"), func=AF.Copy),
                 reads=[("ps", bp2), ("ps", bp2 + 1)], writes=[("tm4", 1)])
            store(lambda h: h.dma_start(out=css[l, :, :], in_=so[0:2 * NSEQ, :]), reads=[("tm4", 1)])
        if last_tile:
            bp2 = bank_pair()
            for c in range(KC):
                bb, off = bp2 + c // 4, (c % 4) * 128
                S.op("pe", lambda h, c=c, bb=bb, off=off: h.transpose(PS[0:2, bb, off:off + 128], akeep[:, l, c, :], ident),
                     reads=[("akeep", l), "ident"], writes=[("ps", bp2), ("ps", bp2 + 1)], inc=(c == KC - 1))
            so = tm4[1]
            S.op("act", lambda h: h.activation(out=so[0:2, :], in_=PS[0:2, bp2:bp2 + 2, :].rearrange("p a b -> p (a b)"), func=AF.Copy),
                 reads=[("ps", bp2), ("ps", bp2 + 1)], writes=[("tm4", 1)])
            store(lambda h: h.dma_start(out=csp[l, :, :], in_=so[0:2, :]), reads=[("tm4", 1)])

        S.transfer(allk("sq", 0) + allk("sq", TP), [("mg", 0), ("mg", TP)])
        for r in range(4):
            gb_, ob_ = merge_blocks[r]
            grv, gcv, grk = w_get2(gb_, KC, 256)
            gck = grk
            rov, cov, rok = w_get2(ob_, KC, 256)
            cok = rok
            for o in range(2):
                oc = r * 2 + o
                sample_step()
                for part in parts:
                    c0, n = part
                    p1, k1 = psum_part(part)
                    p2, k2 = psum_part(part)
                    p3, k3 = psum_part(part)
                    p4, k4 = psum_part(part)
                    fm_group(p1, k1, grv, grk, o * 128, xn, "xn", c0, n)
                    fm_group(p2, k2, rov, rok, o * 128, yT, "yT", c0, n)
                    fm_group(p3, k3, gcv, gck, o * 128, xn, "xn", c0, n)
                    fm_group(p4, k4, cov, cok, o * 128, bzT, "bzT", c0, n)
                    t1, tk1 = tmp()
                    t2, tk2 = tmp()
                    S.op("act", lambda h: h.activation(out=t1[:, 0:n], in_=p1, func=AF.Sigmoid), reads=k1, writes=[tk1])
                    S.op("dve", lambda h: h.tensor_tensor(out=t1[:, 0:n], in0=p2, in1=t1[:, 0:n], op=ALU.mult), reads=k2 + [tk1], writes=[tk1])
                    S.op("act", lambda h: h.activation(out=t2[:, 0:n], in_=p3, func=AF.Sigmoid), reads=k3, writes=[tk2])
                    S.op("dve", lambda h: h.tensor_tensor(out=t2[:, 0:n], in0=p4, in1=t2[:, 0:n], op=ALU.mult), reads=k4 + [tk2], writes=[tk2])
                    S.op("dve", lambda h: h.tensor_tensor(out=mg[:, oc, c0:c0 + n], in0=t1[:, 0:n], in1=t2[:, 0:n], op=ALU.add),
                         reads=[tk1, tk2], writes=[("mg", c0)])
            for bq in (gb_, ob_):
                w_release(bq)
        if DBG_LEVEL == 17:
            return
        S.transfer([("v", c) for c in range(5)] + [("sgm", c) for c in range(5)], allk("fT", 0) + allk("fT", TP))
        S.transfer([("yT", 0), ("yT", TP)], allk("sq2", 0) + allk("sq2", TP))
        for r in range(2):
            wv, wk = w_get(wo_blocks[r], KC, 512)
            if r == 1:
                preload_ln_table()
            for o in range(4):
                oc = r * 4 + o
                sample_step()
                for part in parts:
                    c0, n = part
                    pap, pk = psum_part(part)
                    fm_group(pap, pk, wv, wk, o * 128, mg, "mg", c0, n)
                    S.op("act", lambda h: h.activation(out=fT[:, oc, c0:c0 + n], in_=pap, func=AF.Copy, scale=gcol(l, 3, oc)),
                         reads=pk + ["gains"], writes=[("fT", c0, oc)])
                    S.op("act", lambda h: h.activation(out=yT[:, oc, c0:c0 + n], in_=pap, func=AF.Square),
                         reads=pk, writes=[("sq2", c0, oc)])
            w_release(wo_blocks[r])
        S.transfer([("mg", 0), ("mg", TP)], allk("sq", 0) + allk("sq", TP))
        boundary(l, parts, yT, "sq2", l, 4)
        S.transfer(allk("sq2", 0) + allk("sq2", TP), [("yT", 0), ("yT", TP)])
        S.transfer([("qT", c) for c in range(5)] + [("qxT", c) for c in range(5)] + [("kT", c) for c in range(5)]
                   + [("kz", c) for c in range(5)] + ["KZh", "Qb"], [("hT", 0), ("hT", TP)])

    def load_x(ti):
        for (cs, c0, ntok) in chunks_of(ti):
            st, stk = tm4[cs % 2], ("tm4", cs % 2)
            if ntok == 128:
                r0 = ti * TP + cs * 128
                load(lambda h: h.dma_start(out=st[0:ntok, :], in_=xp[r0:r0 + ntok, :]), writes=[stk])
            else:
                load(lambda h: h.dma_start(out=st[0:ntok, :], in_=xs[:, :]), writes=[stk])
            tb = bank_pair()
            for c in range(KC):
                bb, off = tb + c // 4, (c % 4) * 128
                S.op("pe", lambda h, c=c, bb=bb, off=off: h.transpose(PS[:, bb, off:off + ntok], st[0:ntok, c * 128:(c + 1) * 128],
                                                                        ident[0:ntok, 0:ntok]),
                     reads=[stk, "ident"], writes=[("ps", tb), ("ps", tb + 1)], inc=(c == KC - 1))
            S.op("act", lambda h: h.activation(out=xT[:, :, c0:c0 + ntok],
                                               in_=PS[:, tb:tb + 2, :].rearrange("p a (c t) -> p (a c) t", c=4)[:, :, 0:ntok], func=AF.Copy),
                 reads=[("ps", tb), ("ps", tb + 1)], writes=allk("xT", 0 if c0 < TP else TP))

    def store_y(ti):
        for (cs, c0, ntok) in chunks_of(ti):
            st, stk = tm4[cs % 2], ("tm4", cs % 2)
            tb = bank_pair()
            for c in range(KC):
                bb, off = tb + c // 4, (c % 4) * 128
                S.op("pe", lambda h, c=c, bb=bb, off=off: h.transpose(PS[0:ntok, bb, off:off + 128], xT[:, c, c0:c0 + ntok], ident),
                     reads=[("xT", 0 if c0 < TP else TP, c), "ident"], writes=[("ps", tb), ("ps", tb + 1)], inc=(c == KC - 1))
            S.op("act", lambda h: h.activation(out=st[0:ntok, :], in_=PS[0:ntok, tb:tb + 2, :].rearrange("p a b -> p (a b)"), func=AF.Copy),
                 reads=[("ps", tb), ("ps", tb + 1)], writes=[stk])
            if ntok == 128:
                r0 = ti * TP + cs * 128
                store(lambda h: h.dma_start(out=yp[r0:r0 + ntok, :], in_=st[0:ntok, :]), reads=[stk])
            else:
                store(lambda h: h.dma_start(out=ys[:, :], in_=st[0:ntok, :]), reads=[stk])

    def tap(name, ti):
        if name in tap_out and ti == 0:
            store(lambda h: h.dma_start(out=tap_out[name], in_=xT), reads=allk("xT", 0) + allk("xT", TP))

    plan = []
    for ti in range(n_tiles):
        for l in range(n_layers):
            ent = {}
            if "ffn1" in stages:
                ent["ffn1"] = ffn_blocks(0, l)
            if "mixer" in stages:
                ent["mixer"] = mixer_blocks(l)
            if "ffn2" in stages:
                ent["ffn2"] = ffn_blocks(1, l)
            plan.append((ti, l, ent))
    w_pump()
    full = ("ffn1" in stages and "mixer" in stages and "ffn2" in stages)
    for (ti, l, ent) in plan:
        if l == 0:
            load_x(ti)
        if full:
            ffn(ti, l, 0, 1, ent["ffn1"], need_norm_in=(l == 0), nxt=(l, 2))
            tap(f"ffn1_{l}", ti)
            mixer(ti, l, ent["mixer"])
            tap(f"mixer_{l}", ti)
            ffn(ti, l, 4, 5, ent["ffn2"], need_norm_in=False, nxt=((l + 1, 0) if l + 1 < n_layers else (None, None)))
            tap(f"ffn2_{l}", ti)
        else:
            if "ffn1" in ent:
                ffn(ti, l, 0, 1, ent["ffn1"], need_norm_in=True, nxt=(None, None))
                tap(f"ffn1_{l}", ti)
        if l == n_layers - 1:
            store_y(ti)
    S.wait_all("sp", st_lanes + ld_lanes + wl + wst)
    stack.close()
    print("instruction counts:", {k: (v["lane"].count if v["lane"] else None) for k, v in S.E.items()}, "weight blocks:", len(blocks))
    return nc


_PROGRAM = None


def _get_program():
    global _PROGRAM
    if _PROGRAM is None:
        _PROGRAM = build_program()
    return _PROGRAM


def make_in_maps(inputs):
    f = lambda a: np.ascontiguousarray(np.asarray(a, dtype=np.float32))
    x_prompt = f(inputs["x_prompt"])
    x_sample = f(inputs["x_sample"])
    state_ret = f(inputs["state_ret"])
    state_conv = f(inputs["state_conv"])
    shared = {
        "norms": f(inputs["norms"]).reshape(DEPTH * 6 * KC, 128),
        "convw": f(inputs["conv_w"]).reshape(DEPTH * 3 * KC, 128),
        "w_ffn1_up": f(inputs["w_ffn1_up"]), "w_ffn2_up": f(inputs["w_ffn2_up"]),
        "w_ffn1_down": f(inputs["w_ffn1_down"]), "w_ffn2_down": f(inputs["w_ffn2_down"]),
        "w_in": f(inputs["w_in"]), "w_ret_out": f(inputs["w_ret_out"]),
        "w_conv_out": f(inputs["w_conv_out"]), "w_o": f(inputs["w_o"]),
    }
    for k, v in _CONSTS.items():
        shared["c_" + k] = v
    in_maps = []
    for c in range(NCORES):
        m = dict(shared)
        m["xp"] = x_prompt[c]
        m["xs"] = np.ascontiguousarray(x_sample[c * NSEQ:(c + 1) * NSEQ].reshape(NS, D))
        m["sret"] = np.ascontiguousarray(state_ret[:, c * NSEQ:(c + 1) * NSEQ])
        m["sconv"] = np.ascontiguousarray(state_conv[:, c * NSEQ:(c + 1) * NSEQ].reshape(DEPTH, NSEQ * 2, D))
        in_maps.append(m)
    return in_maps


def kernel(**inputs):
    nc = _get_program()
    in_maps = make_in_maps(inputs)
    res = run_bass_kernel_spmd(nc, in_maps, core_ids=list(range(NCORES)))
    R = res.results
    y_prompt = np.stack([R[c]["yp"] for c in range(NCORES)], axis=0)
    y_sample = np.concatenate([R[c]["ys"].reshape(NSEQ, 4, D) for c in range(NCORES)], axis=0)
    ret_p = np.stack([R[c]["rsp"] for c in range(NCORES)], axis=1)
    conv_p = np.stack([R[c]["csp"] for c in range(NCORES)], axis=1)
    ret_s = np.concatenate([R[c]["rss"] for c in range(NCORES)], axis=1)
    conv_s = np.concatenate([R[c]["css"].reshape(DEPTH, NSEQ, 2, D) for c in range(NCORES)], axis=1)
    return (y_prompt.astype(np.float32), y_sample.astype(np.float32), ret_p.astype(np.float32),
            conv_p.astype(np.float32), ret_s.astype(np.float32), conv_s.astype(np.float32))
```

```python
import math
from contextlib import ExitStack

import numpy as np
import concourse.bass as bass
import concourse.mybir as mybir
from concourse.bass_utils import run_bass_kernel_spmd

F32 = mybir.dt.float32
BF16 = mybir.dt.bfloat16
AF = mybir.ActivationFunctionType
ALU = mybir.AluOpType

D = 1024
KC = 8
DFF = 2816
NJ = 22
NH = 4
HQ = 128
HV = 256
DEPTH = 2
SEQ = 2048
TP = 512
NT = 4
NS = 64
NSEQ = 16
TM = TP + NS
EPS = 1e-6
NSLOT = 5
N_IN = 8192
NCORES = 8
DBG_LEVEL = 9
DBG_NI = 6
DBG_CHUNKS = None
DBG_SUB = 0


class Lane:
    def __init__(self, name, sem, unit):
        self.name, self.sem, self.unit, self.count = name, sem, unit, 0


class Sched:
    def __init__(self, nc, stack):
        self.nc, self.stack = nc, stack
        self.E = {}
        self.lastw = {}
        self.readers = {}
        self.nlanes = 0

    def add_engine(self, name, handle, lane=True):
        ln = None
        if lane:
            sem = self.stack.enter_context(self.nc.semaphore("s_" + name))
            ln = Lane(name, sem, 1)
        self.E[name] = dict(h=handle, lane=ln, seen={})

    def dma_lane(self, name):
        sem = self.stack.enter_context(self.nc.semaphore("d_" + name))
        return Lane("d_" + name, sem, 16)

    def _need(self, mylane, reads, writes):
        need = {}

        def add(tok, same_ok):
            if tok is None:
                return
            l, v = tok
            if l is mylane and l.name == "pe":
                return
            if need.get(l.name, (None, 0))[1] < v:
                need[l.name] = (l, v)

        for k in reads:
            add(self.lastw.get(k), False)
        for k in writes:
            add(self.lastw.get(k), True)
            for tok in self.readers.get(k, {}).values():
                add(tok, True)
        return need

    def _wait(self, ename, need):
        e = self.E[ename]
        for l, v in need.values():
            if e["seen"].get(l.name, 0) < v:
                e["h"].wait_ge(l.sem, v)
                e["seen"][l.name] = v

    def _record(self, tok, reads, writes):
        l = tok[0]
        for k in reads:
            self.readers.setdefault(k, {})[l.name] = tok
        for k in writes:
            self.lastw[k] = tok
            self.readers[k] = {}

    def op(self, ename, fn, reads=(), writes=(), inc=True):
        e = self.E[ename]
        lane = e["lane"]
        psr = [k for k in reads if isinstance(k, tuple) and k[0] == "ps"]
        if psr:
            reads = [k for k in reads if k not in psr]
            writes = list(writes) + [k for k in psr if k not in writes]
        self._wait(ename, self._need(lane, reads, writes))
        ins = fn(e["h"])
        tok = (lane, lane.count + 1)
        if inc:
            ins.then_inc(lane.sem, 1)
            lane.count += 1
        self._record(tok, reads, writes)
        return tok

    def dma(self, qname, lane, fn, reads=(), writes=(), serialize=True):
        need = self._need(None, reads, writes)
        if serialize and lane.count > 0:
            if need.get(lane.name, (None, 0))[1] < lane.count:
                need[lane.name] = (lane, lane.count)
        self._wait(qname, need)
        ins = fn(self.E[qname]["h"])
        ins.then_inc(lane.sem, 16)
        lane.count += 16
        tok = (lane, lane.count)
        self._record(tok, reads, writes)
        return tok

    def transfer(self, old_keys, new_keys):
        merged = {}
        for k in old_keys:
            toks = list(self.readers.get(k, {}).values())
            if self.lastw.get(k) is not None:
                toks.append(self.lastw[k])
            for l, v in toks:
                if merged.get(l.name, (None, 0))[1] < v:
                    merged[l.name] = (l, v)
        for k in new_keys:
            self.lastw[k] = None
            self.readers[k] = dict(merged)

    def wait_all(self, ename, lanes):
        need = {l.name: (l, l.count) for l in lanes if l.count > 0}
        self._wait(ename, need)


def _host_consts():
    f32 = np.float32
    lg = np.log((1.0 - np.exp(np.linspace(math.log(1.0 / 32), math.log(1.0 / 512), NH, dtype=f32))).astype(f32)).astype(f32)
    half = HQ // 2
    inv_freq = (10000.0 ** (-(np.arange(half, dtype=f32) * f32(2.0) / f32(HQ)))).astype(f32)
    pos = np.concatenate([np.arange(SEQ, dtype=f32), (16384 + (np.arange(NS) % 4)).astype(f32)])
    ang = (pos[:, None] * inv_freq[None, :]).astype(f32)
    cos = np.cos(ang).astype(f32)
    sin = np.sin(ang).astype(f32)
    cos2 = np.concatenate([cos, cos], axis=1)
    sinm = np.concatenate([-sin, sin], axis=1)
    idx = np.arange(128, dtype=f32)
    diff = idx[None, :] - idx[:, None]
    dmT = np.where(diff[None] >= 0, np.exp(np.maximum(diff, 0.0)[None] * lg[:, None, None]), 0.0).astype(f32)
    dmT = np.ascontiguousarray(dmT.transpose(1, 0, 2))
    xi = np.exp((idx[None, :] + 1.0) * lg[:, None]).astype(f32)
    xiP = np.ascontiguousarray(np.broadcast_to(xi[None], (128, NH, 128))).astype(f32)
    zeta = np.exp((127.0 - idx)[:, None] * lg[None, :]).astype(f32) * f32(HQ ** -0.5)
    zetaP = np.ascontiguousarray(np.broadcast_to(zeta[:, :, None], (128, NH, 128))).astype(f32)
    t = np.arange(NS)
    sq, jj = t // 4, t % 4
    same = (sq[:, None] == sq[None, :])
    dd = (jj[None, :] - jj[:, None]).astype(f32)
    dmS = np.where((same & (dd >= 0))[None], np.exp(np.maximum(dd, 0.0)[None] * lg[:, None, None]), 0.0).astype(f32)
    dmS = np.ascontiguousarray(dmS.transpose(1, 0, 2))
    xis = np.exp((jj.astype(f32)[None, :] + 1.0) * lg[:, None]).astype(f32)
    xiS = np.ascontiguousarray(np.broadcast_to(xis[None], (128, NH, NS))).astype(f32)
    zs = np.exp((3.0 - jj.astype(f32))[:, None] * lg[None, :]).astype(f32) * f32(HQ ** -0.5)
    zetaS = np.ascontiguousarray(np.broadcast_to(zs[:, :, None], (NS, NH, 128))).astype(f32)
    kmask = (sq[:, None] == np.arange(NSEQ)[None, :]).astype(f32)
    cmask = np.ascontiguousarray(np.broadcast_to((np.arange(NSEQ)[:, None] == sq[None, :])[None], (128, NSEQ, NS))).astype(f32)
    gC = [float(np.exp(f32(128.0) * lg[h])) for h in range(NH)]
    g4 = [float(np.exp(f32(4.0) * lg[h])) for h in range(NH)]
    ident = np.eye(128, dtype=f32)
    return dict(cos2=cos2, sinm=sinm, dmT=dmT, xiP=xiP, zetaP=zetaP, dmS=dmS, xiS=xiS, zetaS=zetaS,
                kmask=kmask, cmask=cmask, ident=ident), gC, g4


_CONSTS, _GC, _G4 = _host_consts()
_CONST_SHAPES = {k: list(v.shape) for k, v in _CONSTS.items()}


def build_program(n_tiles=NT, n_layers=DEPTH, stages=("ffn1", "mixer", "ffn2"), taps=()):
    nc = bass.Bass("TRN2", target_bir_lowering=False)
    stack = ExitStack()

    def din(name, shape):
        return nc.dram_tensor(name, list(shape), F32, kind="ExternalInput").ap()

    def dout(name, shape):
        return nc.dram_tensor(name, list(shape), F32, kind="ExternalOutput").ap()

    xp = din("xp", [SEQ, D])
    xs = din("xs", [NS, D])
    sret = din("sret", [DEPTH, NSEQ, NH, HQ, HV])
    sconv = din("sconv", [DEPTH, NSEQ * 2, D])
    norms = din("norms", [DEPTH * 6 * KC, 128])
    convw = din("convw", [DEPTH * 3 * KC, 128])
    w_up = [din("w_ffn1_up", [DEPTH, D, 2 * DFF]), din("w_ffn2_up", [DEPTH, D, 2 * DFF])]
    w_dn = [din("w_ffn1_down", [DEPTH, DFF, D]), din("w_ffn2_down", [DEPTH, DFF, D])]
    w_in = din("w_in", [DEPTH, D, N_IN])
    w_ro = din("w_ret_out", [DEPTH, D, D])
    w_co = din("w_conv_out", [DEPTH, D, D])
    w_oo = din("w_o", [DEPTH, D, D])
    cst = {k: din("c_" + k, shp) for k, shp in _CONST_SHAPES.items()}

    yp = dout("yp", [SEQ, D])
    ys = dout("ys", [NS, D])
    rsp = dout("rsp", [DEPTH, NH, HQ, HV])
    csp = dout("csp", [DEPTH, 2, D])
    rss = dout("rss", [DEPTH, NSEQ, NH, HQ, HV])
    css = dout("css", [DEPTH, NSEQ * 2, D])
    tap_out = {name: dout("tap_" + name, [128, KC, TM]) for name in taps}

    NW = 53200
    big = stack.enter_context(nc.sbuf_tensor("big", [128, NW], F32))
    PS = stack.enter_context(nc.psum_tensor("ps", [128, 8, 512], F32))
    S = Sched(nc, stack)
    S.add_engine("pe", nc.tensor)
    S.add_engine("act", nc.scalar)
    S.add_engine("dve", nc.vector)
    S.add_engine("pool", nc.gpsimd)
    S.add_engine("sp", nc.sync, lane=False)

    cur = [0]

    def alloc(nbytes):
        nbytes = (nbytes + 63) // 64 * 64
        off = cur[0]
        cur[0] += nbytes
        assert cur[0] <= NW * 4, f"SBUF overflow {cur[0]}"
        return off

    def view(off, shape, dt, parts=128):
        n = int(np.prod(shape))
        nb = n * (2 if dt == BF16 else 4)
        ap = big[0:parts, off // 4:(off + nb) // 4]
        if dt != F32:
            ap = ap.bitcast(dt)
        if len(shape) == 2:
            ap = ap.rearrange("p (a b) -> p a b", a=shape[0])
        elif len(shape) == 3:
            ap = ap.rearrange("p (a b c) -> p a b c", a=shape[0], b=shape[1])
        return ap

    def newbuf(shape, dt, parts=128):
        n = int(np.prod(shape))
        return view(alloc(n * (2 if dt == BF16 else 4)), shape, dt, parts)

    xT = newbuf([KC, TM], F32)
    xn = newbuf([KC, TM], BF16)
    rstd = newbuf([TM], F32)
    R1 = alloc(NJ * TM * 2)
    hT = view(R1, [NJ, TM], BF16)
    qT = view(R1, [NH, TM], BF16)
    qxT = view(R1 + NH * TM * 2, [NH, TM], BF16)
    kT = view(R1 + 2 * NH * TM * 2, [NH, TM], BF16)
    kz = view(R1 + 3 * NH * TM * 2, [5, 512], BF16)
    r1_tail = R1 + 3 * NH * TM * 2 + 5 * 512 * 2
    KZh = view(r1_tail, [NSEQ, 128], BF16, parts=NS)
    Qb = view(r1_tail + NSEQ * 128 * 2, [NSEQ, NS], BF16)
    assert r1_tail + NSEQ * 128 * 2 + NSEQ * NS * 2 <= R1 + NJ * TM * 2
    SQo = alloc(KC * TM * 2)
    sq = view(SQo, [KC, TM], BF16)
    mg = view(SQo, [KC, TM], BF16)
    R2 = alloc(20480)
    fT = view(R2, [KC, TM], F32)
    vtm = view(R2, [5, 1024], BF16)
    sgm = view(R2 + 10240, [5, 1024], BF16)
    yT = newbuf([KC, TM], BF16)
    bzT = newbuf([KC, TM], BF16)
    Wsl = [newbuf([4096], BF16) for _ in range(NSLOT)]
    NTMP = 4
    tmp_off = alloc(NTMP * TM * 4)
    tmps = [view(tmp_off + i * TM * 4, [TM], F32) for i in range(NTMP)]
    S32b = view(tmp_off, [8, HV], F32)
    tm4 = [newbuf([1024], F32) for _ in range(2)]
    qk_off = alloc(2 * 512 * 4)
    qk2 = [view(qk_off + i * 2048, [512], F32) for i in range(2)]
    abuf = newbuf([2 + TP], F32)
    abufs = newbuf([NSEQ, 6], F32)
    sTm = [newbuf([NH, 128], BF16) for _ in range(2)]
    Sst = newbuf([DEPTH, NH, HV], F32)
    Sbf = newbuf([DEPTH, NH, HV], BF16)
    S32s = newbuf([4, HV], F32)
    Sbfs = newbuf([4, HV], BF16)
    ident = newbuf([128], F32)
    ones = newbuf([128], BF16)
    DMT = newbuf([NH, 128], F32)
    XI = newbuf([NH, 128], F32)
    ZETA = newbuf([NH, 128], F32)
    cs_off = alloc(2 * 5 * 128 * 4)
    COS = view(cs_off, [5, 128], F32)
    SIN = view(cs_off + 5 * 128 * 4, [5, 128], F32)
    SoutA = view(cs_off, [4, HV], F32)
    SoutB = view(qk_off, [4, HV], F32)

    def sout(i):
        return (SoutA if i < 4 else SoutB)[:, i % 4, :]
    DMS = newbuf([NH, NS], F32)
    XIS = newbuf([NH, NS], F32)
    ZETAS = newbuf([NH, 128], F32)
    KMASK = newbuf([NSEQ], F32)
    CMASK = newbuf([NSEQ, NS], F32)
    gains = newbuf([DEPTH * 6 * KC], F32)
    cw = newbuf([DEPTH * 3 * KC], F32)
    akeep = newbuf([DEPTH, KC, 2], F32)
    epsc = newbuf([1], F32)
    lnwarm = newbuf([1], F32)
    ssq = newbuf([8], F32)
    cstg = newbuf([KC, 2 * NSEQ], F32)
    cprev = newbuf([KC, 2 * NSEQ], F32)
    print("SBUF bytes used per partition:", cur[0])

    wl = [S.dma_lane(f"w{i}") for i in range(NSLOT)]
    wst = [S.dma_lane(f"wst{i}") for i in range(NSLOT)]
    ld_lanes = [S.dma_lane(f"ld{i}") for i in range(4)]
    st_lanes = [S.dma_lane(f"st{i}") for i in range(4)]
    ldi = [0]
    sti = [0]

    def load(fn, writes, reads=()):
        ln = ld_lanes[ldi[0] % len(ld_lanes)]
        ldi[0] += 1
        return S.dma("sp", ln, fn, reads=reads, writes=writes)

    def store(fn, reads, q="sp"):
        ln = st_lanes[sti[0] % len(st_lanes)]
        sti[0] += 1
        return S.dma(q, ln, fn, reads=reads, writes=())

    bank_rr = [0]

    def bank():
        b = bank_rr[0] % 6
        bank_rr[0] += 1
        return b

    pair_rr = [0]

    def bank_pair():
        b = (pair_rr[0] % 3) * 2
        pair_rr[0] += 1
        return b

    sstep = [0, 0]

    def sample_step():
        sstep[0] ^= 1
        sstep[1] = 0

    def psum_part(part):
        c0, n = part
        if n == TP:
            b = bank()
            return PS[:, b, :], [("ps", b)]
        sb = 6 + sstep[0]
        a = sstep[1]
        sstep[1] += 1
        assert a < 8
        return PS[:, sb, a * 64:a * 64 + n], [("ps", sb)]

    tmp_rr = [0]

    def tmp():
        i = tmp_rr[0] % NTMP
        tmp_rr[0] += 1
        return tmps[i], ("tmp", i)

    blocks = []

    def wsrc(w, l, r0, nrows, c0, ncols):
        return w[l, r0:r0 + nrows, c0:c0 + ncols].rearrange("(kc p) n -> p kc n", p=128)

    def add_block(srcs):
        blocks.append(srcs)
        return len(blocks) - 1

    class WS:
        issued = 0
        released = 0

    WS.nblk = None
    WS.scr = None
    WS.scr_pending = []

    def blk_info(b):
        nb = WS.nblk
        return b // (nb * n_layers), (b // nb) % n_layers, b % nb, sum(k * n for _, k, n in blocks[b])

    def scr_store(b):
        ti_, l_, idx_, nel = blk_info(b)
        s = b % NSLOT
        S.dma("pool", wst[s], (lambda h: h.dma_start(out=WS.scr[l_ * WS.nblk + idx_][:, 0:nel], in_=Wsl[s][:, 0:nel])),
              reads=[("w", s)], writes=[("scr", l_, idx_)], serialize=False)

    def w_pump():
        if WS.nblk is None:
            WS.nblk = len(blocks) // (n_tiles * n_layers)
            assert WS.nblk * n_tiles * n_layers == len(blocks)
            if n_tiles > 1:
                WS.scr = nc.dram_tensor("wscr", [n_layers * WS.nblk, 128, 4096], BF16).ap()
        while WS.issued < len(blocks) and WS.issued < WS.released + NSLOT:
            b = WS.issued
            s = b % NSLOT
            ti_, l_, idx_, nel = blk_info(b)
            if ti_ == 0 or WS.scr is None:
                off = 0
                for i, (ap, kcn, ncols) in enumerate(blocks[b]):
                    dst = Wsl[s][:, off:off + kcn * ncols].rearrange("p (kc n) -> p kc n", kc=kcn)
                    S.dma("pool", wl[s], (lambda h, dst=dst, ap=ap: h.dma_start(out=dst, in_=ap)),
                          writes=[("w", s)], serialize=False)
                    off += kcn * ncols
                if WS.scr is not None:
                    WS.scr_pending.append(b)
            else:
                S.dma("pool", wl[s], (lambda h: h.dma_start(out=Wsl[s][:, 0:nel], in_=WS.scr[l_ * WS.nblk + idx_][:, 0:nel])),
                      reads=[("scr", l_, idx_)], writes=[("w", s)], serialize=False)
            WS.issued += 1
            while WS.scr_pending and (WS.scr_pending[0] <= b - 3 or ti_ > 0):
                scr_store(WS.scr_pending.pop(0))

    def w_get(b, kcn, ncols):
        assert b < WS.issued, "weight block not issued (too many live slots)"
        s = b % NSLOT
        return Wsl[s][:, 0:kcn * ncols].rearrange("p (kc n) -> p kc n", kc=kcn), ("w", s)

    def w_get2(b, kcn, ncols):
        assert b < WS.issued, "weight block not issued (too many live slots)"
        s = b % NSLOT
        n = kcn * ncols
        return (Wsl[s][:, 0:n].rearrange("p (kc n) -> p kc n", kc=kcn),
                Wsl[s][:, n:2 * n].rearrange("p (kc n) -> p kc n", kc=kcn), ("w", s))

    def w_release(b):
        assert b == WS.released
        WS.released += 1
        w_pump()

    def cload(dst, src, key, parts=128):
        load(lambda h: h.dma_start(out=dst, in_=src), writes=[key])

    cload(ident, cst["ident"], "ident")
    cload(DMT, cst["dmT"], "DMT")
    cload(XI, cst["xiP"], "XI")
    cload(ZETA, cst["zetaP"], "ZETA")
    cload(DMS[0:NS], cst["dmS"], "DMS")
    cload(XIS, cst["xiS"], "XIS")
    cload(ZETAS[0:NS], cst["zetaS"], "ZETAS")
    cload(KMASK[0:NS], cst["kmask"], "KMASK")
    cload(CMASK, cst["cmask"], "CMASK")
    S.op("dve", lambda h: h.memset(ones, 1.0), writes=["ones"])
    S.op("dve", lambda h: h.memset(epsc, EPS), writes=["eps"])
    S.op("dve", lambda h: h.memset(Sst.rearrange("p a b c -> p (a b c)"), 0.0), writes=[("S", 0), ("S", 1)])
    S.op("dve", lambda h: h.memset(Sbf.rearrange("p a b c -> p (a b c)"), 0.0), writes=[("Sbf", 0), ("Sbf", 1)])
    S.op("dve", lambda h: h.memset(akeep.rearrange("p a b c -> p (a b c)"), 0.0), writes=[("akeep", 0), ("akeep", 1)])

    def load_small_T(src, nrows, dst, key):
        st = tm4[0]
        load(lambda h: h.dma_start(out=st[0:nrows, 0:128], in_=src), writes=[("tm4", 0)])
        b = bank()
        S.op("pe", lambda h: h.transpose(PS[:, b, 0:nrows], st[0:nrows, 0:128], ident[0:nrows, 0:nrows]),
             reads=[("tm4", 0), "ident"], writes=[("ps", b)])
        S.op("act", lambda h: h.activation(out=dst, in_=PS[:, b, 0:nrows], func=AF.Copy),
             reads=[("ps", b)], writes=[key])

    load_small_T(norms, DEPTH * 6 * KC, gains, "gains")
    load_small_T(convw, DEPTH * 3 * KC, cw, "cw")
    for l in range(DEPTH):
        for ni in (1, 5):
            o = (l * 6 + ni) * KC
            S.op("act", lambda h, o=o: h.mul(gains[:, o:o + KC], gains[:, o:o + KC], 0.5),
                 reads=["gains"], writes=["gains"])

    def gcol(l, ni, c):
        o = (l * 6 + ni) * KC + c
        return gains[:, o:o + 1]

    def cwcol(l, i, c):
        o = (l * 3 + i) * KC + c
        return cw[:, o:o + 1]

    def parts_of(ti):
        return [(0, TP)] + ([(TP, NS)] if ti == 0 else [])

    def chunks_of(ti):
        if DBG_CHUNKS is not None:
            return [x for x in ([(c, c * 128, 128) for c in range(4)] + [(4, TP, NS)]) if x[0] in DBG_CHUNKS]
        return [(c, c * 128, 128) for c in range(4)] + ([(4, TP, NS)] if ti == 0 else [])

    def allk(name, c0):
        return [(name, c0, c) for c in range(KC)]

    def rstd_from_psum(pap, pk, c0, n, inv_n):
        S.op("act", lambda h: h.activation(out=rstd[:, c0:c0 + n], in_=pap, func=AF.Ln, bias=epsc, scale=inv_n),
             reads=pk + ["eps"], writes=[("rstd", c0)])
        S.op("act", lambda h: h.activation(out=rstd[:, c0:c0 + n], in_=rstd[:, c0:c0 + n], func=AF.Exp, scale=-0.5),
             reads=[("rstd", c0)], writes=[("rstd", c0)])

    def stat_mm(pap, pk, sqb, sqname, c0, n, kc):
        S.op("pe", lambda h: h.matmul(pap, lhsT=ones, rhs=sqb[:, kc, c0:c0 + n], start=(kc == 0), stop=(kc == KC - 1)),
             reads=["ones", (sqname, c0, kc)], writes=pk, inc=(kc == KC - 1))

    def norm_in(l, ni, parts):
        sample_step()
        for part in parts:
            c0, n = part
            pap, pk = psum_part(part)
            for c in range(KC):
                S.op("act", lambda h, c=c: h.activation(out=sq[:, c, c0:c0 + n], in_=xT[:, c, c0:c0 + n], func=AF.Square),
                     reads=[("xT", c0, c)], writes=[("sq", c0, c)])
                stat_mm(pap, pk, sq, "sq", c0, n, c)
            rstd_from_psum(pap, pk, c0, n, 1.0 / D)
            for c in range(KC):
                S.op("dve", lambda h, c=c: h.scalar_tensor_tensor(out=xn[:, c, c0:c0 + n], in0=xT[:, c, c0:c0 + n],
                                                                    scalar=gcol(l, ni, c), in1=rstd[:, c0:c0 + n],
                                                                    op0=ALU.mult, op1=ALU.mult),
                     reads=[("xT", c0, c), ("rstd", c0), "gains"], writes=[("xn", c0, c)])

    def preload_ln_table():
        S.op("act", lambda h: h.activation(out=lnwarm, in_=epsc, func=AF.Ln), reads=["eps"], writes=["lnwarm"])

    def boundary(l, parts, sqb, sqname, l_next, ni_next):
        sample_step()
        for part in parts:
            c0, n = part
            pap, pk = psum_part(part)
            for kc in range(KC):
                stat_mm(pap, pk, sqb, sqname, c0, n, kc)
            rstd_from_psum(pap, pk, c0, n, 1.0 / D)
            if ni_next is not None:
                pap2, pk2 = psum_part(part)
            for c in range(KC):
                S.op("dve", lambda h, c=c: h.tensor_tensor(out=fT[:, c, c0:c0 + n], in0=fT[:, c, c0:c0 + n], in1=rstd[:, c0:c0 + n], op=ALU.mult),
                     reads=[("fT", c0, c), ("rstd", c0)], writes=[("fT", c0, c)])
                S.op("dve", lambda h, c=c: h.tensor_tensor(out=xT[:, c, c0:c0 + n], in0=xT[:, c, c0:c0 + n], in1=fT[:, c, c0:c0 + n], op=ALU.add),
                     reads=[("fT", c0, c), ("xT", c0, c)], writes=[("xT", c0, c)])
                if ni_next is not None:
                    S.op("act", lambda h, c=c: h.activation(out=sq[:, c, c0:c0 + n], in_=xT[:, c, c0:c0 + n], func=AF.Square),
                         reads=[("xT", c0, c)], writes=[("sq", c0, c)])
                    stat_mm(pap2, pk2, sq, "sq", c0, n, c)
                    S.op("act", lambda h, c=c: h.activation(out=fT[:, c, c0:c0 + n], in_=xT[:, c, c0:c0 + n], func=AF.Copy,
                                                            scale=gcol(l_next, ni_next, c)),
                         reads=[("xT", c0, c), "gains"], writes=[("fT", c0, c)])
            if ni_next is not None:
                rstd_from_psum(pap2, pk2, c0, n, 1.0 / D)
                for c in range(KC):
                    S.op("dve", lambda h, c=c: h.tensor_tensor(out=xn[:, c, c0:c0 + n], in0=fT[:, c, c0:c0 + n], in1=rstd[:, c0:c0 + n], op=ALU.mult),
                         reads=[("fT", c0, c), ("rstd", c0)], writes=[("xn", c0, c)])

    def fm_group(pap, pk, wv, wkey, ocol, rhs_buf, rhs_key, c0, n, nk=KC, first=True, last=True, kbase=0, force_inc=False):
        for kc in range(nk):
            S.op("pe", lambda h, kc=kc: h.matmul(pap, lhsT=wv[:, kc, ocol:ocol + 128], rhs=rhs_buf[:, kbase + kc, c0:c0 + n],
                                                    start=(first and kc == 0), stop=(last and kc == nk - 1)),
                 reads=[wkey, ((rhs_key, c0, kbase + kc) if rhs_key == "xn" else (rhs_key, c0))], writes=pk,
                 inc=((last or force_inc) and kc == nk - 1))

    def ffn_blocks(which, l):
        ids = []
        for i in range(6):
            ncols = 512 if i < 5 else 256
            g = add_block([(wsrc(w_up[which], l, 0, D, i * 512, ncols), KC, ncols)])
            u = add_block([(wsrc(w_up[which], l, 0, D, DFF + i * 512, ncols), KC, ncols)])
            ids.append((g, u, ncols))
        dn = []
        for op_ in range(4):
            for kh in range(2):
                dn.append(add_block([(wsrc(w_dn[which], l, kh * 1408, 1408, op_ * 256, 256), 11, 256)]))
        return ids, dn

    def ffn(ti, l, ni_in, ni_out, blk, need_norm_in, nxt):
        parts = parts_of(ti)
        ids, dn = blk
        if need_norm_in:
            norm_in(l, ni_in, parts)
        if DBG_LEVEL == 0:
            return
        for i, (gb, ub, ncols) in enumerate(ids):
            gv, gk = w_get(gb, KC, ncols)
            uv, uk = w_get(ub, KC, ncols)
            for jj in range(ncols // 128):
                j = i * 4 + jj
                sample_step()
                for part in parts:
                    c0, n = part
                    gp, gpk = psum_part(part)
                    up, upk = psum_part(part)
                    fm_group(gp, gpk, gv, gk, jj * 128, xn, "xn", c0, n)
                    fm_group(up, upk, uv, uk, jj * 128, xn, "xn", c0, n)
                    t, tk = tmp()
                    S.op("act", lambda h: h.activation(out=t[:, 0:n], in_=gp, func=AF.Silu), reads=gpk, writes=[tk])
                    S.op("dve", lambda h: h.tensor_tensor(out=hT[:, j, c0:c0 + n], in0=up, in1=t[:, 0:n], op=ALU.mult),
                         reads=upk + [tk], writes=[("hT", c0)])
            w_release(gb)
            w_release(ub)
            if DBG_LEVEL == 1 and i == DBG_NI - 1:
                return
        if DBG_LEVEL == 1:
            return
        for op_ in range(4):
            pa = {}
            sample_step()
            for kh in range(2):
                b = dn[op_ * 2 + kh]
                wv, wk = w_get(b, 11, 256)
                if op_ == 3 and kh == 1:
                    preload_ln_table()
                for o in range(2):
                    oc = op_ * 2 + o
                    for part in parts:
                        c0, n = part
                        if kh == 0:
                            pa[(o, part)] = psum_part(part) if n == TP else (PS[:, 6 + o, 0:n], [("ps", 6 + o)])
                        pap, pk = pa[(o, part)]
                        fm_group(pap, pk, wv, wk, o * 128, hT, "hT", c0, n, nk=11, first=(kh == 0), last=(kh == 1),
                                 kbase=kh * 11, force_inc=True)
                        if kh == 1:
                            S.op("act", lambda h: h.activation(out=fT[:, oc, c0:c0 + n], in_=pap, func=AF.Copy, scale=gcol(l, ni_out, oc)),
                                 reads=pk + ["gains"], writes=[("fT", c0, oc)])
                            S.op("act", lambda h: h.activation(out=sq[:, oc, c0:c0 + n], in_=pap, func=AF.Square),
                                 reads=pk, writes=[("sq", c0, oc)])
                w_release(b)
        if DBG_LEVEL == 2:
            return
        boundary(l, parts, sq, "sq", nxt[0], nxt[1])

    def mixer_blocks(l):
        tm_blocks = [add_block([(wsrc(w_in, l, 0, D, c, 512), KC, 512)]) for c in (0, 1024, 1536, 512, 2048, 2560)]
        conv_blocks = []
        for r in range(2):
            conv_blocks.append(tuple(add_block([(wsrc(w_in, l, 0, D, base + r * 512, 512), KC, 512)])
                                     for base in (4096, 5120, 3072)))
        merge_blocks = []
        for r in range(4):
            merge_blocks.append((add_block([(wsrc(w_in, l, 0, D, 6144 + r * 256, 256), KC, 256),
                                            (wsrc(w_in, l, 0, D, 7168 + r * 256, 256), KC, 256)]),
                                 add_block([(wsrc(w_ro, l, 0, D, r * 256, 256), KC, 256),
                                            (wsrc(w_co, l, 0, D, r * 256, 256), KC, 256)])))
        wo_blocks = [add_block([(wsrc(w_oo, l, 0, D, r * 512, 512), KC, 512)]) for r in range(2)]
        return tm_blocks, conv_blocks, merge_blocks, wo_blocks

    def tm_group(b, wv, wk, c0, ntok):
        for kc in range(KC):
            S.op("pe", lambda h, kc=kc: h.matmul(PS[0:ntok, b, :], lhsT=xn[:, kc, c0:c0 + ntok], rhs=wv[:, kc, :],
                                                    start=(kc == 0), stop=(kc == KC - 1)),
                 reads=[wk, ("xn", 0 if c0 < TP else TP, kc)], writes=[("ps", b)], inc=(kc == KC - 1))

    def rotary(b, ntok, cs, dst, dkey):
        src = PS[0:ntok, b, :].rearrange("p (h d) -> p h d", h=NH)
        d3 = dst[0:ntok, :].rearrange("p (h d) -> p h d", h=NH)
        t, tk = tmp()
        t3 = t[0:ntok, 0:512].rearrange("p (h d) -> p h d", h=NH)
        cosb = COS[0:ntok, cs, :].unsqueeze(1).to_broadcast([ntok, NH, 128])
        S.op("dve", lambda h: h.tensor_tensor(out=d3, in0=src, in1=cosb, op=ALU.mult),
             reads=[("ps", b), "COS"], writes=[dkey])
        sl = SIN[0:ntok, cs, 0:64].unsqueeze(1).to_broadcast([ntok, NH, 64])
        sh = SIN[0:ntok, cs, 64:128].unsqueeze(1).to_broadcast([ntok, NH, 64])
        S.op("dve", lambda h: h.tensor_tensor(out=t3[:, :, 0:64], in0=src[:, :, 64:128], in1=sl, op=ALU.mult),
             reads=[("ps", b), "SIN"], writes=[tk])
        S.op("dve", lambda h: h.tensor_tensor(out=t3[:, :, 64:128], in0=src[:, :, 0:64], in1=sh, op=ALU.mult),
             reads=[("ps", b), "SIN"], writes=[tk])
        S.op("dve", lambda h: h.tensor_tensor(out=dst[0:ntok, :], in0=dst[0:ntok, :], in1=t[0:ntok, 0:512], op=ALU.add),
             reads=[dkey, tk], writes=[dkey])

    def mixer(ti, l, blk):
        parts = parts_of(ti)
        chunks = chunks_of(ti)
        tm_blocks, conv_blocks, merge_blocks, wo_blocks = blk
        last_tile = (ti == n_tiles - 1)
        S.transfer([("hT", 0), ("hT", TP)], [("qT", c) for c in range(5)] + [("qxT", c) for c in range(5)]
                   + [("kT", c) for c in range(5)] + [("kz", c) for c in range(5)] + ["KZh", "Qb"])
        S.transfer(allk("fT", 0) + allk("fT", TP), [("v", c) for c in range(5)] + [("sgm", c) for c in range(5)])
        S.transfer([("Sout", i) for i in range(8)], ["COS", "SIN", ("qk2", 0), ("qk2", 1)])
        load(lambda h: h.dma_start(out=COS[:, 0:4, :], in_=cst["cos2"][ti * TP:(ti + 1) * TP, :].rearrange("(c p) f -> p c f", p=128)),
             writes=["COS"])
        load(lambda h: h.dma_start(out=SIN[:, 0:4, :], in_=cst["sinm"][ti * TP:(ti + 1) * TP, :].rearrange("(c p) f -> p c f", p=128)),
             writes=["SIN"])
        if ti == 0:
            load(lambda h: h.dma_start(out=COS[0:NS, 4, :], in_=cst["cos2"][SEQ:SEQ + NS, :]), writes=["COS"])
            load(lambda h: h.dma_start(out=SIN[0:NS, 4, :], in_=cst["sinm"][SEQ:SEQ + NS, :]), writes=["SIN"])

        if DBG_LEVEL == 10:
            return
        def qk_phase(blk_id, is_q, fill_ids, fill_dst, fill_name, fill_func):
            wv, wk = w_get(blk_id, KC, 512)
            fills = [w_get(fb, KC, 512) for fb in fill_ids]
            pend = None

            def finish(p):
                cs, c0, ntok, rb, rbk = p
                b2 = bank()
                for hh in range(NH):
                    S.op("pe", lambda h, hh=hh: h.transpose(PS[:, b2, hh * 128:hh * 128 + ntok], rb[0:ntok, hh * 128:(hh + 1) * 128],
                                                              ident[0:ntok, 0:ntok]),
                         reads=[rbk, "ident"], writes=[("ps", b2)], inc=(hh == NH - 1))
                src_ = PS[:, b2, :].rearrange("p (h t) -> p h t", h=NH)[:, :, 0:ntok]
                if is_q:
                    S.op("act", lambda h: h.activation(out=qT[:, :, c0:c0 + ntok], in_=src_, func=AF.Copy),
                         reads=[("ps", b2)], writes=[("qT", cs)])
                    xi_c, xkey = (XI, "XI") if ntok == 128 else (XIS, "XIS")
                    S.op("dve", lambda h: h.tensor_tensor(out=qxT[:, :, c0:c0 + ntok], in0=src_, in1=xi_c, op=ALU.mult),
                         reads=[("ps", b2), xkey], writes=[("qxT", cs)])
                else:
                    S.op("act", lambda h: h.activation(out=kT[:, :, c0:c0 + ntok], in_=src_, func=AF.Copy, scale=float(HQ ** -0.5)),
                         reads=[("ps", b2)], writes=[("kT", cs)])
                    z_c, zkey = (ZETA, "ZETA") if ntok == 128 else (ZETAS, "ZETAS")
                    S.op("dve", lambda h: h.tensor_tensor(out=kz[0:ntok, cs, :], in0=rb[0:ntok, :],
                                                            in1=z_c[0:ntok].rearrange("p h d -> p (h d)"), op=ALU.mult),
                         reads=[rbk, zkey], writes=[("kz", cs)])

            for i, (cs, c0, ntok) in enumerate(chunks):
                b = bank()
                tm_group(b, wv, wk, c0, ntok)
                for half, (fv, fk) in enumerate(fills):
                    fb_ = bank()
                    tm_group(fb_, fv, fk, c0, ntok)
                    S.op("act", lambda h, half=half, fb_=fb_: h.activation(out=fill_dst[0:ntok, cs, half * 512:(half + 1) * 512],
                                                                          in_=PS[0:ntok, fb_, :], func=fill_func),
                         reads=[("ps", fb_)], writes=[(fill_name, cs)])
                if pend is not None:
                    finish(pend)
                rb, rbk = qk2[i % 2], ("qk2", i % 2)
                rotary(b, ntok, cs, rb, rbk)
                pend = (cs, c0, ntok, rb, rbk)
            finish(pend)
            w_release(blk_id)
            for fb in fill_ids:
                w_release(fb)

        qk_phase(tm_blocks[0], True, tm_blocks[1:3], vtm, "v", AF.Copy)
        qk_phase(tm_blocks[3], False, tm_blocks[4:6], sgm, "sgm", AF.Silu)
        if DBG_LEVEL == 13:
            return
        if ti == 0:
            st = tm4[0]
            load(lambda h: h.dma_start(out=st[0:2 * NSEQ, :], in_=sconv[l, :, :]), writes=[("tm4", 0)])
            bp = bank_pair()
            for c in range(KC):
                bb, off = bp + c // 4, (c % 4) * 128
                S.op("pe", lambda h, c=c, bb=bb, off=off: h.transpose(PS[:, bb, off:off + 2 * NSEQ], st[0:2 * NSEQ, c * 128:(c + 1) * 128],
                                                                        ident[0:2 * NSEQ, 0:2 * NSEQ]),
                     reads=[("tm4", 0), "ident"], writes=[("ps", bp), ("ps", bp + 1)], inc=(c == KC - 1))
            S.op("act", lambda h: h.activation(out=cprev, in_=PS[:, bp:bp + 2, :].rearrange("p a (c t) -> p (a c) t", c=4)[:, :, 0:2 * NSEQ],
                                               func=AF.Copy),
                 reads=[("ps", bp), ("ps", bp + 1)], writes=["cprev"])

        def conv_gen():
            for r in range(2):
                cgb, xcb, bgb = conv_blocks[r]
                cgv, cgk = w_get(cgb, KC, 512)
                xcv, xck = w_get(xcb, KC, 512)
                bgv, bgk = w_get(bgb, KC, 512)
                for o in range(4):
                    oc = r * 4 + o
                    sample_step()
                    for part in parts:
                        c0, n = part
                        p1, k1 = psum_part(part)
                        p2, k2 = psum_part(part)
                        p3, k3 = psum_part(part)
                        fm_group(p1, k1, cgv, cgk, o * 128, xn, "xn", c0, n)
                        fm_group(p2, k2, xcv, xck, o * 128, xn, "xn", c0, n)
                        fm_group(p3, k3, bgv, bgk, o * 128, xn, "xn", c0, n)
                        t, tk = tmp()
                        S.op("act", lambda h: h.activation(out=t[:, 0:n], in_=p1, func=AF.Copy), reads=k1, writes=[tk])
                        if n == TP:
                            S.op("act", lambda h: h.activation(out=abuf[:, 0:2], in_=akeep[:, l, oc, :], func=AF.Copy),
                                 reads=[("akeep", l)], writes=["abuf"])
                            S.op("dve", lambda h: h.tensor_tensor(out=abuf[:, 2:2 + TP], in0=p2, in1=t[:, 0:n], op=ALU.mult),
                                 reads=k2 + [tk], writes=["abuf"])
                            S.op("act", lambda h: h.activation(out=akeep[:, l, oc, :], in_=abuf[:, TP:TP + 2], func=AF.Copy),
                                 reads=["abuf"], writes=[("akeep", l)])
                            a0, a1, a2 = abuf[:, 0:TP], abuf[:, 1:1 + TP], abuf[:, 2:2 + TP]
                            tz = t[:, 0:n]
                            akey = "abuf"
                        else:
                            S.op("act", lambda h: h.activation(out=abufs[:, :, 0:2], in_=cprev[:, oc, :].rearrange("p (s r) -> p s r", s=NSEQ), func=AF.Copy),
                                 reads=["cprev"], writes=["abufs"])
                            S.op("dve", lambda h: h.tensor_tensor(out=abufs[:, :, 2:6], in0=p2.rearrange("p (s j) -> p s j", s=NSEQ),
                                                                    in1=t[:, 0:n].rearrange("p (s j) -> p s j", s=NSEQ), op=ALU.mult),
                                 reads=k2 + [tk], writes=["abufs"])
                            a0, a1, a2 = abufs[:, :, 0:4], abufs[:, :, 1:5], abufs[:, :, 2:6]
                            tz = t[:, 0:n].rearrange("p (s j) -> p s j", s=NSEQ)
                            akey = "abufs"
                        S.op("dve", lambda h: h.tensor_scalar(out=tz, in0=a2, scalar1=cwcol(l, 2, oc), scalar2=None, op0=ALU.mult),
                             reads=[akey, "cw"], writes=[tk])
                        S.op("dve", lambda h: h.scalar_tensor_tensor(out=tz, in0=a1, scalar=cwcol(l, 1, oc), in1=tz, op0=ALU.mult, op1=ALU.add),
                             reads=[akey, "cw", tk], writes=[tk])
                        S.op("dve", lambda h: h.scalar_tensor_tensor(out=tz, in0=a0, scalar=cwcol(l, 0, oc), in1=tz, op0=ALU.mult, op1=ALU.add),
                             reads=[akey, "cw", tk], writes=[tk])
                        S.op("dve", lambda h: h.tensor_tensor(out=bzT[:, oc, c0:c0 + n], in0=p3, in1=t[:, 0:n], op=ALU.mult),
                             reads=k3 + [tk], writes=[("bzT", c0)])
                        if n == NS:
                            S.op("act", lambda h: h.activation(out=cstg[:, oc, :].rearrange("p (s r) -> p s r", s=NSEQ), in_=abufs[:, :, 4:6], func=AF.Copy),
                                 reads=["abufs"], writes=["cstg"])
                    yield
                w_release(cgb)
                w_release(xcb)
                w_release(bgb)
        conv = conv_gen()

        def conv_step():
            for _ in conv:
                return

        for (cs, c0, ntok) in chunks:
            smp = (ntok == NS)
            b = bank()
            for hh in range(NH):
                S.op("pe", lambda h, hh=hh: h.matmul(PS[0:ntok, b, hh * 128:hh * 128 + ntok], lhsT=kT[:, hh, c0:c0 + ntok],
                                                       rhs=qT[:, hh, c0:c0 + ntok], start=True, stop=True),
                     reads=[("kT", cs), ("qT", cs)], writes=[("ps", b)], inc=(hh == NH - 1))
            sm = sTm[cs % 2]
            smk = ("sTm", cs % 2)
            msk = DMS if smp else DMT
            S.op("dve", lambda h: h.tensor_tensor(out=sm[0:ntok, :, 0:ntok],
                                                    in0=PS[0:ntok, b, :].rearrange("p (h t) -> p h t", h=NH)[:, :, 0:ntok],
                                                    in1=msk[0:ntok], op=ALU.mult),
                 reads=[("ps", b), "DMS" if smp else "DMT"], writes=[smk])
            ob = bank_pair() if ti == 0 else 6
            okeys = [("ps", ob), ("ps", ob + 1)]

            def o_ap(hh):
                return PS[0:ntok, ob + hh // 2, (hh % 2) * HV:(hh % 2 + 1) * HV]

            if not smp:
                for hh in range(NH):
                    S.op("pe", lambda h, hh=hh: h.matmul(o_ap(hh), lhsT=sm[0:ntok, hh, 0:ntok], rhs=vtm[0:ntok, cs, hh * HV:(hh + 1) * HV],
                                                           start=True, stop=False),
                         reads=[smk, ("v", cs)], writes=okeys, inc=False)
                    S.op("pe", lambda h, hh=hh: h.matmul(o_ap(hh), lhsT=qxT[:, hh, c0:c0 + ntok], rhs=Sbf[:, l, hh, :],
                                                           start=False, stop=True),
                         reads=[("qxT", cs), ("Sbf", l)], writes=okeys, inc=(hh == NH - 1))
                S.op("dve", lambda h: h.memset(ssq[:, 0:4], 0.0), writes=["ssq"])
                sb_ = bank_pair()
                skeys = [("ps", sb_), ("ps", sb_ + 1)]
                for hh in range(NH):
                    S.op("pe", lambda h, hh=hh: h.matmul(PS[:, sb_ + hh // 2, (hh % 2) * HV:(hh % 2 + 1) * HV],
                                                           lhsT=kz[0:ntok, cs, hh * 128:(hh + 1) * 128], rhs=vtm[0:ntok, cs, hh * HV:(hh + 1) * HV],
                                                           start=True, stop=True),
                         reads=[("kz", cs), ("v", cs)], writes=skeys, inc=(hh == NH - 1))
                for hh in range(NH):
                    S.op("dve", lambda h, hh=hh: h.scalar_tensor_tensor(out=Sst[:, l, hh, :], in0=Sst[:, l, hh, :], scalar=_GC[hh],
                                                                          in1=PS[:, sb_ + hh // 2, (hh % 2) * HV:(hh % 2 + 1) * HV],
                                                                          op0=ALU.mult, op1=ALU.add),
                         reads=skeys + [("S", l)], writes=[("S", l)])
                need_sbf_cast = True
                if last_tile and cs == 3:
                    store(lambda h: h.dma_start(out=rsp[l].rearrange("h d e -> d h e"), in_=Sst[:, l]), reads=[("S", l)])
            else:
                its = [(hh, s) for hh in range(NH) for s in range(NSEQ)]
                PF = 3

                PF = 6

                def emit_load(it):
                    hh_, s_ = its[it]
                    i8_ = it % 8
                    load(lambda h: h.dma_start(out=S32b[:, i8_, :], in_=sret[l, s_, hh_]), writes=[("S32b", i8_)])

                S.transfer(["COS", "SIN", ("qk2", 0), ("qk2", 1)], [("Sout", i) for i in range(8)])
                S.transfer([("tmp", i) for i in range(NTMP)], [("S32b", i) for i in range(8)])
                for it in range(PF):
                    emit_load(it)
                pend_stores = []
                for it, (hh, s) in enumerate(its):
                    if s == 0:
                        S.op("dve", lambda h, hh=hh: h.tensor_tensor(out=Qb, in0=qxT[:, hh, c0:c0 + ntok].unsqueeze(1).to_broadcast([128, NSEQ, NS]),
                                                                       in1=CMASK, op=ALU.mult),
                             reads=[("qxT", cs), "CMASK"], writes=["Qb"])
                        S.op("dve", lambda h, hh=hh: h.tensor_tensor(out=KZh, in0=kz[0:ntok, cs, hh * 128:(hh + 1) * 128].unsqueeze(1).to_broadcast([NS, NSEQ, 128]),
                                                                       in1=KMASK[0:NS, :].unsqueeze(2).to_broadcast([NS, NSEQ, 128]), op=ALU.mult),
                             reads=[("kz", cs), "KMASK"], writes=["KZh"])
                        S.op("pe", lambda h, hh=hh: h.matmul(o_ap(hh), lhsT=sm[0:ntok, hh, 0:ntok], rhs=vtm[0:ntok, cs, hh * HV:(hh + 1) * HV],
                                                               start=True, stop=False),
                             reads=[smk, ("v", cs)], writes=okeys, inc=True)
                    i4 = it % 4
                    s32, s32k = S32b[:, it % 8, :], ("S32b", it % 8)
                    sbf, sbfk = Sbfs[:, i4, :], ("Sbfs", i4)
                    S.op("act", lambda h, s32=s32, sbf=sbf: h.activation(out=sbf, in_=s32, func=AF.Copy), reads=[s32k], writes=[sbfk])
                    if len(pend_stores) >= 3:
                        pend_stores.pop(0)()
                    S.op("pe", lambda h, hh=hh, s=s, sbf=sbf: h.matmul(o_ap(hh), lhsT=Qb[:, s, :], rhs=sbf, start=False, stop=(s == NSEQ - 1)),
                         reads=["Qb", sbfk], writes=okeys, inc=True)
                    hb = i4 % 2
                    up_ap = PS[:, 6 + hb, 0:HV]
                    S.op("pe", lambda h, hh=hh, s=s, up_ap=up_ap: h.matmul(up_ap, lhsT=KZh[:, s, :], rhs=vtm[0:ntok, cs, hh * HV:(hh + 1) * HV],
                                                                          start=True, stop=True),
                         reads=["KZh", ("v", cs)], writes=[("ps", 6 + hb)], inc=True)
                    so, sok = sout(it % 8), ("Sout", it % 8)
                    S.op("dve", lambda h, hh=hh, s32=s32, so=so, up_ap=up_ap: h.scalar_tensor_tensor(out=so, in0=s32, scalar=_G4[hh], in1=up_ap,
                                                                                                      op0=ALU.mult, op1=ALU.add),
                         reads=[("ps", 6 + hb), s32k], writes=[sok])
                    if it + PF < len(its):
                        emit_load(it + PF)
                    pend_stores.append(lambda s=s, hh=hh, so=so, sok=sok:
                                       store(lambda h: h.dma_start(out=rss[l, s, hh], in_=so), reads=[sok], q="act"))
                for ps_ in pend_stores:
                    ps_()
                S.transfer([("S32b", i) for i in range(8)], [("tmp", i) for i in range(NTMP)])
            ytm, ytk = tm4[cs % 2], ("tm4", cs % 2)
            if smp:
                S.op("dve", lambda h: h.memset(ssq[:, 0:4], 0.0), writes=["ssq"])
            for hh in range(NH):
                S.op("act", lambda h, hh=hh: h.activation(out=ytm[0:ntok, hh * HV:(hh + 1) * HV], in_=o_ap(hh), func=AF.Square,
                                                            accum_out=ssq[0:ntok, hh:hh + 1]),
                     reads=okeys, writes=[ytk, "ssq"])
            S.op("act", lambda h: h.activation(out=ssq[0:ntok, 4:8], in_=ssq[0:ntok, 0:4], func=AF.Ln, bias=epsc[0:ntok], scale=1.0 / HV),
                 reads=["ssq", "eps"], writes=["ssq2"])
            S.op("act", lambda h: h.activation(out=ssq[0:ntok, 4:8], in_=ssq[0:ntok, 4:8], func=AF.Exp, scale=-0.5),
                 reads=["ssq2"], writes=["ssq2"])
            for hh in range(NH):
                S.op("dve", lambda h, hh=hh: h.scalar_tensor_tensor(out=ytm[0:ntok, hh * HV:(hh + 1) * HV], in0=o_ap(hh),
                                                                      scalar=ssq[0:ntok, 4 + hh:5 + hh],
                                                                      in1=sgm[0:ntok, cs, hh * HV:(hh + 1) * HV], op0=ALU.mult, op1=ALU.mult),
                     reads=okeys + ["ssq2", ("sgm", cs)], writes=[ytk])
            if not smp:
                S.op("act", lambda h: h.activation(out=Sbf[:, l].rearrange("p a b -> p (a b)"), in_=Sst[:, l].rearrange("p a b -> p (a b)"), func=AF.Copy),
                     reads=[("S", l)], writes=[("Sbf", l)])
                for _ in range(2):
                    conv_step()
            tb = bank_pair()
            for c in range(KC):
                bb, off = tb + c // 4, (c % 4) * 128
                S.op("pe", lambda h, c=c, bb=bb, off=off: h.transpose(PS[:, bb, off:off + ntok], ytm[0:ntok, c * 128:(c + 1) * 128],
                                                                        ident[0:ntok, 0:ntok]),
                     reads=[ytk, "ident"], writes=[("ps", tb), ("ps", tb + 1)], inc=(c == KC - 1))
            S.op("act", lambda h: h.activation(out=yT[:, :, c0:c0 + ntok],
                                               in_=PS[:, tb:tb + 2, :].rearrange("p a (c t) -> p (a c) t", c=4)[:, :, 0:ntok], func=AF.Copy),
                 reads=[("ps", tb), ("ps", tb + 1)], writes=[("yT", 0 if c0 < TP else TP)])

        for _ in conv:
            pass
        if ti == 0:
            bp2 = bank_pair()
            for c in range(KC):
                bb, off = bp2 + c // 4, (c % 4) * 128
                S.op("pe", lambda h, c=c, bb=bb, off=off: h.transpose(PS[0:2 * NSEQ, bb, off:off + 128], cstg[:, c, :], ident),
                     reads=["cstg", "ident"], writes=[("ps", bp2), ("ps", bp2 + 1)], inc=(c == KC - 1))
            so = tm4[1]
            S.op("act", lambda h: h.activation(out=so[0:2 * NSEQ, :], in_=PS[0:2 * NSEQ, bp2:bp2 + 2, :].rearrange("p a b -> p (a b)"), func=AF.Copy),
                 reads=[("ps", bp2), ("ps", bp2 + 1)], writes=[("tm4", 1)])
            store(lambda h: h.dma_start(out=css[l, :, :], in_=so[0:2 * NSEQ, :]), reads=[("tm4", 1)])
        if last_tile:
            bp2 = bank_pair()
            for c in range(KC):
                bb, off = bp2 + c // 4, (c % 4) * 128
                S.op("pe", lambda h, c=c, bb=bb, off=off: h.transpose(PS[0:2, bb, off:off + 128], akeep[:, l, c, :], ident),
                     reads=[("akeep", l), "ident"], writes=[("ps", bp2), ("ps", bp2 + 1)], inc=(c == KC - 1))
            so = tm4[1]
            S.op("act", lambda h: h.activation(out=so[0:2, :], in_=PS[0:2, bp2:bp2 + 2, :].rearrange("p a b -> p (a b)"), func=AF.Copy),
                 reads=[("ps", bp2), ("ps", bp2 + 1)], writes=[("tm4", 1)])
            store(lambda h: h.dma_start(out=csp[l, :, :], in_=so[0:2, :]), reads=[("tm4", 1)])

        S.transfer(allk("sq", 0) + allk("sq", TP), [("mg", 0), ("mg", TP)])
        for r in range(4):
            gb_, ob_ = merge_blocks[r]
            grv, gcv, grk = w_get2(gb_, KC, 256)
            gck = grk
            rov, cov, rok = w_get2(ob_, KC, 256)
            cok = rok
            for o in range(2):
                oc = r * 2 + o
                sample_step()
                for part in parts:
                    c0, n = part
                    p1, k1 = psum_part(part)
                    p2, k2 = psum_part(part)
                    p3, k3 = psum_part(part)
                    p4, k4 = psum_part(part)
                    fm_group(p1, k1, grv, grk, o * 128, xn, "xn", c0, n)
                    fm_group(p2, k2, rov, rok, o * 128, yT, "yT", c0, n)
                    fm_group(p3, k3, gcv, gck, o * 128, xn, "xn", c0, n)
                    fm_group(p4, k4, cov, cok, o * 128, bzT, "bzT", c0, n)
                    t1, tk1 = tmp()
                    t2, tk2 = tmp()
                    S.op("act", lambda h: h.activation(out=t1[:, 0:n], in_=p1, func=AF.Sigmoid), reads=k1, writes=[tk1])
                    S.op("dve", lambda h: h.tensor_tensor(out=t1[:, 0:n], in0=p2, in1=t1[:, 0:n], op=ALU.mult), reads=k2 + [tk1], writes=[tk1])
                    S.op("act", lambda h: h.activation(out=t2[:, 0:n], in_=p3, func=AF.Sigmoid), reads=k3, writes=[tk2])
                    S.op("dve", lambda h: h.tensor_tensor(out=t2[:, 0:n], in0=p4, in1=t2[:, 0:n], op=ALU.mult), reads=k4 + [tk2], writes=[tk2])
                    S.op("dve", lambda h: h.tensor_tensor(out=mg[:, oc, c0:c0 + n], in0=t1[:, 0:n], in1=t2[:, 0:n], op=ALU.add),
                         reads=[tk1, tk2], writes=[("mg", c0)])
            for bq in (gb_, ob_):
                w_release(bq)
        if DBG_LEVEL == 17:
            return
        S.transfer([("v", c) for c in range(5)] + [("sgm", c) for c in range(5)], allk("fT", 0) + allk("fT", TP))
        S.transfer([("yT", 0), ("yT", TP)], allk("sq2", 0) + allk("sq2", TP))
        for r in range(2):
            wv, wk = w_get(wo_blocks[r], KC, 512)
            if r == 1:
                preload_ln_table()
            for o in range(4):
                oc = r * 4 + o
                sample_step()
                for part in parts:
                    c0, n = part
                    pap, pk = psum_part(part)
                    fm_group(pap, pk, wv, wk, o * 128, mg, "mg", c0, n)
                    S.op("act", lambda h: h.activation(out=fT[:, oc, c0:c0 + n], in_=pap, func=AF.Copy, scale=gcol(l, 3, oc)),
                         reads=pk + ["gains"], writes=[("fT", c0, oc)])
                    S.op("act", lambda h: h.activation(out=yT[:, oc, c0:c0 + n], in_=pap, func=AF.Square),
                         reads=pk, writes=[("sq2", c0, oc)])
            w_release(wo_blocks[r])
        S.transfer([("mg", 0), ("mg", TP)], allk("sq", 0) + allk("sq", TP))
        boundary(l, parts, yT, "sq2", l, 4)
        S.transfer(allk("sq2", 0) + allk("sq2", TP), [("yT", 0), ("yT", TP)])
        S.transfer([("qT", c) for c in range(5)] + [("qxT", c) for c in range(5)] + [("kT", c) for c in range(5)]
                   + [("kz", c) for c in range(5)] + ["KZh", "Qb"], [("hT", 0), ("hT", TP)])

    def load_x(ti):
        for (cs, c0, ntok) in chunks_of(ti):
            st, stk = tm4[cs % 2], ("tm4", cs % 2)
            if ntok == 128:
                r0 = ti * TP + cs * 128
                load(lambda h: h.dma_start(out=st[0:ntok, :], in_=xp[r0:r0 + ntok, :]), writes=[stk])
            else:
                load(lambda h: h.dma_start(out=st[0:ntok, :], in_=xs[:, :]), writes=[stk])
            tb = bank_pair()
            for c in range(KC):
                bb, off = tb + c // 4, (c % 4) * 128
                S.op("pe", lambda h, c=c, bb=bb, off=off: h.transpose(PS[:, bb, off:off + ntok], st[0:ntok, c * 128:(c + 1) * 128],
                                                                        ident[0:ntok, 0:ntok]),
                     reads=[stk, "ident"], writes=[("ps", tb), ("ps", tb + 1)], inc=(c == KC - 1))
            S.op("act", lambda h: h.activation(out=xT[:, :, c0:c0 + ntok],
                                               in_=PS[:, tb:tb + 2, :].rearrange("p a (c t) -> p (a c) t", c=4)[:, :, 0:ntok], func=AF.Copy),
                 reads=[("ps", tb), ("ps", tb + 1)], writes=allk("xT", 0 if c0 < TP else TP))

    def store_y(ti):
        for (cs, c0, ntok) in chunks_of(ti):
            st, stk = tm4[cs % 2], ("tm4", cs % 2)
            tb = bank_pair()
            for c in range(KC):
                bb, off = tb + c // 4, (c % 4) * 128
                S.op("pe", lambda h, c=c, bb=bb, off=off: h.transpose(PS[0:ntok, bb, off:off + 128], xT[:, c, c0:c0 + ntok], ident),
                     reads=[("xT", 0 if c0 < TP else TP, c), "ident"], writes=[("ps", tb), ("ps", tb + 1)], inc=(c == KC - 1))
            S.op("act", lambda h: h.activation(out=st[0:ntok, :], in_=PS[0:ntok, tb:tb + 2, :].rearrange("p a b -> p (a b)"), func=AF.Copy),
                 reads=[("ps", tb), ("ps", tb + 1)], writes=[stk])
            if ntok == 128:
                r0 = ti * TP + cs * 128
                store(lambda h: h.dma_start(out=yp[r0:r0 + ntok, :], in_=st[0:ntok, :]), reads=[stk])
            else:
                store(lambda h: h.dma_start(out=ys[:, :], in_=st[0:ntok, :]), reads=[stk])

    def tap(name, ti):
        if name in tap_out and ti == 0:
            store(lambda h: h.dma_start(out=tap_out[name], in_=xT), reads=allk("xT", 0) + allk("xT", TP))

    plan = []
    for ti in range(n_tiles):
        for l in range(n_layers):
            ent = {}
            if "ffn1" in stages:
                ent["ffn1"] = ffn_blocks(0, l)
            if "mixer" in stages:
                ent["mixer"] = mixer_blocks(l)
            if "ffn2" in stages:
                ent["ffn2"] = ffn_blocks(1, l)
            plan.append((ti, l, ent))
    w_pump()
    full = ("ffn1" in stages and "mixer" in stages and "ffn2" in stages)
    for (ti, l, ent) in plan:
        if l == 0:
            load_x(ti)
        if full:
            ffn(ti, l, 0, 1, ent["ffn1"], need_norm_in=(l == 0), nxt=(l, 2))
            tap(f"ffn1_{l}", ti)
            mixer(ti, l, ent["mixer"])
            tap(f"mixer_{l}", ti)
            ffn(ti, l, 4, 5, ent["ffn2"], need_norm_in=False, nxt=((l + 1, 0) if l + 1 < n_layers else (None, None)))
            tap(f"ffn2_{l}", ti)
        else:
            if "ffn1" in ent:
                ffn(ti, l, 0, 1, ent["ffn1"], need_norm_in=True, nxt=(None, None))
                tap(f"ffn1_{l}", ti)
        if l == n_layers - 1:
            store_y(ti)
    S.wait_all("sp", st_lanes + ld_lanes + wl + wst)
    stack.close()
    print("instruction counts:", {k: (v["lane"].count if v["lane"] else None) for k, v in S.E.items()}, "weight blocks:", len(blocks))
    return nc


_PROGRAM = None


def _get_program():
    global _PROGRAM
    if _PROGRAM is None:
        _PROGRAM = build_program()
    return _PROGRAM


def make_in_maps(inputs):
    f = lambda a: np.ascontiguousarray(np.asarray(a, dtype=np.float32))
    x_prompt = f(inputs["x_prompt"])
    x_sample = f(inputs["x_sample"])
    state_ret = f(inputs["state_ret"])
    state_conv = f(inputs["state_conv"])
    shared = {
        "norms": f(inputs["norms"]).reshape(DEPTH * 6 * KC, 128),
        "convw": f(inputs["conv_w"]).reshape(DEPTH * 3 * KC, 128),
        "w_ffn1_up": f(inputs["w_ffn1_up"]), "w_ffn2_up": f(inputs["w_ffn2_up"]),
        "w_ffn1_down": f(inputs["w_ffn1_down"]), "w_ffn2_down": f(inputs["w_ffn2_down"]),
        "w_in": f(inputs["w_in"]), "w_ret_out": f(inputs["w_ret_out"]),
        "w_conv_out": f(inputs["w_conv_out"]), "w_o": f(inputs["w_o"]),
    }
    for k, v in _CONSTS.items():
        shared["c_" + k] = v
    in_maps = []
    for c in range(NCORES):
        m = dict(shared)
        m["xp"] = x_prompt[c]
        m["xs"] = np.ascontiguousarray(x_sample[c * NSEQ:(c + 1) * NSEQ].reshape(NS, D))
        m["sret"] = np.ascontiguousarray(state_ret[:, c * NSEQ:(c + 1) * NSEQ])
        m["sconv"] = np.ascontiguousarray(state_conv[:, c * NSEQ:(c + 1) * NSEQ].reshape(DEPTH, NSEQ * 2, D))
        in_maps.append(m)
    return in_maps


def kernel(**inputs):
    nc = _get_program()
    in_maps = make_in_maps(inputs)
    res = run_bass_kernel_spmd(nc, in_maps, core_ids=list(range(NCORES)))
    R = res.results
    y_prompt = np.stack([R[c]["yp"] for c in range(NCORES)], axis=0)
    y_sample = np.concatenate([R[c]["ys"].reshape(NSEQ, 4, D) for c in range(NCORES)], axis=0)
    ret_p = np.stack([R[c]["rsp"] for c in range(NCORES)], axis=1)
    conv_p = np.stack([R[c]["csp"] for c in range(NCORES)], axis=1)
    ret_s = np.concatenate([R[c]["rss"] for c in range(NCORES)], axis=1)
    conv_s = np.concatenate([R[c]["css"].reshape(DEPTH, NSEQ, 2, D) for c in range(NCORES)], axis=1)
    return (y_prompt.astype(np.float32), y_sample.astype(np.float32), ret_p.astype(np.float32),
            conv_p.astype(np.float32), ret_s.astype(np.float32), conv_s.astype(np.float32))
```

```python
import math
from contextlib import ExitStack

import numpy as np
import concourse.bass as bass
import concourse.mybir as mybir
from concourse.bass_utils import run_bass_kernel_spmd

F32 = mybir.dt.float32
BF16 = mybir.dt.bfloat16
AF = mybir.ActivationFunctionType
ALU = mybir.AluOpType

D = 1024
KC = 8
DFF = 2816
NJ = 22
NH = 4
HQ = 128
HV = 256
DEPTH = 2
SEQ = 2048
TP = 512
NT = 4
NS = 64
NSEQ = 16
TM = TP + NS
EPS = 1e-6
NSLOT = 5
N_IN = 8192
NCORES = 8
DBG_LEVEL = 9
DBG_NI = 6
DBG_CHUNKS = None
DBG_SUB = 0


class Lane:
    def __init__(self, name, sem, unit):
        self.name, self.sem, self.unit, self.count = name, sem, unit, 0


class Sched:
    def __init__(self, nc, stack):
        self.nc, self.stack = nc, stack
        self.E = {}
        self.lastw = {}
        self.readers = {}
        self.nlanes = 0

    def add_engine(self, name, handle, lane=True):
        ln = None
        if lane:
            sem = self.stack.enter_context(self.nc.semaphore("s_" + name))
            ln = Lane(name, sem, 1)
        self.E[name] = dict(h=handle, lane=ln, seen={})

    def dma_lane(self, name):
        sem = self.stack.enter_context(self.nc.semaphore("d_" + name))
        return Lane("d_" + name, sem, 16)

    def _need(self, mylane, reads, writes):
        need = {}

        def add(tok, same_ok):
            if tok is None:
                return
            l, v = tok
            if l is mylane and l.name == "pe":
                return
            if need.get(l.name, (None, 0))[1] < v:
                need[l.name] = (l, v)

        for k in reads:
            add(self.lastw.get(k), False)
        for k in writes:
            add(self.lastw.get(k), True)
            for tok in self.readers.get(k, {}).values():
                add(tok, True)
        return need

    def _wait(self, ename, need):
        e = self.E[ename]
        for l, v in need.values():
            if e["seen"].get(l.name, 0) < v:
                e["h"].wait_ge(l.sem, v)
                e["seen"][l.name] = v

    def _record(self, tok, reads, writes):
        l = tok[0]
        for k in reads:
            self.readers.setdefault(k, {})[l.name] = tok
        for k in writes:
            self.lastw[k] = tok
            self.readers[k] = {}

    def op(self, ename, fn, reads=(), writes=(), inc=True):
        e = self.E[ename]
        lane = e["lane"]
        psr = [k for k in reads if isinstance(k, tuple) and k[0] == "ps"]
        if psr:
            reads = [k for k in reads if k not in psr]
            writes = list(writes) + [k for k in psr if k not in writes]
        self._wait(ename, self._need(lane, reads, writes))
        ins = fn(e["h"])
        tok = (lane, lane.count + 1)
        if inc:
            ins.then_inc(lane.sem, 1)
            lane.count += 1
        self._record(tok, reads, writes)
        return tok

    def dma(self, qname, lane, fn, reads=(), writes=(), serialize=True):
        need = self._need(None, reads, writes)
        if serialize and lane.count > 0:
            if need.get(lane.name, (None, 0))[1] < lane.count:
                need[lane.name] = (lane, lane.count)
        self._wait(qname, need)
        ins = fn(self.E[qname]["h"])
        ins.then_inc(lane.sem, 16)
        lane.count += 16
        tok = (lane, lane.count)
        self._record(tok, reads, writes)
        return tok

    def transfer(self, old_keys, new_keys):
        merged = {}
        for k in old_keys:
            toks = list(self.readers.get(k, {}).values())
            if self.lastw.get(k) is not None:
                toks.append(self.lastw[k])
            for l, v in toks:
                if merged.get(l.name, (None, 0))[1] < v:
                    merged[l.name] = (l, v)
        for k in new_keys:
            self.lastw[k] = None
            self.readers[k] = dict(merged)

    def wait_all(self, ename, lanes):
        need = {l.name: (l, l.count) for l in lanes if l.count > 0}
        self._wait(ename, need)


def _host_consts():
    f32 = np.float32
    lg = np.log((1.0 - np.exp(np.linspace(math.log(1.0 / 32), math.log(1.0 / 512), NH, dtype=f32))).astype(f32)).astype(f32)
    half = HQ // 2
    inv_freq = (10000.0 ** (-(np.arange(half, dtype=f32) * f32(2.0) / f32(HQ)))).astype(f32)
    pos = np.concatenate([np.arange(SEQ, dtype=f32), (16384 + (np.arange(NS) % 4)).astype(f32)])
    ang = (pos[:, None] * inv_freq[None, :]).astype(f32)
    cos = np.cos(ang).astype(f32)
    sin = np.sin(ang).astype(f32)
    cos2 = np.concatenate([cos, cos], axis=1)
    sinm = np.concatenate([-sin, sin], axis=1)
    idx = np.arange(128, dtype=f32)
    diff = idx[None, :] - idx[:, None]
    dmT = np.where(diff[None] >= 0, np.exp(np.maximum(diff, 0.0)[None] * lg[:, None, None]), 0.0).astype(f32)
    dmT = np.ascontiguousarray(dmT.transpose(1, 0, 2))
    xi = np.exp((idx[None, :] + 1.0) * lg[:, None]).astype(f32)
    xiP = np.ascontiguousarray(np.broadcast_to(xi[None], (128, NH, 128))).astype(f32)
    zeta = np.exp((127.0 - idx)[:, None] * lg[None, :]).astype(f32) * f32(HQ ** -0.5)
    zetaP = np.ascontiguousarray(np.broadcast_to(zeta[:, :, None], (128, NH, 128))).astype(f32)
    t = np.arange(NS)
    sq, jj = t // 4, t % 4
    same = (sq[:, None] == sq[None, :])
    dd = (jj[None, :] - jj[:, None]).astype(f32)
    dmS = np.where((same & (dd >= 0))[None], np.exp(np.maximum(dd, 0.0)[None] * lg[:, None, None]), 0.0).astype(f32)
    dmS = np.ascontiguousarray(dmS.transpose(1, 0, 2))
    xis = np.exp((jj.astype(f32)[None, :] + 1.0) * lg[:, None]).astype(f32)
    xiS = np.ascontiguousarray(np.broadcast_to(xis[None], (128, NH, NS))).astype(f32)
    zs = np.exp((3.0 - jj.astype(f32))[:, None] * lg[None, :]).astype(f32) * f32(HQ ** -0.5)
    zetaS = np.ascontiguousarray(np.broadcast_to(zs[:, :, None], (NS, NH, 128))).astype(f32)
    kmask = (sq[:, None] == np.arange(NSEQ)[None, :]).astype(f32)
    cmask = np.ascontiguousarray(np.broadcast_to((np.arange(NSEQ)[:, None] == sq[None, :])[None], (128, NSEQ, NS))).astype(f32)
    gC = [float(np.exp(f32(128.0) * lg[h])) for h in range(NH)]
    g4 = [float(np.exp(f32(4.0) * lg[h])) for h in range(NH)]
    ident = np.eye(128, dtype=f32)
    return dict(cos2=cos2, sinm=sinm, dmT=dmT, xiP=xiP, zetaP=zetaP, dmS=dmS, xiS=xiS, zetaS=zetaS,
                kmask=kmask, cmask=cmask, ident=ident), gC, g4


_CONSTS, _GC, _G4 = _host_consts()
_CONST_SHAPES = {k: list(v.shape) for k, v in _CONSTS.items()}


def build_program(n_tiles=NT, n_layers=DEPTH, stages=("ffn1", "mixer", "ffn2"), taps=()):
    nc = bass.Bass("TRN2", target_bir_lowering=False)
    stack = ExitStack()

    def din(name, shape):
        return nc.dram_tensor(name, list(shape), F32, kind="ExternalInput").ap()

    def dout(name, shape):
        return nc.dram_tensor(name, list(shape), F32, kind="ExternalOutput").ap()

    xp = din("xp", [SEQ, D])
    xs = din("xs", [NS, D])
    sret = din("sret", [DEPTH, NSEQ, NH, HQ, HV])
    sconv = din("sconv", [DEPTH, NSEQ * 2, D])
    norms = din("norms", [DEPTH * 6 * KC, 128])
    convw = din("convw", [DEPTH * 3 * KC, 128])
    w_up = [din("w_ffn1_up", [DEPTH, D, 2 * DFF]), din("w_ffn2_up", [DEPTH, D, 2 * DFF])]
    w_dn = [din("w_ffn1_down", [DEPTH, DFF, D]), din("w_ffn2_down", [DEPTH, DFF, D])]
    w_in = din("w_in", [DEPTH, D, N_IN])
    w_ro = din("w_ret_out", [DEPTH, D, D])
    w_co = din("w_conv_out", [DEPTH, D, D])
    w_oo = din("w_o", [DEPTH, D, D])
    cst = {k: din("c_" + k, shp) for k, shp in _CONST_SHAPES.items()}

    yp = dout("yp", [SEQ, D])
    ys = dout("ys", [NS, D])
    rsp = dout("rsp", [DEPTH, NH, HQ, HV])
    csp = dout("csp", [DEPTH, 2, D])
    rss = dout("rss", [DEPTH, NSEQ, NH, HQ, HV])
    css = dout("css", [DEPTH, NSEQ * 2, D])
    tap_out = {name: dout("tap_" + name, [128, KC, TM]) for name in taps}

    NW = 53200
    big = stack.enter_context(nc.sbuf_tensor("big", [128, NW], F32))
    PS = stack.enter_context(nc.psum_tensor("ps", [128, 8, 512], F32))
    S = Sched(nc, stack)
    S.add_engine("pe", nc.tensor)
    S.add_engine("act", nc.scalar)
    S.add_engine("dve", nc.vector)
    S.add_engine("pool", nc.gpsimd)
    S.add_engine("sp", nc.sync, lane=False)

    cur = [0]

    def alloc(nbytes):
        nbytes = (nbytes + 63) // 64 * 64
        off = cur[0]
        cur[0] += nbytes
        assert cur[0] <= NW * 4, f"SBUF overflow {cur[0]}"
        return off

    def view(off, shape, dt, parts=128):
        n = int(np.prod(shape))
        nb = n * (2 if dt == BF16 else 4)
        ap = big[0:parts, off // 4:(off + nb) // 4]
        if dt != F32:
            ap = ap.bitcast(dt)
        if len(shape) == 2:
            ap = ap.rearrange("p (a b) -> p a b", a=shape[0])
        elif len(shape) == 3:
            ap = ap.rearrange("p (a b c) -> p a b c", a=shape[0], b=shape[1])
        return ap

    def newbuf(shape, dt, parts=128):
        n = int(np.prod(shape))
        return view(alloc(n * (2 if dt == BF16 else 4)), shape, dt, parts)

    xT = newbuf([KC, TM], F32)
    xn = newbuf([KC, TM], BF16)
    rstd = newbuf([TM], F32)
    R1 = alloc(NJ * TM * 2)
    hT = view(R1, [NJ, TM], BF16)
    qT = view(R1, [NH, TM], BF16)
    qxT = view(R1 + NH * TM * 2, [NH, TM], BF16)
    kT = view(R1 + 2 * NH * TM * 2, [NH, TM], BF16)
    kz = view(R1 + 3 * NH * TM * 2, [5, 512], BF16)
    r1_tail = R1 + 3 * NH * TM * 2 + 5 * 512 * 2
    KZh = view(r1_tail, [NSEQ, 128], BF16, parts=NS)
    Qb = view(r1_tail + NSEQ * 128 * 2, [NSEQ, NS], BF16)
    assert r1_tail + NSEQ * 128 * 2 + NSEQ * NS * 2 <= R1 + NJ * TM * 2
    SQo = alloc(KC * TM * 2)
    sq = view(SQo, [KC, TM], BF16)
    mg = view(SQo, [KC, TM], BF16)
    R2 = alloc(20480)
    fT = view(R2, [KC, TM], F32)
    vtm = view(R2, [5, 1024], BF16)
    sgm = view(R2 + 10240, [5, 1024], BF16)
    yT = newbuf([KC, TM], BF16)
    bzT = newbuf([KC, TM], BF16)
    Wsl = [newbuf([4096], BF16) for _ in range(NSLOT)]
    NTMP = 4
    tmp_off = alloc(NTMP * TM * 4)
    tmps = [view(tmp_off + i * TM * 4, [TM], F32) for i in range(NTMP)]
    S32b = view(tmp_off, [8, HV], F32)
    tm4 = [newbuf([1024], F32) for _ in range(2)]
    qk_off = alloc(2 * 512 * 4)
    qk2 = [view(qk_off + i * 2048, [512], F32) for i in range(2)]
    abuf = newbuf([2 + TP], F32)
    abufs = newbuf([NSEQ, 6], F32)
    sTm = [newbuf([NH, 128], BF16) for _ in range(2)]
    Sst = newbuf([DEPTH, NH, HV], F32)
    Sbf = newbuf([DEPTH, NH, HV], BF16)
    S32s = newbuf([4, HV], F32)
    Sbfs = newbuf([4, HV], BF16)
    ident = newbuf([128], F32)
    ones = newbuf([128], BF16)
    DMT = newbuf([NH, 128], F32)
    XI = newbuf([NH, 128], F32)
    ZETA = newbuf([NH, 128], F32)
    cs_off = alloc(2 * 5 * 128 * 4)
    COS = view(cs_off, [5, 128], F32)
    SIN = view(cs_off + 5 * 128 * 4, [5, 128], F32)
    SoutA = view(cs_off, [4, HV], F32)
    SoutB = view(qk_off, [4, HV], F32)

    def sout(i):
        return (SoutA if i < 4 else SoutB)[:, i % 4, :]
    DMS = newbuf([NH, NS], F32)
    XIS = newbuf([NH, NS], F32)
    ZETAS = newbuf([NH, 128], F32)
    KMASK = newbuf([NSEQ], F32)
    CMASK = newbuf([NSEQ, NS], F32)
    gains = newbuf([DEPTH * 6 * KC], F32)
    cw = newbuf([DEPTH * 3 * KC], F32)
    akeep = newbuf([DEPTH, KC, 2], F32)
    epsc = newbuf([1], F32)
    lnwarm = newbuf([1], F32)
    ssq = newbuf([8], F32)
    cstg = newbuf([KC, 2 * NSEQ], F32)
    cprev = newbuf([KC, 2 * NSEQ], F32)
    print("SBUF bytes used per partition:", cur[0])

    wl = [S.dma_lane(f"w{i}") for i in range(NSLOT)]
    wst = [S.dma_lane(f"wst{i}") for i in range(NSLOT)]
    ld_lanes = [S.dma_lane(f"ld{i}") for i in range(4)]
    st_lanes = [S.dma_lane(f"st{i}") for i in range(4)]
    ldi = [0]
    sti = [0]

    def load(fn, writes, reads=()):
        ln = ld_lanes[ldi[0] % len(ld_lanes)]
        ldi[0] += 1
        return S.dma("sp", ln, fn, reads=reads, writes=writes)

    pst_lanes = [S.dma_lane(f"pst{i}") for i in range(4)]
    psti = [0]

    def store(fn, reads, q="sp"):
        if q == "pool":
            ln = pst_lanes[psti[0] % len(pst_lanes)]
            psti[0] += 1
        else:
            ln = st_lanes[sti[0] % len(st_lanes)]
            sti[0] += 1
        return S.dma(q, ln, fn, reads=reads, writes=())

    bank_rr = [0]

    def bank():
        b = bank_rr[0] % 6
        bank_rr[0] += 1
        return b

    pair_rr = [0]

    def bank_pair():
        b = (pair_rr[0] % 3) * 2
        pair_rr[0] += 1
        return b

    sstep = [0, 0]

    def sample_step():
        sstep[0] ^= 1
        sstep[1] = 0

    def psum_part(part):
        c0, n = part
        if n == TP:
            b = bank()
            return PS[:, b, :], [("ps", b)]
        sb = 6 + sstep[0]
        a = sstep[1]
        sstep[1] += 1
        assert a < 8
        return PS[:, sb, a * 64:a * 64 + n], [("ps", sb)]

    tmp_rr = [0]

    def tmp():
        i = tmp_rr[0] % NTMP
        tmp_rr[0] += 1
        return tmps[i], ("tmp", i)

    blocks = []

    def wsrc(w, l, r0, nrows, c0, ncols):
        return w[l, r0:r0 + nrows, c0:c0 + ncols].rearrange("(kc p) n -> p kc n", p=128)

    def add_block(srcs):
        blocks.append(srcs)
        return len(blocks) - 1

    class WS:
        issued = 0
        released = 0

    WS.nblk = None
    WS.scr = None
    WS.scr_pending = []

    def blk_info(b):
        nb = WS.nblk
        return b // (nb * n_layers), (b // nb) % n_layers, b % nb, sum(k * n for _, k, n in blocks[b])

    def scr_store(b):
        ti_, l_, idx_, nel = blk_info(b)
        s = b % NSLOT
        S.dma("pool", wst[s], (lambda h: h.dma_start(out=WS.scr[l_ * WS.nblk + idx_][:, 0:nel], in_=Wsl[s][:, 0:nel])),
              reads=[("w", s)], writes=[("scr", l_, idx_)], serialize=False)

    def w_pump():
        if WS.nblk is None:
            WS.nblk = len(blocks) // (n_tiles * n_layers)
            assert WS.nblk * n_tiles * n_layers == len(blocks)
            if n_tiles > 1:
                WS.scr = nc.dram_tensor("wscr", [n_layers * WS.nblk, 128, 4096], BF16).ap()
        while WS.issued < len(blocks) and WS.issued < WS.released + NSLOT:
            b = WS.issued
            s = b % NSLOT
            ti_, l_, idx_, nel = blk_info(b)
            if ti_ == 0 or WS.scr is None:
                off = 0
                for i, (ap, kcn, ncols) in enumerate(blocks[b]):
                    dst = Wsl[s][:, off:off + kcn * ncols].rearrange("p (kc n) -> p kc n", kc=kcn)
                    S.dma("pool", wl[s], (lambda h, dst=dst, ap=ap: h.dma_start(out=dst, in_=ap)),
                          writes=[("w", s)], serialize=False)
                    off += kcn * ncols
                if WS.scr is not None:
                    WS.scr_pending.append(b)
            else:
                S.dma("pool", wl[s], (lambda h: h.dma_start(out=Wsl[s][:, 0:nel], in_=WS.scr[l_ * WS.nblk + idx_][:, 0:nel])),
                      reads=[("scr", l_, idx_)], writes=[("w", s)], serialize=False)
            WS.issued += 1
            while WS.scr_pending and (WS.scr_pending[0] <= b - 3 or ti_ > 0):
                scr_store(WS.scr_pending.pop(0))

    def w_get(b, kcn, ncols):
        assert b < WS.issued, "weight block not issued (too many live slots)"
        s = b % NSLOT
        return Wsl[s][:, 0:kcn * ncols].rearrange("p (kc n) -> p kc n", kc=kcn), ("w", s)

    def w_get2(b, kcn, ncols):
        assert b < WS.issued, "weight block not issued (too many live slots)"
        s = b % NSLOT
        n = kcn * ncols
        return (Wsl[s][:, 0:n].rearrange("p (kc n) -> p kc n", kc=kcn),
                Wsl[s][:, n:2 * n].rearrange("p (kc n) -> p kc n", kc=kcn), ("w", s))

    def w_release(b):
        assert b == WS.released
        WS.released += 1
        w_pump()

    def cload(dst, src, key, parts=128):
        load(lambda h: h.dma_start(out=dst, in_=src), writes=[key])

    cload(ident, cst["ident"], "ident")
    cload(DMT, cst["dmT"], "DMT")
    cload(XI, cst["xiP"], "XI")
    cload(ZETA, cst["zetaP"], "ZETA")
    cload(DMS[0:NS], cst["dmS"], "DMS")
    cload(XIS, cst["xiS"], "XIS")
    cload(ZETAS[0:NS], cst["zetaS"], "ZETAS")
    cload(KMASK[0:NS], cst["kmask"], "KMASK")
    cload(CMASK, cst["cmask"], "CMASK")
    S.op("dve", lambda h: h.memset(ones, 1.0), writes=["ones"])
    S.op("dve", lambda h: h.memset(epsc, EPS), writes=["eps"])
    S.op("dve", lambda h: h.memset(Sst.rearrange("p a b c -> p (a b c)"), 0.0), writes=[("S", 0), ("S", 1)])
    S.op("dve", lambda h: h.memset(Sbf.rearrange("p a b c -> p (a b c)"), 0.0), writes=[("Sbf", 0), ("Sbf", 1)])
    S.op("dve", lambda h: h.memset(akeep.rearrange("p a b c -> p (a b c)"), 0.0), writes=[("akeep", 0), ("akeep", 1)])

    def load_small_T(src, nrows, dst, key):
        st = tm4[0]
        load(lambda h: h.dma_start(out=st[0:nrows, 0:128], in_=src), writes=[("tm4", 0)])
        b = bank()
        S.op("pe", lambda h: h.transpose(PS[:, b, 0:nrows], st[0:nrows, 0:128], ident[0:nrows, 0:nrows]),
             reads=[("tm4", 0), "ident"], writes=[("ps", b)])
        S.op("act", lambda h: h.activation(out=dst, in_=PS[:, b, 0:nrows], func=AF.Copy),
             reads=[("ps", b)], writes=[key])

    load_small_T(norms, DEPTH * 6 * KC, gains, "gains")
    load_small_T(convw, DEPTH * 3 * KC, cw, "cw")
    for l in range(DEPTH):
        for ni in (1, 5):
            o = (l * 6 + ni) * KC
            S.op("act", lambda h, o=o: h.mul(gains[:, o:o + KC], gains[:, o:o + KC], 0.5),
                 reads=["gains"], writes=["gains"])

    def gcol(l, ni, c):
        o = (l * 6 + ni) * KC + c
        return gains[:, o:o + 1]

    def cwcol(l, i, c):
        o = (l * 3 + i) * KC + c
        return cw[:, o:o + 1]

    def parts_of(ti):
        return [(0, TP)] + ([(TP, NS)] if ti == 0 else [])

    def chunks_of(ti):
        if DBG_CHUNKS is not None:
            return [x for x in ([(c, c * 128, 128) for c in range(4)] + [(4, TP, NS)]) if x[0] in DBG_CHUNKS]
        return [(c, c * 128, 128) for c in range(4)] + ([(4, TP, NS)] if ti == 0 else [])

    def allk(name, c0):
        return [(name, c0, c) for c in range(KC)]

    def rstd_from_psum(pap, pk, c0, n, inv_n):
        S.op("act", lambda h: h.activation(out=rstd[:, c0:c0 + n], in_=pap, func=AF.Ln, bias=epsc, scale=inv_n),
             reads=pk + ["eps"], writes=[("rstd", c0)])
        S.op("act", lambda h: h.activation(out=rstd[:, c0:c0 + n], in_=rstd[:, c0:c0 + n], func=AF.Exp, scale=-0.5),
             reads=[("rstd", c0)], writes=[("rstd", c0)])

    def stat_mm(pap, pk, sqb, sqname, c0, n, kc):
        S.op("pe", lambda h: h.matmul(pap, lhsT=ones, rhs=sqb[:, kc, c0:c0 + n], start=(kc == 0), stop=(kc == KC - 1)),
             reads=["ones", (sqname, c0, kc)], writes=pk, inc=(kc == KC - 1))

    def norm_in(l, ni, parts):
        sample_step()
        for part in parts:
            c0, n = part
            pap, pk = psum_part(part)
            for c in range(KC):
                S.op("act", lambda h, c=c: h.activation(out=sq[:, c, c0:c0 + n], in_=xT[:, c, c0:c0 + n], func=AF.Square),
                     reads=[("xT", c0, c)], writes=[("sq", c0, c)])
                stat_mm(pap, pk, sq, "sq", c0, n, c)
            rstd_from_psum(pap, pk, c0, n, 1.0 / D)
            for c in range(KC):
                S.op("dve", lambda h, c=c: h.scalar_tensor_tensor(out=xn[:, c, c0:c0 + n], in0=xT[:, c, c0:c0 + n],
                                                                    scalar=gcol(l, ni, c), in1=rstd[:, c0:c0 + n],
                                                                    op0=ALU.mult, op1=ALU.mult),
                     reads=[("xT", c0, c), ("rstd", c0), "gains"], writes=[("xn", c0, c)])

    def preload_ln_table():
        S.op("act", lambda h: h.activation(out=lnwarm, in_=epsc, func=AF.Ln), reads=["eps"], writes=["lnwarm"])

    def boundary(l, parts, sqb, sqname, l_next, ni_next):
        sample_step()
        for part in parts:
            c0, n = part
            pap, pk = psum_part(part)
            for kc in range(KC):
                stat_mm(pap, pk, sqb, sqname, c0, n, kc)
            rstd_from_psum(pap, pk, c0, n, 1.0 / D)
            if ni_next is not None:
                pap2, pk2 = psum_part(part)
            for c in range(KC):
                S.op("dve", lambda h, c=c: h.tensor_tensor(out=fT[:, c, c0:c0 + n], in0=fT[:, c, c0:c0 + n], in1=rstd[:, c0:c0 + n], op=ALU.mult),
                     reads=[("fT", c0, c), ("rstd", c0)], writes=[("fT", c0, c)])
                S.op("dve", lambda h, c=c: h.tensor_tensor(out=xT[:, c, c0:c0 + n], in0=xT[:, c, c0:c0 + n], in1=fT[:, c, c0:c0 + n], op=ALU.add),
                     reads=[("fT", c0, c), ("xT", c0, c)], writes=[("xT", c0, c)])
                if ni_next is not None:
                    S.op("act", lambda h, c=c: h.activation(out=sq[:, c, c0:c0 + n], in_=xT[:, c, c0:c0 + n], func=AF.Square),
                         reads=[("xT", c0, c)], writes=[("sq", c0, c)])
                    stat_mm(pap2, pk2, sq, "sq", c0, n, c)
                    S.op("act", lambda h, c=c: h.activation(out=fT[:, c, c0:c0 + n], in_=xT[:, c, c0:c0 + n], func=AF.Copy,
                                                            scale=gcol(l_next, ni_next, c)),
                         reads=[("xT", c0, c), "gains"], writes=[("fT", c0, c)])
            if ni_next is not None:
                rstd_from_psum(pap2, pk2, c0, n, 1.0 / D)
                for c in range(KC):
                    S.op("dve", lambda h, c=c: h.tensor_tensor(out=xn[:, c, c0:c0 + n], in0=fT[:, c, c0:c0 + n], in1=rstd[:, c0:c0 + n], op=ALU.mult),
                         reads=[("fT", c0, c), ("rstd", c0)], writes=[("xn", c0, c)])

    def fm_group(pap, pk, wv, wkey, ocol, rhs_buf, rhs_key, c0, n, nk=KC, first=True, last=True, kbase=0, force_inc=False):
        for kc in range(nk):
            S.op("pe", lambda h, kc=kc: h.matmul(pap, lhsT=wv[:, kc, ocol:ocol + 128], rhs=rhs_buf[:, kbase + kc, c0:c0 + n],
                                                    start=(first and kc == 0), stop=(last and kc == nk - 1)),
                 reads=[wkey, ((rhs_key, c0, kbase + kc) if rhs_key == "xn" else (rhs_key, c0))], writes=pk,
                 inc=((last or force_inc) and kc == nk - 1))

    def ffn_blocks(which, l):
        ids = []
        for i in range(6):
            ncols = 512 if i < 5 else 256
            g = add_block([(wsrc(w_up[which], l, 0, D, i * 512, ncols), KC, ncols)])
            u = add_block([(wsrc(w_up[which], l, 0, D, DFF + i * 512, ncols), KC, ncols)])
            ids.append((g, u, ncols))
        dn = []
        for op_ in range(4):
            for kh in range(2):
                dn.append(add_block([(wsrc(w_dn[which], l, kh * 1408, 1408, op_ * 256, 256), 11, 256)]))
        return ids, dn

    def ffn(ti, l, ni_in, ni_out, blk, need_norm_in, nxt):
        parts = parts_of(ti)
        ids, dn = blk
        if need_norm_in:
            norm_in(l, ni_in, parts)
        if DBG_LEVEL == 0:
            return
        for i, (gb, ub, ncols) in enumerate(ids):
            gv, gk = w_get(gb, KC, ncols)
            uv, uk = w_get(ub, KC, ncols)
            for jj in range(ncols // 128):
                j = i * 4 + jj
                sample_step()
                for part in parts:
                    c0, n = part
                    gp, gpk = psum_part(part)
                    up, upk = psum_part(part)
                    fm_group(gp, gpk, gv, gk, jj * 128, xn, "xn", c0, n)
                    fm_group(up, upk, uv, uk, jj * 128, xn, "xn", c0, n)
                    t, tk = tmp()
                    S.op("act", lambda h: h.activation(out=t[:, 0:n], in_=gp, func=AF.Silu), reads=gpk, writes=[tk])
                    S.op("dve", lambda h: h.tensor_tensor(out=hT[:, j, c0:c0 + n], in0=up, in1=t[:, 0:n], op=ALU.mult),
                         reads=upk + [tk], writes=[("hT", c0)])
            w_release(gb)
            w_release(ub)
            if DBG_LEVEL == 1 and i == DBG_NI - 1:
                return
        if DBG_LEVEL == 1:
            return
        for op_ in range(4):
            pa = {}
            sample_step()
            for kh in range(2):
                b = dn[op_ * 2 + kh]
                wv, wk = w_get(b, 11, 256)
                if op_ == 3 and kh == 1:
                    preload_ln_table()
                for o in range(2):
                    oc = op_ * 2 + o
                    for part in parts:
                        c0, n = part
                        if kh == 0:
                            pa[(o, part)] = psum_part(part) if n == TP else (PS[:, 6 + o, 0:n], [("ps", 6 + o)])
                        pap, pk = pa[(o, part)]
                        fm_group(pap, pk, wv, wk, o * 128, hT, "hT", c0, n, nk=11, first=(kh == 0), last=(kh == 1),
                                 kbase=kh * 11, force_inc=True)
                        if kh == 1:
                            S.op("act", lambda h: h.activation(out=fT[:, oc, c0:c0 + n], in_=pap, func=AF.Copy, scale=gcol(l, ni_out, oc)),
                                 reads=pk + ["gains"], writes=[("fT", c0, oc)])
                            S.op("act", lambda h: h.activation(out=sq[:, oc, c0:c0 + n], in_=pap, func=AF.Square),
                                 reads=pk, writes=[("sq", c0, oc)])
                w_release(b)
        if DBG_LEVEL == 2:
            return
        boundary(l, parts, sq, "sq", nxt[0], nxt[1])

    def mixer_blocks(l):
        tm_blocks = [add_block([(wsrc(w_in, l, 0, D, c, 512), KC, 512)]) for c in (0, 1024, 1536, 512, 2048, 2560)]
        conv_blocks = []
        for r in range(2):
            conv_blocks.append(tuple(add_block([(wsrc(w_in, l, 0, D, base + r * 512, 512), KC, 512)])
                                     for base in (4096, 5120, 3072)))
        merge_blocks = []
        for r in range(4):
            merge_blocks.append((add_block([(wsrc(w_in, l, 0, D, 6144 + r * 256, 256), KC, 256),
                                            (wsrc(w_in, l, 0, D, 7168 + r * 256, 256), KC, 256)]),
                                 add_block([(wsrc(w_ro, l, 0, D, r * 256, 256), KC, 256),
                                            (wsrc(w_co, l, 0, D, r * 256, 256), KC, 256)])))
        wo_blocks = [add_block([(wsrc(w_oo, l, 0, D, r * 512, 512), KC, 512)]) for r in range(2)]
        return tm_blocks, conv_blocks, merge_blocks, wo_blocks

    def tm_group(b, wv, wk, c0, ntok):
        for kc in range(KC):
            S.op("pe", lambda h, kc=kc: h.matmul(PS[0:ntok, b, :], lhsT=xn[:, kc, c0:c0 + ntok], rhs=wv[:, kc, :],
                                                    start=(kc == 0), stop=(kc == KC - 1)),
                 reads=[wk, ("xn", 0 if c0 < TP else TP, kc)], writes=[("ps", b)], inc=(kc == KC - 1))

    def rotary(b, ntok, cs, dst, dkey):
        src = PS[0:ntok, b, :].rearrange("p (h d) -> p h d", h=NH)
        d3 = dst[0:ntok, :].rearrange("p (h d) -> p h d", h=NH)
        t, tk = tmp()
        t3 = t[0:ntok, 0:512].rearrange("p (h d) -> p h d", h=NH)
        cosb = COS[0:ntok, cs, :].unsqueeze(1).to_broadcast([ntok, NH, 128])
        S.op("dve", lambda h: h.tensor_tensor(out=d3, in0=src, in1=cosb, op=ALU.mult),
             reads=[("ps", b), "COS"], writes=[dkey])
        sl = SIN[0:ntok, cs, 0:64].unsqueeze(1).to_broadcast([ntok, NH, 64])
        sh = SIN[0:ntok, cs, 64:128].unsqueeze(1).to_broadcast([ntok, NH, 64])
        S.op("dve", lambda h: h.tensor_tensor(out=t3[:, :, 0:64], in0=src[:, :, 64:128], in1=sl, op=ALU.mult),
             reads=[("ps", b), "SIN"], writes=[tk])
        S.op("dve", lambda h: h.tensor_tensor(out=t3[:, :, 64:128], in0=src[:, :, 0:64], in1=sh, op=ALU.mult),
             reads=[("ps", b), "SIN"], writes=[tk])
        S.op("dve", lambda h: h.tensor_tensor(out=dst[0:ntok, :], in0=dst[0:ntok, :], in1=t[0:ntok, 0:512], op=ALU.add),
             reads=[dkey, tk], writes=[dkey])

    def mixer(ti, l, blk):
        parts = parts_of(ti)
        chunks = chunks_of(ti)
        tm_blocks, conv_blocks, merge_blocks, wo_blocks = blk
        last_tile = (ti == n_tiles - 1)
        S.transfer([("hT", 0), ("hT", TP)], [("qT", c) for c in range(5)] + [("qxT", c) for c in range(5)]
                   + [("kT", c) for c in range(5)] + [("kz", c) for c in range(5)] + ["KZh", "Qb"])
        S.transfer(allk("fT", 0) + allk("fT", TP), [("v", c) for c in range(5)] + [("sgm", c) for c in range(5)])
        S.transfer([("Sout", i) for i in range(8)], ["COS", "SIN", ("qk2", 0), ("qk2", 1)])
        load(lambda h: h.dma_start(out=COS[:, 0:4, :], in_=cst["cos2"][ti * TP:(ti + 1) * TP, :].rearrange("(c p) f -> p c f", p=128)),
             writes=["COS"])
        load(lambda h: h.dma_start(out=SIN[:, 0:4, :], in_=cst["sinm"][ti * TP:(ti + 1) * TP, :].rearrange("(c p) f -> p c f", p=128)),
             writes=["SIN"])
        if ti == 0:
            load(lambda h: h.dma_start(out=COS[0:NS, 4, :], in_=cst["cos2"][SEQ:SEQ + NS, :]), writes=["COS"])
            load(lambda h: h.dma_start(out=SIN[0:NS, 4, :], in_=cst["sinm"][SEQ:SEQ + NS, :]), writes=["SIN"])

        if DBG_LEVEL == 10:
            return
        def qk_phase(blk_id, is_q, fill_ids, fill_dst, fill_name, fill_func):
            wv, wk = w_get(blk_id, KC, 512)
            fills = [w_get(fb, KC, 512) for fb in fill_ids]
            pend = None

            def finish(p):
                cs, c0, ntok, rb, rbk = p
                b2 = bank()
                for hh in range(NH):
                    S.op("pe", lambda h, hh=hh: h.transpose(PS[:, b2, hh * 128:hh * 128 + ntok], rb[0:ntok, hh * 128:(hh + 1) * 128],
                                                              ident[0:ntok, 0:ntok]),
                         reads=[rbk, "ident"], writes=[("ps", b2)], inc=(hh == NH - 1))
                src_ = PS[:, b2, :].rearrange("p (h t) -> p h t", h=NH)[:, :, 0:ntok]
                if is_q:
                    S.op("act", lambda h: h.activation(out=qT[:, :, c0:c0 + ntok], in_=src_, func=AF.Copy),
                         reads=[("ps", b2)], writes=[("qT", cs)])
                    xi_c, xkey = (XI, "XI") if ntok == 128 else (XIS, "XIS")
                    S.op("dve", lambda h: h.tensor_tensor(out=qxT[:, :, c0:c0 + ntok], in0=src_, in1=xi_c, op=ALU.mult),
                         reads=[("ps", b2), xkey], writes=[("qxT", cs)])
                else:
                    S.op("act", lambda h: h.activation(out=kT[:, :, c0:c0 + ntok], in_=src_, func=AF.Copy, scale=float(HQ ** -0.5)),
                         reads=[("ps", b2)], writes=[("kT", cs)])
                    z_c, zkey = (ZETA, "ZETA") if ntok == 128 else (ZETAS, "ZETAS")
                    S.op("dve", lambda h: h.tensor_tensor(out=kz[0:ntok, cs, :], in0=rb[0:ntok, :],
                                                            in1=z_c[0:ntok].rearrange("p h d -> p (h d)"), op=ALU.mult),
                         reads=[rbk, zkey], writes=[("kz", cs)])

            for i, (cs, c0, ntok) in enumerate(chunks):
                b = bank()
                tm_group(b, wv, wk, c0, ntok)
                for half, (fv, fk) in enumerate(fills):
                    fb_ = bank()
                    tm_group(fb_, fv, fk, c0, ntok)
                    S.op("act", lambda h, half=half, fb_=fb_: h.activation(out=fill_dst[0:ntok, cs, half * 512:(half + 1) * 512],
                                                                          in_=PS[0:ntok, fb_, :], func=fill_func),
                         reads=[("ps", fb_)], writes=[(fill_name, cs)])
                if pend is not None:
                    finish(pend)
                rb, rbk = qk2[i % 2], ("qk2", i % 2)
                rotary(b, ntok, cs, rb, rbk)
                pend = (cs, c0, ntok, rb, rbk)
            finish(pend)
            w_release(blk_id)
            for fb in fill_ids:
                w_release(fb)

        qk_phase(tm_blocks[0], True, tm_blocks[1:3], vtm, "v", AF.Copy)
        qk_phase(tm_blocks[3], False, tm_blocks[4:6], sgm, "sgm", AF.Silu)
        if DBG_LEVEL == 13:
            return
        if ti == 0:
            st = tm4[0]
            load(lambda h: h.dma_start(out=st[0:2 * NSEQ, :], in_=sconv[l, :, :]), writes=[("tm4", 0)])
            bp = bank_pair()
            for c in range(KC):
                bb, off = bp + c // 4, (c % 4) * 128
                S.op("pe", lambda h, c=c, bb=bb, off=off: h.transpose(PS[:, bb, off:off + 2 * NSEQ], st[0:2 * NSEQ, c * 128:(c + 1) * 128],
                                                                        ident[0:2 * NSEQ, 0:2 * NSEQ]),
                     reads=[("tm4", 0), "ident"], writes=[("ps", bp), ("ps", bp + 1)], inc=(c == KC - 1))
            S.op("act", lambda h: h.activation(out=cprev, in_=PS[:, bp:bp + 2, :].rearrange("p a (c t) -> p (a c) t", c=4)[:, :, 0:2 * NSEQ],
                                               func=AF.Copy),
                 reads=[("ps", bp), ("ps", bp + 1)], writes=["cprev"])

        def conv_gen():
            for r in range(2):
                cgb, xcb, bgb = conv_blocks[r]
                cgv, cgk = w_get(cgb, KC, 512)
                xcv, xck = w_get(xcb, KC, 512)
                bgv, bgk = w_get(bgb, KC, 512)
                for o in range(4):
                    oc = r * 4 + o
                    sample_step()
                    for part in parts:
                        c0, n = part
                        p1, k1 = psum_part(part)
                        p2, k2 = psum_part(part)
                        p3, k3 = psum_part(part)
                        fm_group(p1, k1, cgv, cgk, o * 128, xn, "xn", c0, n)
                        fm_group(p2, k2, xcv, xck, o * 128, xn, "xn", c0, n)
                        fm_group(p3, k3, bgv, bgk, o * 128, xn, "xn", c0, n)
                        t, tk = tmp()
                        S.op("act", lambda h: h.activation(out=t[:, 0:n], in_=p1, func=AF.Copy), reads=k1, writes=[tk])
                        if n == TP:
                            S.op("act", lambda h: h.activation(out=abuf[:, 0:2], in_=akeep[:, l, oc, :], func=AF.Copy),
                                 reads=[("akeep", l)], writes=["abuf"])
                            S.op("dve", lambda h: h.tensor_tensor(out=abuf[:, 2:2 + TP], in0=p2, in1=t[:, 0:n], op=ALU.mult),
                                 reads=k2 + [tk], writes=["abuf"])
                            S.op("act", lambda h: h.activation(out=akeep[:, l, oc, :], in_=abuf[:, TP:TP + 2], func=AF.Copy),
                                 reads=["abuf"], writes=[("akeep", l)])
                            a0, a1, a2 = abuf[:, 0:TP], abuf[:, 1:1 + TP], abuf[:, 2:2 + TP]
                            tz = t[:, 0:n]
                            akey = "abuf"
                        else:
                            S.op("act", lambda h: h.activation(out=abufs[:, :, 0:2], in_=cprev[:, oc, :].rearrange("p (s r) -> p s r", s=NSEQ), func=AF.Copy),
                                 reads=["cprev"], writes=["abufs"])
                            S.op("dve", lambda h: h.tensor_tensor(out=abufs[:, :, 2:6], in0=p2.rearrange("p (s j) -> p s j", s=NSEQ),
                                                                    in1=t[:, 0:n].rearrange("p (s j) -> p s j", s=NSEQ), op=ALU.mult),
                                 reads=k2 + [tk], writes=["abufs"])
                            a0, a1, a2 = abufs[:, :, 0:4], abufs[:, :, 1:5], abufs[:, :, 2:6]
                            tz = t[:, 0:n].rearrange("p (s j) -> p s j", s=NSEQ)
                            akey = "abufs"
                        S.op("dve", lambda h: h.tensor_scalar(out=tz, in0=a2, scalar1=cwcol(l, 2, oc), scalar2=None, op0=ALU.mult),
                             reads=[akey, "cw"], writes=[tk])
                        S.op("dve", lambda h: h.scalar_tensor_tensor(out=tz, in0=a1, scalar=cwcol(l, 1, oc), in1=tz, op0=ALU.mult, op1=ALU.add),
                             reads=[akey, "cw", tk], writes=[tk])
                        S.op("dve", lambda h: h.scalar_tensor_tensor(out=tz, in0=a0, scalar=cwcol(l, 0, oc), in1=tz, op0=ALU.mult, op1=ALU.add),
                             reads=[akey, "cw", tk], writes=[tk])
                        S.op("dve", lambda h: h.tensor_tensor(out=bzT[:, oc, c0:c0 + n], in0=p3, in1=t[:, 0:n], op=ALU.mult),
                             reads=k3 + [tk], writes=[("bzT", c0)])
                        if n == NS:
                            S.op("act", lambda h: h.activation(out=cstg[:, oc, :].rearrange("p (s r) -> p s r", s=NSEQ), in_=abufs[:, :, 4:6], func=AF.Copy),
                                 reads=["abufs"], writes=["cstg"])
                    yield
                w_release(cgb)
                w_release(xcb)
                w_release(bgb)
        conv = conv_gen()

        def conv_step():
            for _ in conv:
                return

        for (cs, c0, ntok) in chunks:
            smp = (ntok == NS)
            b = bank()
            for hh in range(NH):
                S.op("pe", lambda h, hh=hh: h.matmul(PS[0:ntok, b, hh * 128:hh * 128 + ntok], lhsT=kT[:, hh, c0:c0 + ntok],
                                                       rhs=qT[:, hh, c0:c0 + ntok], start=True, stop=True),
                     reads=[("kT", cs), ("qT", cs)], writes=[("ps", b)], inc=(hh == NH - 1))
            sm = sTm[cs % 2]
            smk = ("sTm", cs % 2)
            msk = DMS if smp else DMT
            S.op("dve", lambda h: h.tensor_tensor(out=sm[0:ntok, :, 0:ntok],
                                                    in0=PS[0:ntok, b, :].rearrange("p (h t) -> p h t", h=NH)[:, :, 0:ntok],
                                                    in1=msk[0:ntok], op=ALU.mult),
                 reads=[("ps", b), "DMS" if smp else "DMT"], writes=[smk])
            ob = bank_pair() if ti == 0 else 6
            okeys = [("ps", ob), ("ps", ob + 1)]

            def o_ap(hh):
                return PS[0:ntok, ob + hh // 2, (hh % 2) * HV:(hh % 2 + 1) * HV]

            if not smp:
                for hh in range(NH):
                    S.op("pe", lambda h, hh=hh: h.matmul(o_ap(hh), lhsT=sm[0:ntok, hh, 0:ntok], rhs=vtm[0:ntok, cs, hh * HV:(hh + 1) * HV],
                                                           start=True, stop=False),
                         reads=[smk, ("v", cs)], writes=okeys, inc=False)
                    S.op("pe", lambda h, hh=hh: h.matmul(o_ap(hh), lhsT=qxT[:, hh, c0:c0 + ntok], rhs=Sbf[:, l, hh, :],
                                                           start=False, stop=True),
                         reads=[("qxT", cs), ("Sbf", l)], writes=okeys, inc=(hh == NH - 1))
                S.op("dve", lambda h: h.memset(ssq[:, 0:4], 0.0), writes=["ssq"])
                sb_ = bank_pair()
                skeys = [("ps", sb_), ("ps", sb_ + 1)]
                for hh in range(NH):
                    S.op("pe", lambda h, hh=hh: h.matmul(PS[:, sb_ + hh // 2, (hh % 2) * HV:(hh % 2 + 1) * HV],
                                                           lhsT=kz[0:ntok, cs, hh * 128:(hh + 1) * 128], rhs=vtm[0:ntok, cs, hh * HV:(hh + 1) * HV],
                                                           start=True, stop=True),
                         reads=[("kz", cs), ("v", cs)], writes=skeys, inc=(hh == NH - 1))
                for hh in range(NH):
                    S.op("dve", lambda h, hh=hh: h.scalar_tensor_tensor(out=Sst[:, l, hh, :], in0=Sst[:, l, hh, :], scalar=_GC[hh],
                                                                          in1=PS[:, sb_ + hh // 2, (hh % 2) * HV:(hh % 2 + 1) * HV],
                                                                          op0=ALU.mult, op1=ALU.add),
                         reads=skeys + [("S", l)], writes=[("S", l)])
                need_sbf_cast = True
                if last_tile and cs == 3:
                    store(lambda h: h.dma_start(out=rsp[l].rearrange("h d e -> d h e"), in_=Sst[:, l]), reads=[("S", l)])
            else:
                its = [(hh, s) for hh in range(NH) for s in range(NSEQ)]
                PF = 3

                PF = 6

                def emit_load(it):
                    hh_, s_ = its[it]
                    i8_ = it % 8
                    load(lambda h: h.dma_start(out=S32b[:, i8_, :], in_=sret[l, s_, hh_]), writes=[("S32b", i8_)])

                S.transfer(["COS", "SIN", ("qk2", 0), ("qk2", 1)], [("Sout", i) for i in range(8)])
                S.transfer([("tmp", i) for i in range(NTMP)], [("S32b", i) for i in range(8)])
                for it in range(PF):
                    emit_load(it)
                pend_stores = []
                for it, (hh, s) in enumerate(its):
                    if s == 0:
                        S.op("dve", lambda h, hh=hh: h.tensor_tensor(out=Qb, in0=qxT[:, hh, c0:c0 + ntok].unsqueeze(1).to_broadcast([128, NSEQ, NS]),
                                                                       in1=CMASK, op=ALU.mult),
                             reads=[("qxT", cs), "CMASK"], writes=["Qb"])
                        S.op("dve", lambda h, hh=hh: h.tensor_tensor(out=KZh, in0=kz[0:ntok, cs, hh * 128:(hh + 1) * 128].unsqueeze(1).to_broadcast([NS, NSEQ, 128]),
                                                                       in1=KMASK[0:NS, :].unsqueeze(2).to_broadcast([NS, NSEQ, 128]), op=ALU.mult),
                             reads=[("kz", cs), "KMASK"], writes=["KZh"])
                        S.op("pe", lambda h, hh=hh: h.matmul(o_ap(hh), lhsT=sm[0:ntok, hh, 0:ntok], rhs=vtm[0:ntok, cs, hh * HV:(hh + 1) * HV],
                                                               start=True, stop=False),
                             reads=[smk, ("v", cs)], writes=okeys, inc=True)
                    i4 = it % 4
                    s32, s32k = S32b[:, it % 8, :], ("S32b", it % 8)
                    sbf, sbfk = Sbfs[:, i4, :], ("Sbfs", i4)
                    S.op("act", lambda h, s32=s32, sbf=sbf: h.activation(out=sbf, in_=s32, func=AF.Copy), reads=[s32k], writes=[sbfk])
                    if len(pend_stores) >= 3:
                        pend_stores.pop(0)()
                    S.op("pe", lambda h, hh=hh, s=s, sbf=sbf: h.matmul(o_ap(hh), lhsT=Qb[:, s, :], rhs=sbf, start=False, stop=(s == NSEQ - 1)),
                         reads=["Qb", sbfk], writes=okeys, inc=True)
                    hb = i4 % 2
                    up_ap = PS[:, 6 + hb, 0:HV]
                    S.op("pe", lambda h, hh=hh, s=s, up_ap=up_ap: h.matmul(up_ap, lhsT=KZh[:, s, :], rhs=vtm[0:ntok, cs, hh * HV:(hh + 1) * HV],
                                                                          start=True, stop=True),
                         reads=["KZh", ("v", cs)], writes=[("ps", 6 + hb)], inc=True)
                    so, sok = sout(it % 8), ("Sout", it % 8)
                    S.op("dve", lambda h, hh=hh, s32=s32, so=so, up_ap=up_ap: h.scalar_tensor_tensor(out=so, in0=s32, scalar=_G4[hh], in1=up_ap,
                                                                                                      op0=ALU.mult, op1=ALU.add),
                         reads=[("ps", 6 + hb), s32k], writes=[sok])
                    if it + PF < len(its):
                        emit_load(it + PF)
                    pend_stores.append(lambda s=s, hh=hh, so=so, sok=sok:
                                       store(lambda h: h.dma_start(out=rss[l, s, hh], in_=so), reads=[sok], q="pool"))
                for ps_ in pend_stores:
                    ps_()
                S.transfer([("S32b", i) for i in range(8)], [("tmp", i) for i in range(NTMP)])
            ytm, ytk = tm4[cs % 2], ("tm4", cs % 2)
            if smp:
                S.op("dve", lambda h: h.memset(ssq[:, 0:4], 0.0), writes=["ssq"])
            for hh in range(NH):
                S.op("act", lambda h, hh=hh: h.activation(out=ytm[0:ntok, hh * HV:(hh + 1) * HV], in_=o_ap(hh), func=AF.Square,
                                                            accum_out=ssq[0:ntok, hh:hh + 1]),
                     reads=okeys, writes=[ytk, "ssq"])
            S.op("act", lambda h: h.activation(out=ssq[0:ntok, 4:8], in_=ssq[0:ntok, 0:4], func=AF.Ln, bias=epsc[0:ntok], scale=1.0 / HV),
                 reads=["ssq", "eps"], writes=["ssq2"])
            S.op("act", lambda h: h.activation(out=ssq[0:ntok, 4:8], in_=ssq[0:ntok, 4:8], func=AF.Exp, scale=-0.5),
                 reads=["ssq2"], writes=["ssq2"])
            for hh in range(NH):
                S.op("dve", lambda h, hh=hh: h.scalar_tensor_tensor(out=ytm[0:ntok, hh * HV:(hh + 1) * HV], in0=o_ap(hh),
                                                                      scalar=ssq[0:ntok, 4 + hh:5 + hh],
                                                                      in1=sgm[0:ntok, cs, hh * HV:(hh + 1) * HV], op0=ALU.mult, op1=ALU.mult),
                     reads=okeys + ["ssq2", ("sgm", cs)], writes=[ytk])
            if not smp:
                S.op("act", lambda h: h.activation(out=Sbf[:, l].rearrange("p a b -> p (a b)"), in_=Sst[:, l].rearrange("p a b -> p (a b)"), func=AF.Copy),
                     reads=[("S", l)], writes=[("Sbf", l)])
                for _ in range(2):
                    conv_step()
            tb = bank_pair()
            for c in range(KC):
                bb, off = tb + c // 4, (c % 4) * 128
                S.op("pe", lambda h, c=c, bb=bb, off=off: h.transpose(PS[:, bb, off:off + ntok], ytm[0:ntok, c * 128:(c + 1) * 128],
                                                                        ident[0:ntok, 0:ntok]),
                     reads=[ytk, "ident"], writes=[("ps", tb), ("ps", tb + 1)], inc=(c == KC - 1))
            S.op("act", lambda h: h.activation(out=yT[:, :, c0:c0 + ntok],
                                               in_=PS[:, tb:tb + 2, :].rearrange("p a (c t) -> p (a c) t", c=4)[:, :, 0:ntok], func=AF.Copy),
                 reads=[("ps", tb), ("ps", tb + 1)], writes=[("yT", 0 if c0 < TP else TP)])

        for _ in conv:
            pass
        if ti == 0:
            bp2 = bank_pair()
            for c in range(KC):
                bb, off = bp2 + c // 4, (c % 4) * 128
                S.op("pe", lambda h, c=c, bb=bb, off=off: h.transpose(PS[0:2 * NSEQ, bb, off:off + 128], cstg[:, c, :], ident),
                     reads=["cstg", "ident"], writes=[("ps", bp2), ("ps", bp2 + 1)], inc=(c == KC - 1))
            so = tm4[1]
            S.op("act", lambda h: h.activation(out=so[0:2 * NSEQ, :], in_=PS[0:2 * NSEQ, bp2:bp2 + 2, :].rearrange("p a b -> p (a b)"), func=AF.Copy),
                 reads=[("ps", bp2), ("ps", bp2 + 1)], writes=[("tm4", 1)])
            store(lambda h: h.dma_start(out=css[l, :, :], in_=so[0:2 * NSEQ, :]), reads=[("tm4", 1)])
        if last_tile:
            bp2 = bank_pair()
            for c in range(KC):
                bb, off = bp2 + c // 4, (c % 4) * 128
                S.op("pe", lambda h, c=c, bb=bb, off=off: h.transpose(PS[0:2, bb, off:off + 128], akeep[:, l, c, :], ident),
                     reads=[("akeep", l), "ident"], writes=[("ps", bp2), ("ps", bp2 + 1)], inc=(c == KC - 1))
            so = tm4[1]
            S.op("act", lambda h: h.activation(out=so[0:2, :], in_=PS[0:2, bp2:bp2 + 2, :].rearrange("p a b -> p (a b)"), func=AF.Copy),
                 reads=[("ps", bp2), ("ps", bp2 + 1)], writes=[("tm4", 1)])
            store(lambda h: h.dma_start(out=csp[l, :, :], in_=so[0:2, :]), reads=[("tm4", 1)])

        S.transfer(allk("sq", 0) + allk("sq", TP), [("mg", 0), ("mg", TP)])
        for r in range(4):
            gb_, ob_ = merge_blocks[r]
            grv, gcv, grk = w_get2(gb_, KC, 256)
            gck = grk
            rov, cov, rok = w_get2(ob_, KC, 256)
            cok = rok
            for o in range(2):
                oc = r * 2 + o
                sample_step()
                for part in parts:
                    c0, n = part
                    p1, k1 = psum_part(part)
                    p2, k2 = psum_part(part)
                    p3, k3 = psum_part(part)
                    p4, k4 = psum_part(part)
                    fm_group(p1, k1, grv, grk, o * 128, xn, "xn", c0, n)
                    fm_group(p2, k2, rov, rok, o * 128, yT, "yT", c0, n)
                    fm_group(p3, k3, gcv, gck, o * 128, xn, "xn", c0, n)
                    fm_group(p4, k4, cov, cok, o * 128, bzT, "bzT", c0, n)
                    t1, tk1 = tmp()
                    t2, tk2 = tmp()
                    S.op("act", lambda h: h.activation(out=t1[:, 0:n], in_=p1, func=AF.Sigmoid), reads=k1, writes=[tk1])
                    S.op("dve", lambda h: h.tensor_tensor(out=t1[:, 0:n], in0=p2, in1=t1[:, 0:n], op=ALU.mult), reads=k2 + [tk1], writes=[tk1])
                    S.op("act", lambda h: h.activation(out=t2[:, 0:n], in_=p3, func=AF.Sigmoid), reads=k3, writes=[tk2])
                    S.op("dve", lambda h: h.tensor_tensor(out=t2[:, 0:n], in0=p4, in1=t2[:, 0:n], op=ALU.mult), reads=k4 + [tk2], writes=[tk2])
                    S.op("dve", lambda h: h.tensor_tensor(out=mg[:, oc, c0:c0 + n], in0=t1[:, 0:n], in1=t2[:, 0:n], op=ALU.add),
                         reads=[tk1, tk2], writes=[("mg", c0)])
            for bq in (gb_, ob_):
                w_release(bq)
        if DBG_LEVEL == 17:
            return
        S.transfer([("v", c) for c in range(5)] + [("sgm", c) for c in range(5)], allk("fT", 0) + allk("fT", TP))
        S.transfer([("yT", 0), ("yT", TP)], allk("sq2", 0) + allk("sq2", TP))
        for r in range(2):
            wv, wk = w_get(wo_blocks[r], KC, 512)
            if r == 1:
                preload_ln_table()
            for o in range(4):
                oc = r * 4 + o
                sample_step()
                for part in parts:
                    c0, n = part
                    pap, pk = psum_part(part)
                    fm_group(pap, pk, wv, wk, o * 128, mg, "mg", c0, n)
                    S.op("act", lambda h: h.activation(out=fT[:, oc, c0:c0 + n], in_=pap, func=AF.Copy, scale=gcol(l, 3, oc)),
                         reads=pk + ["gains"], writes=[("fT", c0, oc)])
                    S.op("act", lambda h: h.activation(out=yT[:, oc, c0:c0 + n], in_=pap, func=AF.Square),
                         reads=pk, writes=[("sq2", c0, oc)])
            w_release(wo_blocks[r])
        S.transfer([("mg", 0), ("mg", TP)], allk("sq", 0) + allk("sq", TP))
        boundary(l, parts, yT, "sq2", l, 4)
        S.transfer(allk("sq2", 0) + allk("sq2", TP), [("yT", 0), ("yT", TP)])
        S.transfer([("qT", c) for c in range(5)] + [("qxT", c) for c in range(5)] + [("kT", c) for c in range(5)]
                   + [("kz", c) for c in range(5)] + ["KZh", "Qb"], [("hT", 0), ("hT", TP)])

    def load_x(ti):
        for (cs, c0, ntok) in chunks_of(ti):
            st, stk = tm4[cs % 2], ("tm4", cs % 2)
            if ntok == 128:
                r0 = ti * TP + cs * 128
                load(lambda h: h.dma_start(out=st[0:ntok, :], in_=xp[r0:r0 + ntok, :]), writes=[stk])
            else:
                load(lambda h: h.dma_start(out=st[0:ntok, :], in_=xs[:, :]), writes=[stk])
            tb = bank_pair()
            for c in range(KC):
                bb, off = tb + c // 4, (c % 4) * 128
                S.op("pe", lambda h, c=c, bb=bb, off=off: h.transpose(PS[:, bb, off:off + ntok], st[0:ntok, c * 128:(c + 1) * 128],
                                                                        ident[0:ntok, 0:ntok]),
                     reads=[stk, "ident"], writes=[("ps", tb), ("ps", tb + 1)], inc=(c == KC - 1))
            S.op("act", lambda h: h.activation(out=xT[:, :, c0:c0 + ntok],
                                               in_=PS[:, tb:tb + 2, :].rearrange("p a (c t) -> p (a c) t", c=4)[:, :, 0:ntok], func=AF.Copy),
                 reads=[("ps", tb), ("ps", tb + 1)], writes=allk("xT", 0 if c0 < TP else TP))

    def store_y(ti):
        for (cs, c0, ntok) in chunks_of(ti):
            st, stk = tm4[cs % 2], ("tm4", cs % 2)
            tb = bank_pair()
            for c in range(KC):
                bb, off = tb + c // 4, (c % 4) * 128
                S.op("pe", lambda h, c=c, bb=bb, off=off: h.transpose(PS[0:ntok, bb, off:off + 128], xT[:, c, c0:c0 + ntok], ident),
                     reads=[("xT", 0 if c0 < TP else TP, c), "ident"], writes=[("ps", tb), ("ps", tb + 1)], inc=(c == KC - 1))
            S.op("act", lambda h: h.activation(out=st[0:ntok, :], in_=PS[0:ntok, tb:tb + 2, :].rearrange("p a b -> p (a b)"), func=AF.Copy),
                 reads=[("ps", tb), ("ps", tb + 1)], writes=[stk])
            if ntok == 128:
                r0 = ti * TP + cs * 128
                store(lambda h: h.dma_start(out=yp[r0:r0 + ntok, :], in_=st[0:ntok, :]), reads=[stk])
            else:
                store(lambda h: h.dma_start(out=ys[:, :], in_=st[0:ntok, :]), reads=[stk])

    def tap(name, ti):
        if name in tap_out and ti == 0:
            store(lambda h: h.dma_start(out=tap_out[name], in_=xT), reads=allk("xT", 0) + allk("xT", TP))

    plan = []
    for ti in range(n_tiles):
        for l in range(n_layers):
            ent = {}
            if "ffn1" in stages:
                ent["ffn1"] = ffn_blocks(0, l)
            if "mixer" in stages:
                ent["mixer"] = mixer_blocks(l)
            if "ffn2" in stages:
                ent["ffn2"] = ffn_blocks(1, l)
            plan.append((ti, l, ent))
    w_pump()
    full = ("ffn1" in stages and "mixer" in stages and "ffn2" in stages)
    for (ti, l, ent) in plan:
        if l == 0:
            load_x(ti)
        if full:
            ffn(ti, l, 0, 1, ent["ffn1"], need_norm_in=(l == 0), nxt=(l, 2))
            tap(f"ffn1_{l}", ti)
            mixer(ti, l, ent["mixer"])
            tap(f"mixer_{l}", ti)
            ffn(ti, l, 4, 5, ent["ffn2"], need_norm_in=False, nxt=((l + 1, 0) if l + 1 < n_layers else (None, None)))
            tap(f"ffn2_{l}", ti)
        else:
            if "ffn1" in ent:
                ffn(ti, l, 0, 1, ent["ffn1"], need_norm_in=True, nxt=(None, None))
                tap(f"ffn1_{l}", ti)
        if l == n_layers - 1:
            store_y(ti)
    S.wait_all("sp", st_lanes + pst_lanes + ld_lanes + wl + wst)
    stack.close()
    print("instruction counts:", {k: (v["lane"].count if v["lane"] else None) for k, v in S.E.items()}, "weight blocks:", len(blocks))
    return nc


_PROGRAM = None


def _get_program():
    global _PROGRAM
    if _PROGRAM is None:
        _PROGRAM = build_program()
    return _PROGRAM


def make_in_maps(inputs):
    f = lambda a: np.ascontiguousarray(np.asarray(a, dtype=np.float32))
    x_prompt = f(inputs["x_prompt"])
    x_sample = f(inputs["x_sample"])
    state_ret = f(inputs["state_ret"])
    state_conv = f(inputs["state_conv"])
    shared = {
        "norms": f(inputs["norms"]).reshape(DEPTH * 6 * KC, 128),
        "convw": f(inputs["conv_w"]).reshape(DEPTH * 3 * KC, 128),
        "w_ffn1_up": f(inputs["w_ffn1_up"]), "w_ffn2_up": f(inputs["w_ffn2_up"]),
        "w_ffn1_down": f(inputs["w_ffn1_down"]), "w_ffn2_down": f(inputs["w_ffn2_down"]),
        "w_in": f(inputs["w_in"]), "w_ret_out": f(inputs["w_ret_out"]),
        "w_conv_out": f(inputs["w_conv_out"]), "w_o": f(inputs["w_o"]),
    }
    for k, v in _CONSTS.items():
        shared["c_" + k] = v
    in_maps = []
    for c in range(NCORES):
        m = dict(shared)
        m["xp"] = x_prompt[c]
        m["xs"] = np.ascontiguousarray(x_sample[c * NSEQ:(c + 1) * NSEQ].reshape(NS, D))
        m["sret"] = np.ascontiguousarray(state_ret[:, c * NSEQ:(c + 1) * NSEQ])
        m["sconv"] = np.ascontiguousarray(state_conv[:, c * NSEQ:(c + 1) * NSEQ].reshape(DEPTH, NSEQ * 2, D))
        in_maps.append(m)
    return in_maps


def kernel(**inputs):
    nc = _get_program()
    in_maps = make_in_maps(inputs)
    res = run_bass_kernel_spmd(nc, in_maps, core_ids=list(range(NCORES)))
    R = res.results
    y_prompt = np.stack([R[c]["yp"] for c in range(NCORES)], axis=0)
    y_sample = np.concatenate([R[c]["ys"].reshape(NSEQ, 4, D) for c in range(NCORES)], axis=0)
    ret_p = np.stack([R[c]["rsp"] for c in range(NCORES)], axis=1)
    conv_p = np.stack([R[c]["csp"] for c in range(NCORES)], axis=1)
    ret_s = np.concatenate([R[c]["rss"] for c in range(NCORES)], axis=1)
    conv_s = np.concatenate([R[c]["css"].reshape(DEPTH, NSEQ, 2, D) for c in range(NCORES)], axis=1)
    return (y_prompt.astype(np.float32), y_sample.astype(np.float32), ret_p.astype(np.float32),
            conv_p.astype(np.float32), ret_s.astype(np.float32), conv_s.astype(np.float32))
```
